# Optimizing a Trainium2 kernel written in Bass

```python
import math
import jax, jax.numpy as jnp
from jax import lax
import numpy as np

D_MODEL = 2048
BATCH = 4
SEQ = 2048
DEPTH = 1
DEC_BATCH = 128
DEC_SEQ = 1
PAST_LEN = 16384
PAGE_SIZE = 128

D_MIX = D_MODEL
D_A = D_MIX // 2
D_B = D_MIX - D_A
H_A = 4
DK = D_A // H_A
DV = D_A // H_A
G_B = 8
CONV_W = 3
D_FF = 5504
CHUNK = 64
GATE_CAP = 15.0
EPS = 1e-6
D_IN_TOT = 4 * D_A + 2 * H_A + 3 * D_B

kernel_name = "hymba_mlstm_shortconv_macaron_step"


def rmsnorm(x, g):
    xf = x.astype(jnp.float32)
    return xf * lax.rsqrt(jnp.mean(xf * xf, axis=-1, keepdims=True) + EPS) * g.astype(jnp.float32)


def swiglu(x, wg, wu, wd):
    return (jax.nn.silu(x @ wg) * (x @ wu)) @ wd


def soft_cap(x):
    return GATE_CAP * jnp.tanh(x / GATE_CAP)


def mlstm_chunkwise(q, k, v, logi, logf, C0, n0, m0):
    Bsz, H, S, _ = q.shape
    L = math.gcd(S, CHUNK)
    NC = S // L

    def to_chunks(a):
        a = a.reshape((Bsz, H, NC, L) + a.shape[3:])
        return jnp.moveaxis(a, 2, 0)

    xs = (to_chunks(q), to_chunks(k), to_chunks(v), to_chunks(logi), to_chunks(logf))
    causal = jnp.tril(jnp.ones((L, L), dtype=bool))

    def step(carry, xc):
        C, n, m = carry
        qc, kc, vc, ic, fc = xc
        b = jnp.cumsum(fc, axis=-1)
        dmat = b[..., :, None] - b[..., None, :] + ic[..., None, :]
        dmat = jnp.where(causal, dmat, -jnp.inf)
        m_inter = b + m[..., None]
        m_t = jnp.maximum(m_inter, jnp.max(dmat, axis=-1))
        s = jnp.einsum('bhtd,bhsd->bhts', qc, kc) * jnp.exp(dmat - m_t[..., None])
        scale_inter = jnp.exp(m_inter - m_t)
        num = scale_inter[..., None] * jnp.einsum('bhtd,bhde->bhte', qc, C) + jnp.einsum('bhts,bhse->bhte', s, vc)
        den = scale_inter * jnp.einsum('bhtd,bhd->bht', qc, n) + jnp.sum(s, axis=-1)
        h = num / jnp.maximum(jnp.abs(den), jnp.exp(-m_t))[..., None]
        b_last = b[..., -1]
        g = b_last[..., None] - b + ic
        m_new = jnp.maximum(b_last + m, jnp.max(g, axis=-1))
        decay = jnp.exp(b_last + m - m_new)
        wk = jnp.exp(g - m_new[..., None])[..., None] * kc
        C_new = decay[..., None, None] * C + jnp.einsum('bhsd,bhse->bhde', wk, vc)
        n_new = decay[..., None] * n + jnp.sum(wk, axis=-2)
        return (C_new, n_new, m_new), h

    (C_f, n_f, m_f), hs = lax.scan(step, (C0, n0, m0), xs)
    h = jnp.moveaxis(hs, 0, 2).reshape(Bsz, H, S, DV)
    return h, C_f, n_f, m_f


def mixing_block(h, w_in, b_gates, conv_w, conv_b, norm_mlstm, norm_conv, w_out, C0, n0, m0, buf0):
    Bsz, S, _ = h.shape
    z = h @ w_in.astype(jnp.float32)
    offs = np.cumsum([D_A, D_A, D_A, D_A, H_A, H_A, D_B, D_B])
    q, k, v, o, ig, fg, gb, gc, xc = jnp.split(z, offs, axis=-1)

    def heads(a):
        return a.reshape(Bsz, S, H_A, -1).transpose(0, 2, 1, 3)

    qh, kh, vh = heads(q), heads(k) * (DK ** -0.5), heads(v)
    bg = b_gates.astype(jnp.float32)
    logi = soft_cap(ig + bg[:H_A]).transpose(0, 2, 1)
    logf = jax.nn.log_sigmoid(soft_cap(fg + bg[H_A:])).transpose(0, 2, 1)
    ha, C_f, n_f, m_f = mlstm_chunkwise(qh, kh, vh, logi, logf,
                                        C0.astype(jnp.float32), n0.astype(jnp.float32), m0.astype(jnp.float32))
    ha = ha * lax.rsqrt(jnp.mean(ha * ha, axis=-1, keepdims=True) + EPS)
    ha = ha.transpose(0, 2, 1, 3).reshape(Bsz, S, D_A) * norm_mlstm.astype(jnp.float32) * jax.nn.sigmoid(o)

    u = gc * xc
    u_ext = jnp.concatenate([buf0.astype(jnp.float32), u], axis=1)
    cw = conv_w.astype(jnp.float32)
    yc = sum(cw[j] * u_ext[:, j:j + S] for j in range(CONV_W)) + conv_b.astype(jnp.float32)
    yb = gb * yc
    yb = yb.reshape(Bsz, S, G_B, D_B // G_B)
    yb = (yb * lax.rsqrt(jnp.mean(yb * yb, axis=-1, keepdims=True) + EPS)).reshape(Bsz, S, D_B)
    yb = yb * norm_conv.astype(jnp.float32)
    new_buf = u_ext[:, -(CONV_W - 1):]

    out = jnp.concatenate([ha, yb], axis=-1) @ w_out.astype(jnp.float32)
    return out, C_f, n_f, m_f, new_buf


def run_trunk(x, C_in, n_in, m_in, conv_in,
              norm_ffn1, ffn1_gate, ffn1_up, ffn1_down, norm_mix, w_in, b_gates, conv_w, conv_b,
              norm_mlstm, norm_conv, w_out, norm_ffn2, ffn2_gate, ffn2_up, ffn2_down, norm_final):
    h = x.astype(jnp.float32)
    Cs, ns, ms, bufs = [], [], [], []
    for l in range(DEPTH):
        h = h + 0.5 * swiglu(rmsnorm(h, norm_ffn1[l]), ffn1_gate[l].astype(jnp.float32),
                             ffn1_up[l].astype(jnp.float32), ffn1_down[l].astype(jnp.float32))
        mix, C_f, n_f, m_f, buf = mixing_block(rmsnorm(h, norm_mix[l]), w_in[l], b_gates[l], conv_w[l], conv_b[l],
                                               norm_mlstm[l], norm_conv[l], w_out[l],
                                               C_in[l], n_in[l], m_in[l], conv_in[l])
        h = h + mix
        h = h + 0.5 * swiglu(rmsnorm(h, norm_ffn2[l]), ffn2_gate[l].astype(jnp.float32),
                             ffn2_up[l].astype(jnp.float32), ffn2_down[l].astype(jnp.float32))
        Cs.append(C_f); ns.append(n_f); ms.append(m_f); bufs.append(buf)
    y = rmsnorm(h, norm_final).astype(x.dtype)
    return y, jnp.stack(Cs), jnp.stack(ns), jnp.stack(ms), jnp.stack(bufs)


def setup_inputs(seed: int = 0) -> dict:
    key = jax.random.key(seed)
    ks = iter(jax.random.split(key, 32))
    f32 = jnp.float32

    def nrm(shape, scale):
        return jax.random.normal(next(ks), shape, f32) * scale

    def gain(shape):
        return 1.0 + nrm(shape, 0.05)

    b_in = jax.random.normal(next(ks), (DEPTH, H_A), f32) * 0.1
    b_fg = 3.0 + jax.random.normal(next(ks), (DEPTH, H_A), f32) * 0.1
    return {
        "x_prompt": nrm((BATCH, SEQ, D_MODEL), 1.0),
        "x_sample": nrm((DEC_BATCH, DEC_SEQ, D_MODEL), 1.0),
        "state_mlstm_C": nrm((DEPTH, DEC_BATCH, H_A, DK, DV), 0.1),
        "state_mlstm_n": nrm((DEPTH, DEC_BATCH, H_A, DK), 0.1),
        "state_mlstm_m": nrm((DEPTH, DEC_BATCH, H_A), 1.0),
        "state_conv": nrm((DEPTH, DEC_BATCH, CONV_W - 1, D_B), 1.0),
        "norm_ffn1": gain((DEPTH, D_MODEL)),
        "ffn1_gate": nrm((DEPTH, D_MODEL, D_FF), D_MODEL ** -0.5),
        "ffn1_up": nrm((DEPTH, D_MODEL, D_FF), D_MODEL ** -0.5),
        "ffn1_down": nrm((DEPTH, D_FF, D_MODEL), D_FF ** -0.5),
        "norm_mix": gain((DEPTH, D_MODEL)),
        "w_in": nrm((DEPTH, D_MODEL, D_IN_TOT), D_MODEL ** -0.5),
        "b_gates": jnp.concatenate([b_in, b_fg], axis=-1),
        "conv_w": nrm((DEPTH, CONV_W, D_B), CONV_W ** -0.5),
        "conv_b": nrm((DEPTH, D_B), 0.02),
        "norm_mlstm": gain((DEPTH, D_A)),
        "norm_conv": gain((DEPTH, D_B)),
        "w_out": nrm((DEPTH, D_MIX, D_MODEL), D_MIX ** -0.5),
        "norm_ffn2": gain((DEPTH, D_MODEL)),
        "ffn2_gate": nrm((DEPTH, D_MODEL, D_FF), D_MODEL ** -0.5),
        "ffn2_up": nrm((DEPTH, D_MODEL, D_FF), D_MODEL ** -0.5),
        "ffn2_down": nrm((DEPTH, D_FF, D_MODEL), D_FF ** -0.5),
        "norm_final": gain((D_MODEL,)),
    }


def reference(x_prompt, x_sample, state_mlstm_C, state_mlstm_n, state_mlstm_m, state_conv,
              norm_ffn1, ffn1_gate, ffn1_up, ffn1_down, norm_mix, w_in, b_gates, conv_w, conv_b,
              norm_mlstm, norm_conv, w_out, norm_ffn2, ffn2_gate, ffn2_up, ffn2_down, norm_final):
    weights = (norm_ffn1, ffn1_gate, ffn1_up, ffn1_down, norm_mix, w_in, b_gates, conv_w, conv_b,
               norm_mlstm, norm_conv, w_out, norm_ffn2, ffn2_gate, ffn2_up, ffn2_down, norm_final)
    Bp = x_prompt.shape[0]
    C0 = jnp.zeros((DEPTH, Bp, H_A, DK, DV), jnp.float32)
    n0 = jnp.zeros((DEPTH, Bp, H_A, DK), jnp.float32)
    m0 = jnp.zeros((DEPTH, Bp, H_A), jnp.float32)
    buf0 = jnp.zeros((DEPTH, Bp, CONV_W - 1, D_B), jnp.float32)
    y_prompt, C_p, n_p, m_p, conv_p = run_trunk(x_prompt, C0, n0, m0, buf0, *weights)
    y_sample, C_s, n_s, m_s, conv_s = run_trunk(x_sample, state_mlstm_C, state_mlstm_n, state_mlstm_m,
                                                state_conv, *weights)
    return (y_prompt, y_sample, C_p, n_p, m_p, conv_p, C_s, n_s, m_s, conv_s)
```

```python
from contextlib import ExitStack

import numpy as np
import concourse.bass as bass
import concourse.mybir as mybir
from concourse.bass_utils import run_bass_kernel_spmd

F32 = mybir.dt.float32
BF16 = mybir.dt.bfloat16
AF = mybir.ActivationFunctionType
ALU = mybir.AluOpType

D = 2048
DFF = 5504
NFF = DFF // 128
KD = D // 128
H = 4
DK = 256
DIN = 7176
NPR = 1024
NS = 16
NT = NPR + NS
EPS = 1e-6
WINDOW = 4
SLOT_ELEMS = 4096
NSLOT = 5


class Tk:
    __slots__ = ("w", "r")

    def __init__(self):
        self.w = None
        self.r = {}


class Eng:
    def __init__(self, nc, name, eng, ndma=0):
        self.name = name
        self.eng = eng
        self.sem = nc.alloc_semaphore("s_" + name)
        self.seq = 0
        self.waited = {}
        self.slots = [[nc.alloc_semaphore("d_%s%d" % (name, i)), 0] for i in range(ndma)]
        self.nxt = 0


class KB:
    def __init__(self, nc):
        self.nc = nc
        self.pe = Eng(nc, "pe", nc.tensor)
        self.act = Eng(nc, "act", nc.scalar, ndma=6)
        self.dve = Eng(nc, "dve", nc.vector)
        self.pool = Eng(nc, "pool", nc.gpsimd, ndma=10)
        self.sp = Eng(nc, "sp", nc.sync, ndma=12)
        self.engs = [self.pe, self.act, self.dve, self.pool, self.sp]

    def _deps(self, E, reads, writes):
        need = {}

        def add(tok):
            sem, val, owner = tok
            if owner is E:
                if E is self.pe or E.seq - val >= WINDOW:
                    return
            k = id(sem)
            cur = need.get(k)
            if cur is None or cur[1] < val:
                need[k] = (sem, val)

        for t in reads:
            if t.w is not None:
                add(t.w)
        for t in writes:
            if t.w is not None:
                add(t.w)
            for tok in t.r.values():
                add(tok)
        for k, (sem, val) in need.items():
            if E.waited.get(k, 0) >= val:
                continue
            E.eng.wait_ge(sem, val)
            E.waited[k] = val

    def _mark(self, tok, reads, writes):
        k = id(tok[0])
        for t in reads:
            t.r[k] = tok
        for t in writes:
            t.w = tok
            t.r = {}

    def op(self, E, fn, reads=(), writes=()):
        self._deps(E, reads, writes)
        inst = fn(E.eng)
        E.seq += 1
        inst.then_inc(E.sem, 1)
        self._mark((E.sem, E.seq, E), reads, writes)

    def mm(self, out_ap, pairs, reads=(), writes=(), transpose=False):
        E = self.pe
        self._deps(E, reads, writes)
        n = len(pairs)
        inst = None
        for i, (l, r) in enumerate(pairs):
            if transpose:
                inst = self.nc.tensor.transpose(out_ap[i], l, r)
            else:
                inst = self.nc.tensor.matmul(out_ap, lhsT=l, rhs=r, start=(i == 0), stop=(i == n - 1))
        E.seq += 1
        inst.then_inc(E.sem, 1)
        self._mark((E.sem, E.seq, E), reads, writes)

    def dma(self, Q, out_ap, in_ap, reads=(), writes=(), **kw):
        self._deps(Q, reads, writes)
        slot = Q.slots[Q.nxt]
        Q.nxt = (Q.nxt + 1) % len(Q.slots)
        sem, cnt = slot
        k = id(sem)
        if cnt > 0 and Q.waited.get(k, 0) < cnt:
            Q.eng.wait_ge(sem, cnt)
            Q.waited[k] = cnt
        inst = Q.eng.dma_start(out=out_ap, in_=in_ap, **kw)
        slot[1] = cnt + 16
        inst.then_inc(sem, 16)
        self._mark((sem, cnt + 16, None), reads, writes)

    def barrier(self):
        for E in self.engs:
            for Fe in self.engs:
                if Fe is not E and Fe.seq > 0 and E.waited.get(id(Fe.sem), 0) < Fe.seq:
                    E.eng.wait_ge(Fe.sem, Fe.seq)
                    E.waited[id(Fe.sem)] = Fe.seq
                for sem, cnt in Fe.slots:
                    if cnt > 0 and E.waited.get(id(sem), 0) < cnt:
                        E.eng.wait_ge(sem, cnt)
                        E.waited[id(sem)] = cnt

    def finish(self):
        for Q in (self.sp, self.pool, self.act):
            for sem, cnt in Q.slots:
                if cnt > 0 and Q.waited.get(id(sem), 0) < cnt:
                    Q.eng.wait_ge(sem, cnt)
                    Q.waited[id(sem)] = cnt


def token_tiles(nt):
    out = []
    t = 0
    while t < nt:
        n = min(512, nt - t)
        out.append((t, n))
        t += n
    return out


class Ring:
    def __init__(self, prog, stack, nslots):
        self.prog = prog
        self.slots = [prog.scr(stack, "wring", [128, SLOT_ELEMS], BF16) for _ in range(nslots)]
        self.T = [Tk() for _ in range(nslots)]
        self.nxt = 0

    def load(self, wap, r0, nrow_chunks, c0, ncols):
        assert nrow_chunks * ncols <= SLOT_ELEMS
        i = self.nxt
        self.nxt = (i + 1) % len(self.slots)
        view = self.slots[i][:, 0:nrow_chunks * ncols].rearrange("p (k c) -> p k c", k=nrow_chunks)
        src = wap[r0 * 128:(r0 + nrow_chunks) * 128, c0:c0 + ncols].rearrange("(k p) c -> p k c", p=128)
        kb = self.prog.kb
        if ncols * 4 < 512:
            with self.prog.nc.allow_non_contiguous_dma(reason="narrow gate columns"):
                kb.dma(kb.pool, view, src, writes=[self.T[i]])
        else:
            kb.dma(kb.pool, view, src, writes=[self.T[i]])
        return view, self.T[i]


class Prog:
    def __init__(self, stages=("all",)):
        self.stages = stages
        nc = bass.Bass("TRN2", target_bir_lowering=False)
        self.nc = nc
        self.kb = KB(nc)
        self.din = {}
        self.dout = {}
        self.uid = 0

    def inp(self, name, shape, dt=F32):
        t = self.nc.dram_tensor(name, list(shape), dt, kind="ExternalInput")
        self.din[name] = t
        return t.ap()

    def outp(self, name, shape, dt=F32):
        t = self.nc.dram_tensor(name, list(shape), dt, kind="ExternalOutput")
        self.dout[name] = t
        return t.ap()

    def dscr(self, name, shape, dt=F32):
        return self.nc.dram_tensor(name, list(shape), dt).ap()

    def sb(self, name, shape, dt):
        return self.nc.alloc_sbuf_tensor(name, list(shape), dt).ap()

    def scr(self, stack, name, shape, dt):
        self.uid += 1
        return stack.enter_context(self.nc.sbuf_tensor("%s_u%d" % (name, self.uid), list(shape), dt)).ap()

    def build(self):
        nc, kb = self.nc, self.kb
        PE, ACT, DVE, POOL, SP = kb.pe, kb.act, kb.dve, kb.pool, kb.sp
        st = self.stages
        ALL = "all" in st

        xm = self.inp("xm", [NT, D])
        pcol_d = self.inp("pcol", [128, 128])
        cst_d = self.inp("cst", [128, 512])
        flag_d = self.inp("flag", [128, 1])
        nmb_d = self.inp("nmb", [128, 1024])
        gfb_d = self.inp("gfb", [128, D])
        bg_d = self.inp("bg", [4, 2])
        sC_d = self.inp("sC", [NS, H, DK, DK])
        sn_d = self.inp("sn", [NS * H, DK])
        sm_d = self.inp("sm", [NS, H])
        sconv_d = self.inp("sconv", [NS * 2, 1024])
        w = {}
        for nm, shp in [("ffn1_gate", [D, DFF]), ("ffn1_up", [D, DFF]), ("ffn1_down", [DFF, D]),
                        ("w_in", [D, DIN]), ("w_out", [D, D]),
                        ("ffn2_gate", [D, DFF]), ("ffn2_up", [D, DFF]), ("ffn2_down", [DFF, D])]:
            w[nm] = self.inp(nm, shp)
        y_out = self.outp("y", [NT, D])
        Cp_o = self.outp("Cp", [H, DK, DK])
        np_o = self.outp("np", [H, DK])
        mp_o = self.outp("mp", [H, 1])
        convp_o = self.outp("convp", [2, 1024])
        Cs_o = self.outp("Cs", [NS, H, DK, DK])
        ns_o = self.outp("ns", [NS * H, DK])
        ms_o = self.outp("ms", [NS, H])
        convs_o = self.outp("convs", [NS, 2, 1024])
        XR, XC = 1024, 264
        xsrc = self.dscr("cc_src", [XR, XC])
        xdst = self.dscr("cc_dst", [2 * XR, XC])
        xsa = self.dscr("cc_src_a", [32, XC])
        xda = self.dscr("cc_dst_a", [64, XC])
        cc_sem = nc.alloc_semaphore("cc_sem")
        cc_sem_a = nc.alloc_semaphore("cc_sem_a")
        T_xdst = Tk()
        T_xda = Tk()
        negM_d = self.dscr("scr_negM", [H, 1025])
        sgs_d = self.dscr("scr_sgs", [H, 48])

        onesb = self.sb("onesb", [128, 128], BF16)
        identb = self.sb("identb", [128, 128], BF16)
        pcol = self.sb("pcol_sb", [128, 128], F32)
        cst = self.sb("cst_sb", [128, 512], F32)
        ident = cst[:, 0:128]
        epsc = self.sb("epsc", [128, 1], F32)
        onec = self.sb("onec", [128, 1], F32)
        flag = self.sb("flag_sb", [128, 1], F32)
        minit = self.sb("minit", [4, 1], F32)
        xnpre = self.sb("xnpre", [128, KD, 2], BF16)
        g2 = self.sb("g2", [128, 2], F32)
        T_g2 = Tk()
        hT = self.sb("hT", [128, KD, NT], F32)
        xnT = self.sb("xnT", [128, KD, NT], BF16)
        psum = [nc.alloc_psum_tensor("ps%d" % i, [128, 512], F32).ap() for i in range(8)]
        psum_T = [Tk() for _ in range(8)]
        T_const = Tk()
        T_minit = Tk()
        T_convinit = Tk()
        T_cinit = Tk()
        hT_T = {}
        xnT_T = {}
        self.pcnt = 0

        def hTk(k, t0):
            return hT_T.setdefault((k, t0), Tk())

        def xnTk(k, t0):
            return xnT_T.setdefault((k, t0), Tk())

        def pick(lo=0, hi=6):
            self.pcnt += 1
            return lo + (self.pcnt % (hi - lo))

        kb.dma(SP, cst, cst_d, writes=[T_const])
        kb.dma(SP, pcol, pcol_d, writes=[T_const])
        kb.dma(SP, flag, flag_d, writes=[T_const])
        kb.op(POOL, lambda e: e.memset(onesb, 1.0), writes=[T_const])
        kb.op(POOL, lambda e: e.tensor_copy(out=identb, in_=cst[:, 0:128]), reads=[T_const], writes=[T_const])
        kb.op(POOL, lambda e: e.memset(epsc, EPS), writes=[T_const])
        kb.op(POOL, lambda e: e.memset(onec, 1.0), writes=[T_const])
        maskneg = cst[:, 128:256]

        def tts_of(t0):
            return (t0 // 512) * 512

        def load_T(x_dram, nt, stack):
            xrow = [self.scr(stack, "xrow%d" % i, [128, D], F32) for i in range(2)]
            xrow_T = [Tk(), Tk()]
            nrt = (nt + 127) // 128
            cnt = 0
            for r in range(nrt):
                rows = min(128, nt - r * 128)
                b = r % 2
                kb.dma(SP, xrow[b][:rows, :], x_dram[r * 128:r * 128 + rows, :], writes=[xrow_T[b]])
                tt0 = tts_of(r * 128)
                for kg in range(4):
                    pi = 6 + (cnt % 2)
                    outs = [psum[pi][:, i * 128:i * 128 + rows] for i in range(4)]
                    pairs = [(xrow[b][:rows, (kg * 4 + i) * 128:(kg * 4 + i + 1) * 128], ident[:rows, :rows])
                             for i in range(4)]
                    kb.mm(outs, pairs, reads=[xrow_T[b], T_const], writes=[psum_T[pi]], transpose=True)
                    src = psum[pi].rearrange("p (a b) -> p a b", a=4)[:, :, 0:rows]
                    dst = hT[:, kg * 4:(kg + 1) * 4, r * 128:r * 128 + rows]
                    wr = [hTk(kg * 4 + i, tt0) for i in range(4)]
                    if cnt % 2 == 0:
                        kb.op(ACT, lambda e: e.copy(out=dst, in_=src), reads=[psum_T[pi]], writes=wr)
                    else:
                        kb.op(DVE, lambda e: e.tensor_copy(out=dst, in_=src), reads=[psum_T[pi]], writes=wr)
                    cnt += 1

        def rstd_from(srcs, n, inv_count, sq, sq_T, rs_ap, rs_T, pi):
            ns_ = len(srcs)
            for k, (ap, tks) in enumerate(srcs):
                b = k % 2
                kb.op(ACT, lambda e: e.activation(out=sq[b][:, :n], in_=ap, func=AF.Square),
                      reads=tks, writes=[sq_T[b]])
                E = kb.pe
                kb._deps(E, [sq_T[b], T_const], [psum_T[pi]] if k == 0 else [])
                inst = nc.tensor.matmul(psum[pi][:, :n], lhsT=onesb, rhs=sq[b][:, :n], start=(k == 0),
                                        stop=(k == ns_ - 1))
                E.seq += 1
                inst.then_inc(E.sem, 1)
                tok = (E.sem, E.seq, E)
                kb._mark(tok, [sq_T[b]], [psum_T[pi]] if k == ns_ - 1 else [])
                if k != ns_ - 1:
                    psum_T[pi].w = tok
            kb.op(ACT, lambda e: e.activation(out=rs_ap, in_=psum[pi][:, :n], func=AF.Ln,
                                              scale=inv_count, bias=epsc),
                  reads=[psum_T[pi], T_const], writes=[rs_T])
            kb.op(ACT, lambda e: e.activation(out=rs_ap, in_=rs_ap, func=AF.Exp, scale=-0.5),
                  reads=[rs_T], writes=[rs_T])

        def rmsnorm(nt, gcol0, stack, out_bf16=True):
            sq = [self.scr(stack, "nsq%d" % i, [128, 512], BF16) for i in range(2)]
            sq_T = [Tk(), Tk()]
            rs = self.scr(stack, "nrs", [128, 512], F32)
            rs_T = Tk()
            for (t0, n) in token_tiles(nt):
                rstd_from([(hT[:, k, t0:t0 + n], [hTk(k, t0)]) for k in range(KD)], n, 1.0 / D,
                          sq, sq_T, rs[:, :n], rs_T, 6)
                for k in range(KD):
                    if out_bf16:
                        o_ap, o_T = xnT[:, k, t0:t0 + n], [xnTk(k, t0)]
                    else:
                        o_ap, o_T = hT[:, k, t0:t0 + n], [hTk(k, t0)]
                    kb.op(DVE, lambda e: e.scalar_tensor_tensor(out=o_ap, in0=hT[:, k, t0:t0 + n],
                                                                scalar=pcol[:, gcol0 + k:gcol0 + k + 1],
                                                                in1=rs[:, :n], op0=ALU.mult, op1=ALU.mult),
                          reads=[hTk(k, t0), rs_T, T_const], writes=o_T)

        def ffn(nt, wg, wu, wd, stack):
            tts = token_tiles(nt)
            NH = 22
            ring = Ring(self, stack, 5)
            actT = self.scr(stack, "actT", [128, NH, nt], BF16)
            act_T = {}
            sg = [self.scr(stack, "sg%d" % i, [128, 512], F32) for i in range(3)]
            sg_T = [Tk(), Tk(), Tk()]
            PG, PU, PD = [0, 1, 4], [2, 3, 5], [4, 5, 0, 1]
            cnt = 0
            for (j0, j1) in [(0, NH), (NH, NFF)]:
                j = j0
                while j < j1:
                    nj = min(2, j1 - j)
                    gv, gT = ring.load(wg, 0, KD, j * 128, nj * 128)
                    uv, uT = ring.load(wu, 0, KD, j * 128, nj * 128)
                    for jj in range(nj):
                        jl = j + jj - j0
                        for (t0, n) in tts:
                            b = cnt % 3
                            cnt += 1
                            pg, pu = PG[b], PU[b]
                            rd = [xnTk(k, t0) for k in range(KD)]
                            kb.mm(psum[pg][:, :n],
                                  [(gv[:, k, jj * 128:(jj + 1) * 128], xnT[:, k, t0:t0 + n]) for k in range(KD)],
                                  reads=rd + [gT], writes=[psum_T[pg]])
                            kb.mm(psum[pu][:, :n],
                                  [(uv[:, k, jj * 128:(jj + 1) * 128], xnT[:, k, t0:t0 + n]) for k in range(KD)],
                                  reads=rd + [uT], writes=[psum_T[pu]])
                            kb.op(ACT, lambda e: e.activation(out=sg[b][:, :n], in_=psum[pg][:, :n], func=AF.Silu),
                                  reads=[psum_T[pg]], writes=[sg_T[b]])
                            aT = act_T.setdefault((jl, t0), Tk())
                            kb.op(DVE, lambda e: e.tensor_tensor(out=actT[:, jl, t0:t0 + n], in0=sg[b][:, :n],
                                                                 in1=psum[pu][:, :n], op=ALU.mult),
                                  reads=[sg_T[b], psum_T[pu]], writes=[aT])
                    j += nj
                njh = j1 - j0
                for i in range(KD):
                    dv, dT = ring.load(wd, j0, njh, i * 128, 128)
                    for (t0, n) in tts:
                        pd = PD[cnt % 4]
                        cnt += 1
                        kb.mm(psum[pd][:, :n],
                              [(dv[:, jl, :], actT[:, jl, t0:t0 + n]) for jl in range(njh)],
                              reads=[act_T[(jl, t0)] for jl in range(njh)] + [dT], writes=[psum_T[pd]])
                        kb.op(DVE, lambda e: e.scalar_tensor_tensor(out=hT[:, i, t0:t0 + n], in0=psum[pd][:, :n],
                                                                    scalar=0.5, in1=hT[:, i, t0:t0 + n],
                                                                    op0=ALU.mult, op1=ALU.add),
                              reads=[psum_T[pd], hTk(i, t0)], writes=[hTk(i, t0)])

        def final_out(y_dram, nt, stack):
            sq = [self.scr(stack, "fsq%d" % i, [128, 512], BF16) for i in range(2)]
            sq_T = [Tk(), Tk()]
            rs = self.scr(stack, "frs", [128, 512], F32)
            rs_T = Tk()
            gfb = self.scr(stack, "gfb_sb", [128, D], F32)
            gfb_T = Tk()
            rcol = self.scr(stack, "rcol", [128, 16], F32)
            rcol_T = Tk()
            yrow = [self.scr(stack, "yrow%d" % i, [128, D], F32) for i in range(2)]
            yrow_T = [Tk(), Tk()]
            kb.dma(SP, gfb, gfb_d, writes=[gfb_T])
            cnt = 0
            for (t0, n) in token_tiles(nt):
                rstd_from([(hT[:, k, t0:t0 + n], [hTk(k, t0)]) for k in range(KD)], n, 1.0 / D,
                          sq, sq_T, rs[:, :n], rs_T, 6)
                rts = [(r, min(128, nt - r * 128)) for r in range((nt + 127) // 128)
                       if t0 <= r * 128 < t0 + n]
                for (r, rows) in rts:
                    off = r * 128 - t0
                    kb.mm([psum[7][:rows, 0:1]], [(rs[0:1, off:off + rows], ident[0:1, 0:1])],
                          reads=[rs_T, T_const], writes=[psum_T[7]], transpose=True)
                    kb.op(ACT, lambda e: e.copy(out=rcol[:rows, r:r + 1], in_=psum[7][:rows, 0:1]),
                          reads=[psum_T[7]], writes=[rcol_T])
                for (r, rows) in rts:
                    b = r % 2
                    for kg in range(4):
                        pi = 6 + (cnt % 2)
                        cnt += 1
                        outs = [psum[pi][:rows, i * 128:(i + 1) * 128] for i in range(4)]
                        pairs = [(hT[:, kg * 4 + i, r * 128:r * 128 + rows], ident) for i in range(4)]
                        kb.mm(outs, pairs, reads=[hTk(kg * 4 + i, t0) for i in range(4)] + [T_const],
                              writes=[psum_T[pi]], transpose=True)
                        kb.op(DVE, lambda e: e.scalar_tensor_tensor(out=yrow[b][:rows, kg * 512:(kg + 1) * 512],
                                                                    in0=psum[pi][:rows, :],
                                                                    scalar=rcol[:rows, r:r + 1],
                                                                    in1=gfb[:rows, kg * 512:(kg + 1) * 512],
                                                                    op0=ALU.mult, op1=ALU.mult),
                              reads=[psum_T[pi], rcol_T, gfb_T], writes=[yrow_T[b]])
                    kb.dma(SP, y_dram[r * 128:r * 128 + rows, :], yrow[b][:rows, :], reads=[yrow_T[b]])

        def store_T(y_dram, nt, stack):
            yrow = [self.scr(stack, "yrow%d" % i, [128, D], F32) for i in range(2)]
            yrow_T = [Tk(), Tk()]
            nrt = (nt + 127) // 128
            cnt = 0
            for r in range(nrt):
                rows = min(128, nt - r * 128)
                b = r % 2
                tt0 = tts_of(r * 128)
                for kg in range(4):
                    pi = 6 + (cnt % 2)
                    outs = [psum[pi][:rows, i * 128:(i + 1) * 128] for i in range(4)]
                    pairs = [(hT[:, kg * 4 + i, r * 128:r * 128 + rows], ident) for i in range(4)]
                    kb.mm(outs, pairs, reads=[hTk(kg * 4 + i, tt0) for i in range(4)] + [T_const],
                          writes=[psum_T[pi]], transpose=True)
                    dst = yrow[b][:rows, kg * 512:(kg + 1) * 512]
                    src = psum[pi][:rows, :]
                    if cnt % 2 == 0:
                        kb.op(ACT, lambda e: e.copy(out=dst, in_=src), reads=[psum_T[pi]], writes=[yrow_T[b]])
                    else:
                        kb.op(DVE, lambda e: e.tensor_copy(out=dst, in_=src), reads=[psum_T[pi]], writes=[yrow_T[b]])
                    cnt += 1
                kb.dma(SP, y_dram[r * 128:r * 128 + rows, :], yrow[b][:rows, :], reads=[yrow_T[b]])

        def dense_fm(view, vT, ncols, nt, evac, m_rows=128):
            for ec in range((ncols + m_rows - 1) // m_rows):
                m = min(m_rows, ncols - ec * m_rows)
                for (t0, n) in token_tiles(nt):
                    pi = pick()
                    kb.mm(psum[pi][:m, :n],
                          [(view[:, k, ec * m_rows:ec * m_rows + m], xnT[:, k, t0:t0 + n]) for k in range(KD)],
                          reads=[xnTk(k, t0) for k in range(KD)] + [vT], writes=[psum_T[pi]])
                    evac(ec, t0, n, psum[pi][:m, :n], psum_T[pi])

        def dense_tm(view, vT, ncols, nt, evac):
            for r in range((nt + 127) // 128):
                rows = min(128, nt - r * 128)
                pi = pick()
                kb.mm(psum[pi][:rows, :ncols],
                      [(xnT[:, k, r * 128:r * 128 + rows], view[:, k, 0:ncols]) for k in range(KD)],
                      reads=[xnTk(k, tts_of(r * 128)) for k in range(KD)] + [vT], writes=[psum_T[pi]])
                evac(r, rows, psum[pi][:rows, :ncols], psum_T[pi])

        def gate_rows(nt, nscan, m0, m0_reads, stack, ring):
            R = {}

            def row(name, p=4, n=nt):
                R[name] = self.scr(stack, "r_" + name, [p, n], F32)
                R[name + "_T"] = Tk()
                return R[name], R[name + "_T"]

            bg = self.scr(stack, "bg_sb", [4, 2], F32)
            bg_T = Tk()
            kb.dma(SP, bg, bg_d, writes=[bg_T])
            kb.op(DVE, lambda e: e.tensor_scalar(out=bg, in0=bg, scalar1=1.0 / 15.0, scalar2=None, op0=ALU.mult),
                  reads=[bg_T], writes=[bg_T])
            gv, gT = ring.load(w["w_in"], 0, KD, 4096, 8)
            graw, graw_T = row("graw", 8)
            fgraw, fgraw_T = row("fgraw")

            def ev(ec, t0, n, ps, psT):
                kb.op(ACT, lambda e: e.copy(out=graw[:, t0:t0 + n], in_=ps), reads=[psT], writes=[graw_T])

            dense_fm(gv, gT, 8, nt, ev, m_rows=8)
            kb.dma(SP, fgraw, graw[4:8, :], reads=[graw_T], writes=[fgraw_T])
            logi, logi_T = row("logi")
            logf, logf_T = row("logf")
            kb.op(ACT, lambda e: e.activation(out=logi, in_=graw[0:4, :], func=AF.Tanh, scale=1.0 / 15.0,
                                              bias=bg[:, 0:1]), reads=[graw_T, bg_T], writes=[logi_T])
            kb.op(ACT, lambda e: e.activation(out=logf, in_=fgraw, func=AF.Tanh, scale=1.0 / 15.0,
                                              bias=bg[:, 1:2]), reads=[fgraw_T, bg_T], writes=[logf_T])
            kb.op(ACT, lambda e: e.activation(out=logf, in_=logf, func=AF.Exp, scale=-15.0),
                  reads=[logf_T], writes=[logf_T])
            kb.op(ACT, lambda e: e.activation(out=logf, in_=logf, func=AF.Ln, scale=1.0, bias=onec[0:4, :]),
                  reads=[logf_T, T_const], writes=[logf_T])
            kb.op(DVE, lambda e: e.tensor_scalar(out=logf, in0=logf, scalar1=-1.0, scalar2=None, op0=ALU.mult),
                  reads=[logf_T], writes=[logf_T])
            kb.op(DVE, lambda e: e.tensor_scalar(out=logi, in0=logi, scalar1=15.0, scalar2=None, op0=ALU.mult),
                  reads=[logi_T], writes=[logi_T])
            zeros, zeros_T = row("zeros", 4, nscan)
            kb.op(POOL, lambda e: e.memset(zeros, 0.0), writes=[zeros_T])
            Bc, Bc_T = row("B", 4, nscan)
            mr, mr_T = row("m", 4, nscan)
            kb.op(DVE, lambda e: e.tensor_tensor_scan(out=Bc, data0=logf[:, 0:nscan], data1=zeros, initial=0.0,
                                                      op0=ALU.add, op1=ALU.add),
                  reads=[logf_T, zeros_T], writes=[Bc_T])
            kb.op(DVE, lambda e: e.tensor_tensor_scan(out=mr, data0=logf[:, 0:nscan], data1=logi[:, 0:nscan],
                                                      initial=m0, op0=ALU.add, op1=ALU.max),
                  reads=[logf_T, logi_T] + m0_reads, writes=[mr_T])
            return R

        def transpose_rows(row_ap, rowT, nrow, nchunks, dst_ap, dst_T, pi):
            outs = [psum[pi][:, c * nrow:(c + 1) * nrow] for c in range(nchunks)]
            pairs = [(row_ap[0:nrow, c * 128:(c + 1) * 128], ident[:nrow, :nrow]) for c in range(nchunks)]
            kb.mm(outs, pairs, reads=[rowT, T_const], writes=[psum_T[pi]], transpose=True)
            src = psum[pi][:, 0:nchunks * nrow].rearrange("p (c h) -> p c h", c=nchunks)
            kb.op(DVE, lambda e: e.tensor_copy(out=dst_ap, in_=src), reads=[psum_T[pi]], writes=[dst_T])

        def prefix_state(G, stack):
            ring = Ring(self, stack, 4)
            R = gate_rows(NT, NPR, 0.0, [], stack, ring)
            logi, logf, Bc, mr = R["logi"], R["logf"], R["B"], R["m"]
            a = self.scr(stack, "pa", [4, NPR], F32)
            a_T = Tk()
            kb.op(DVE, lambda e: e.tensor_tensor(out=a, in0=logi[:, 0:NPR], in1=Bc, op=ALU.subtract),
                  reads=[R["logi_T"], R["B_T"]], writes=[a_T])
            nME = self.scr(stack, "pnME", [4, 1], F32)
            nME_T = Tk()
            kb.op(DVE, lambda e: e.tensor_tensor(out=nME, in0=Bc[:, NPR - 1:NPR], in1=mr[:, NPR - 1:NPR],
                                                 op=ALU.subtract),
                  reads=[R["B_T"], R["m_T"]], writes=[nME_T])
            kb.op(ACT, lambda e: e.activation(out=a, in_=a, func=AF.Exp, scale=1.0, bias=nME),
                  reads=[a_T, nME_T], writes=[a_T])
            wcol = self.scr(stack, "pwcol", [128, 8, 4], F32)
            wcol_T = Tk()
            transpose_rows(a, a_T, 4, 8, wcol, wcol_T, 7)
            T_src = []

            def src_tk():
                T_src.append(Tk())
                return T_src[-1]

            T_srca = []

            def srca_tk():
                T_srca.append(Tk())
                return T_srca[-1]

            with nc.allow_non_contiguous_dma(reason="tiny m row"):
                kb.dma(SP, xsa[16:17, 0:4].rearrange("o h -> h o"), mr[:, NPR - 1:NPR], reads=[R["m_T"]],
                       writes=[srca_tk()])
            xnp32 = self.scr(stack, "xnp32", [128, KD, 2], F32)
            x32_T = Tk()
            kb.op(DVE, lambda e: e.tensor_copy(out=xnp32, in_=xnT[:, :, NPR - 2:NPR]),
                  reads=[xnTk(k, 512) for k in range(KD)], writes=[x32_T])
            with nc.allow_non_contiguous_dma(reason="2-token boundary block"):
                kb.dma(SP, xsa[0:16, 0:256].rearrange("k (p j) -> p k j", j=2), xnp32, reads=[x32_T],
                       writes=[srca_tk()])
            ktok = self.scr(stack, "pktok", [128, 8, 256], BF16)
            ktok_T = Tk()
            vext = self.scr(stack, "pvext", [128, 8, 258], BF16)
            vext_T = Tk()
            wk = self.scr(stack, "pwk", [128, 8, 256], BF16)
            wk_T = Tk()
            cst_t = self.scr(stack, "pcst", [128, 2, 257], F32)
            cst_T = Tk()
            kb.op(POOL, lambda e: e.memset(vext[:, :, 256:258], 1.0), writes=[vext_T])
            for h in range(H):
                kv, kT_ = ring.load(w["w_in"], 0, KD, 1024 + h * 256, 256)

                def evk(r, rows, ps, psT):
                    kb.op(ACT, lambda e: e.activation(out=ktok[:rows, r, :], in_=ps, func=AF.Copy, scale=0.0625),
                          reads=[psT], writes=[ktok_T])

                dense_tm(kv, kT_, 256, NPR, evk)
                vv, vT_ = ring.load(w["w_in"], 0, KD, 2048 + h * 256, 256)

                def evv(r, rows, ps, psT):
                    kb.op(ACT, lambda e: e.copy(out=vext[:rows, r, 0:256], in_=ps), reads=[psT], writes=[vext_T])

                dense_tm(vv, vT_, 256, NPR, evv)
                for c in range(8):
                    kb.op(DVE, lambda e: e.tensor_scalar(out=wk[:, c, :], in0=ktok[:, c, :],
                                                         scalar1=wcol[:, c, h:h + 1], scalar2=None, op0=ALU.mult),
                          reads=[ktok_T, wcol_T], writes=[wk_T])
                for dc in range(2):
                    pi = pick()
                    kb.mm(psum[pi][:, 0:257],
                          [(wk[:, c, dc * 128:(dc + 1) * 128], vext[:, c, 0:257]) for c in range(8)],
                          reads=[wk_T, vext_T], writes=[psum_T[pi]])
                    kb.op(ACT, lambda e: e.copy(out=cst_t[:, dc, :], in_=psum[pi][:, 0:257]),
                          reads=[psum_T[pi]], writes=[cst_T])
                kb.dma(SP, xsrc[h * 256:(h + 1) * 256, 0:257].rearrange("(dc p) e -> p dc e", p=128), cst_t,
                       reads=[cst_T], writes=[src_tk()])
                if h == 0:
                    kb._deps(POOL, T_srca, [T_xda])
                    inst = nc.gpsimd.collective_compute("AllGather", ALU.bypass,
                                                        replica_groups=[[0, 1], [2, 3], [4, 5], [6, 7]],
                                                        ins=[xsa], outs=[xda])
                    inst.then_inc(cc_sem_a)
                    kb._mark((cc_sem_a, 1, None), T_srca, [T_xda])
                    gates_pre(G, stack, R)
                    snrow0 = self.scr(stack, "snrow0", [64, 256], F32)
                    snrow0_T = Tk()
                    kb.dma(SP, snrow0, sn_d, writes=[snrow0_T])
                    for dc in range(2):
                        pi = 6 + dc
                        kb.mm([psum[pi][:, 0:64]], [(snrow0[:, dc * 128:(dc + 1) * 128], ident[:64, :64])],
                              reads=[snrow0_T, T_const], writes=[psum_T[pi]], transpose=True)
                        kb.op(DVE, lambda e: e.tensor_copy(out=G["nTt"][:, dc, :], in_=psum[pi][:, 0:64]),
                              reads=[psum_T[pi]], writes=[G["nTt_T"]])

                if h == 2:
                    mraw = self.scr(stack, "mraw", [4, 1], F32)
                    mraw_T = Tk()
                    with nc.allow_non_contiguous_dma(reason="tiny m row"):
                        kb.dma(SP, mraw, xda[16:17, 0:4].rearrange("o h -> h o"), reads=[T_xda], writes=[mraw_T])
                    kb.op(DVE, lambda e: e.tensor_scalar(out=minit, in0=mraw, scalar1=flag[0:4, :], scalar2=None,
                                                         op0=ALU.mult),
                          reads=[mraw_T, T_const], writes=[T_minit])
                    with nc.allow_non_contiguous_dma(reason="2-token boundary block"):
                        kb.dma(SP, xnp32, xda[0:16, 0:256].rearrange("k (p j) -> p k j", j=2), reads=[T_xda],
                               writes=[x32_T])
                    kb.op(DVE, lambda e: e.tensor_copy(out=xnpre, in_=xnp32), reads=[x32_T], writes=[T_convinit])
                    mr2 = self.scr(stack, "mr2", [4, NPR], F32)
                    mr2_T = Tk()
                    kb.op(DVE, lambda e: e.tensor_tensor_scan(out=mr2, data0=logf[:, 0:NPR], data1=logi[:, 0:NPR],
                                                              initial=minit, op0=ALU.add, op1=ALU.max),
                          reads=[R["logf_T"], R["logi_T"], T_minit], writes=[mr2_T])
                    gates_post(G, stack, R, mr2, mr2_T)

            kb._deps(POOL, T_src, [T_xdst])
            inst = nc.gpsimd.collective_compute("AllGather", ALU.bypass,
                                                replica_groups=[[0, 1], [2, 3], [4, 5], [6, 7]],
                                                ins=[xsrc], outs=[xdst])
            inst.then_inc(cc_sem)
            kb._mark((cc_sem, 1, None), T_src, [T_xdst])

        def gates_post(G, stack, R, mr, mr_T):
            logi, logf, Bc = R["logi"], R["logf"], R["B"]
            a = self.scr(stack, "ma", [4, NPR], F32)
            a_T = Tk()
            negMx = self.scr(stack, "mnegMx", [4, 1025], F32)
            negMx_T = Tk()
            negm = self.scr(stack, "mnegm", [4, NPR], F32)
            negm_T = Tk()
            kb.op(DVE, lambda e: e.tensor_tensor(out=a, in0=logi[:, 0:NPR], in1=Bc, op=ALU.subtract),
                  reads=[R["logi_T"], R["B_T"]], writes=[a_T])
            kb.op(DVE, lambda e: e.tensor_tensor(out=negMx[:, 1:1025], in0=Bc, in1=mr, op=ALU.subtract),
                  reads=[R["B_T"], mr_T], writes=[negMx_T])
            kb.op(DVE, lambda e: e.tensor_scalar(out=negMx[:, 0:1], in0=minit, scalar1=-1.0, scalar2=None,
                                                 op0=ALU.mult),
                  reads=[T_minit], writes=[negMx_T])
            kb.op(DVE, lambda e: e.tensor_scalar(out=negm, in0=mr, scalar1=-1.0, scalar2=None, op0=ALU.mult),
                  reads=[mr_T], writes=[negm_T])
            kb.dma(SP, negM_d, negMx, reads=[negMx_T], writes=[G["negM_dT"]])
            kb.dma(SP, mp_o, mr[:, NPR - 1:NPR], reads=[mr_T])
            Gcol = G["Gcol"]
            transpose_rows(a, a_T, 4, 8, Gcol[:, :, 0, :], G["Gcol_T"], 7)
            transpose_rows(negMx[:, 1:1025], negMx_T, 4, 8, Gcol[:, :, 1, :], G["Gcol_T"], 6)
            transpose_rows(negm, negm_T, 4, 8, Gcol[:, :, 2, :], G["Gcol_T"], 7)
            kb.op(ACT, lambda e: e.activation(out=Gcol[:, :, 2, :], in_=Gcol[:, :, 2, :], func=AF.Exp),
                  reads=[G["Gcol_T"]], writes=[G["Gcol_T"]])

        def gates_pre(G, stack, R):
            logi, logf = R["logi"], R["logf"]
            mold = self.scr(stack, "smold", [4, NS], F32)
            mold_T = Tk()
            with nc.allow_non_contiguous_dma(reason="tiny state transpose"):
                kb.dma(SP, mold, sm_d.rearrange("s h -> h s"), writes=[mold_T])
            srow = self.scr(stack, "srow", [4, 48], F32)
            srow_T = Tk()
            mnew = self.scr(stack, "smnew", [4, NS], F32)
            mnew_T = Tk()
            lfs, lis = logf[:, NPR:NT], logi[:, NPR:NT]
            kb.op(DVE, lambda e: e.tensor_tensor(out=mold, in0=mold, in1=lfs, op=ALU.add),
                  reads=[mold_T, R["logf_T"]], writes=[mold_T])
            kb.op(DVE, lambda e: e.tensor_tensor(out=mnew, in0=mold, in1=lis, op=ALU.max),
                  reads=[mold_T, R["logi_T"]], writes=[mnew_T])
            kb.op(DVE, lambda e: e.tensor_tensor(out=srow[:, 0:16], in0=mold, in1=mnew, op=ALU.subtract),
                  reads=[mold_T, mnew_T], writes=[srow_T])
            kb.op(DVE, lambda e: e.tensor_tensor(out=srow[:, 16:32], in0=lis, in1=mnew, op=ALU.subtract),
                  reads=[R["logi_T"], mnew_T], writes=[srow_T])
            kb.op(DVE, lambda e: e.tensor_scalar(out=srow[:, 32:48], in0=mnew, scalar1=-1.0, scalar2=None,
                                                 op0=ALU.mult),
                  reads=[mnew_T], writes=[srow_T])
            kb.op(ACT, lambda e: e.activation(out=srow, in_=srow, func=AF.Exp), reads=[srow_T], writes=[srow_T])
            kb.dma(SP, sgs_d, srow, reads=[srow_T], writes=[G["sgs_dT"]])
            with nc.allow_non_contiguous_dma(reason="tiny state transpose"):
                kb.dma(SP, ms_o.rearrange("s h -> h s"), mnew, reads=[mnew_T])
            kb.dma(SP, G["sgb"], sgs_d.rearrange("h c -> (h c)").partition_broadcast(128),
                   reads=[G["sgs_dT"]], writes=[G["sgb_T"]])
            with nc.allow_non_contiguous_dma(reason="tiny state transpose"):
                kb.dma(SP, G["semt"], sgs_d[:, 32:48].rearrange("h s -> s h"), reads=[G["sgs_dT"]],
                       writes=[G["sgb_T"]])
                kb.dma(SP, G["swt"], sgs_d[:, 16:32].rearrange("h s -> s h"), reads=[G["sgs_dT"]],
                       writes=[G["sgb_T"]])


        def mix_gates(G, stack):
            ring = Ring(self, stack, 2)
            R = run_gen(gate_rows(NT, NPR, minit, [T_minit], stack, ring)) if False else gate_rows(NT, NPR, minit, [T_minit], stack, ring)
            gates_post(G, stack, R, R["m"], R["m_T"])
            gates_pre(G, stack, R)

        def finish_h(P_, NC_, num, num_T, emt, emt_T, nmso_fn, nmso_T, h, col_fn, tiles, gen=None, npre=0, nper=0):
            sc, sc_T, junk, junk_T, ha, ha_T = tiles
            S = lambda q: sc[:P_, q, 0:NC_]
            kb.op(ACT, lambda e: e.activation(out=S(0), in_=num[:P_, :, 256], func=AF.Abs),
                  reads=[num_T], writes=[sc_T])
            kb.op(DVE, lambda e: e.tensor_tensor(out=S(0), in0=S(0), in1=emt, op=ALU.max),
                  reads=[sc_T, emt_T], writes=[sc_T])
            kb.op(DVE, lambda e: e.reciprocal(out=S(1), in_=S(0)), reads=[sc_T], writes=[sc_T])
            for c in range(NC_):
                kb.op(ACT, lambda e: e.activation(out=junk[:P_, :], in_=num[:P_, c, 0:256], func=AF.Square,
                                                  accum_out=sc[:P_, 2, c:c + 1]),
                      reads=[num_T, sc_T], writes=[junk_T, sc_T])
            kb.op(DVE, lambda e: e.tensor_tensor(out=S(3), in0=S(2), in1=S(1), op=ALU.mult),
                  reads=[sc_T], writes=[sc_T])
            kb.op(DVE, lambda e: e.tensor_tensor(out=S(3), in0=S(3), in1=S(1), op=ALU.mult),
                  reads=[sc_T], writes=[sc_T])
            kb.op(ACT, lambda e: e.activation(out=S(4), in_=S(3), func=AF.Ln, scale=1.0 / 256.0,
                                              bias=epsc[:P_, :]), reads=[sc_T, T_const], writes=[sc_T])
            kb.op(ACT, lambda e: e.activation(out=S(4), in_=S(4), func=AF.Exp, scale=-0.5),
                  reads=[sc_T], writes=[sc_T])
            kb.op(DVE, lambda e: e.tensor_tensor(out=S(5), in0=S(4), in1=S(1), op=ALU.mult),
                  reads=[sc_T], writes=[sc_T])
            if gen is not None:
                for _ in range(npre):
                    next(gen, None)
            for c in range(NC_):
                b = c % 2
                col0 = col_fn(c)
                if gen is not None:
                    for _ in range(nper):
                        next(gen, None)
                kb.op(DVE, lambda e: e.scalar_tensor_tensor(out=ha[b][:P_, :], in0=num[:P_, c, 0:256],
                                                            scalar=sc[:P_, 5, c:c + 1], in1=nmso_fn(c),
                                                            op0=ALU.mult, op1=ALU.mult),
                      reads=[num_T, sc_T, nmso_T], writes=[ha_T[b]])
                pi = 6 + b
                outs = [psum[pi][:, ec * 128:ec * 128 + P_] for ec in range(2)]
                pairs = [(ha[b][:P_, ec * 128:(ec + 1) * 128], ident[:P_, :P_]) for ec in range(2)]
                kb.mm(outs, pairs, reads=[ha_T[b], T_const], writes=[psum_T[pi]], transpose=True)
                src = psum[pi][:, 0:256].rearrange("p (a b) -> p a b", a=2)[:, :, 0:P_]
                dst = self.mixT[:, 2 * h:2 * h + 2, col0:col0 + P_]
                kb.op(ACT, lambda e: e.copy(out=dst, in_=src), reads=[psum_T[pi]],
                      writes=[self.mix_T[(2 * h, tts_of(col0))], self.mix_T[(2 * h + 1, tts_of(col0))]])

        def mix_heads(G, stack):
            ring = Ring(self, stack, 2)
            Gcol, Gcol_T = G["Gcol"], G["Gcol_T"]
            qT = self.scr(stack, "qT", [128, 2, NT], BF16)
            kT = self.scr(stack, "kT", [128, 2, NT], BF16)
            ktok = self.scr(stack, "ktok", [128, 9, 256], BF16)
            vext = self.scr(stack, "vext", [128, 9, 258], BF16)
            nmso = self.scr(stack, "nmso", [128, 9, 256], BF16)
            nmb = self.scr(stack, "nmb", [128, 256], F32)
            negMb = self.scr(stack, "negMb", [128, 1025], F32)
            Mend = self.scr(stack, "Mend", [128, 9], F32)
            Cst = self.scr(stack, "Cst", [128, 2, 257], F32)
            Cbs = [self.scr(stack, "Cb%d" % i, [128, 2, 258], BF16) for i in range(2)]
            Cb_Ts = [Tk(), Tk()]
            arg = [self.scr(stack, "arg%d" % i, [128, 128], F32) for i in range(2)]
            Dm = arg
            SpT = [self.scr(stack, "SpT%d" % i, [128, 128], BF16) for i in range(2)]
            tmp1f = self.scr(stack, "tmp1", [128, 1, 258], F32)
            tmp1 = tmp1f[:, 0, 0:257]
            num = self.scr(stack, "num", [128, 8, 257], F32)
            sc = self.scr(stack, "sc", [128, 6, 8], F32)
            junk = self.scr(stack, "junk", [128, 256], BF16)
            ha0 = self.scr(stack, "ha0", [128, 256], F32)
            ha = [ha0, ha0]
            wkc = [self.scr(stack, "wkc%d" % i, [128, 256], BF16) for i in range(2)]
            cs = self.scr(stack, "cs", [128, 3, 8], F32)
            q_T, k_T, ktok_T, vext_T, nmso_T, nmb_T, negMb_T, Mend_T = [Tk() for _ in range(8)]
            Cst_T, Cb_T, tmp1_T, num_T, sc_T, junk_T, cs_T = [Tk() for _ in range(7)]
            arg_T, SpT_T, wkc_T = [[Tk(), Tk()] for _ in range(3)]
            ha_T0 = Tk()
            ha_T = [ha_T0, ha_T0]
            Dm_T = arg_T
            fin_tiles = (sc, sc_T, junk, junk_T, ha, ha_T)
            NB = 3
            snrow = ha[0][0:64, :]
            nTt, nTt_T = G["nTt"], G["nTt_T"]
            nTn = self.scr(stack, "nTn", [128, 2, 64], F32)
            Cs_ = [self.scr(stack, "Cs%d" % i, [128, 2, 257], F32) for i in range(NB)]
            Cnb = [self.scr(stack, "Cnb%d" % i, [128, 2, 258], BF16) for i in range(2)]
            vsel = [self.scr(stack, "vsel%d" % i, [NS, 258], BF16) for i in range(2)]
            qsel = self.scr(stack, "qsel", [128, 2, NS, NS], BF16)
            Wdg = self.scr(stack, "Wdg", [NS, NS], F32)
            hs = tmp1f
            snrow_T, nTn_T, qsel_T, Wdg_T = [Tk() for _ in range(4)]
            hs_T = tmp1_T
            Cs_T = [Tk() for _ in range(NB)]
            Cnb_T = [Tk(), Tk()]
            vsel_T = [Tk(), Tk()]
            sgb, sgb_T, semt, swt = G["sgb"], G["sgb_T"], G["semt"], G["swt"]
            eyeb = cst[:, 256:512].rearrange("p (a b) -> p a b", a=NS)

            kb.op(POOL, lambda e: e.memset(vext[:, :, 256:258], 1.0), writes=[vext_T])

            ks_s = self.scr(stack, "ks_s", [NS, 256], BF16)
            vs_s = self.scr(stack, "vs_s", [NS, 258], BF16)
            nm_s = self.scr(stack, "nm_s", [NS, 256], BF16)
            ks_sT, vs_sT, nm_sT = Tk(), Tk(), Tk()

            def prep(h):
                kb.dma(SP, negMb, negM_d[h:h + 1, :].partition_broadcast(128), reads=[G["negM_dT"]],
                       writes=[negMb_T])
                kb.op(DVE, lambda e: e.tensor_scalar(out=Mend[:, 0:8],
                                                     in0=negMb[:, 0:1024].rearrange("p (c t) -> p c t", t=128)[:, :, 0],
                                                     scalar1=-1.0, scalar2=None, op0=ALU.mult),
                      reads=[negMb_T], writes=[Mend_T])
                kb.op(DVE, lambda e: e.tensor_scalar(out=Mend[:, 8:9], in0=negMb[:, 1024:1025], scalar1=-1.0,
                                                     scalar2=None, op0=ALU.mult), reads=[negMb_T], writes=[Mend_T])
                kb.op(DVE, lambda e: e.tensor_tensor(out=cs[:, 0, :], in0=Gcol[:, :, 1, h], in1=Mend[:, 0:8], op=ALU.add),
                      reads=[Gcol_T, Mend_T], writes=[cs_T])
                kb.op(DVE, lambda e: e.tensor_tensor(out=cs[:, 1, :], in0=Gcol[:, :, 0, h], in1=Mend[:, 1:9],
                                                     op=ALU.subtract), reads=[Gcol_T, Mend_T], writes=[cs_T])
                kb.op(DVE, lambda e: e.tensor_tensor(out=cs[:, 2, :], in0=Mend[:, 0:8], in1=Mend[:, 1:9],
                                                     op=ALU.subtract), reads=[Mend_T], writes=[cs_T])
                kb.op(ACT, lambda e: e.activation(out=cs, in_=cs, func=AF.Exp), reads=[cs_T], writes=[cs_T])
                kb.dma(SP, Cst, xdst[h * 256:(h + 1) * 256, 0:257].rearrange("(dc p) e -> p dc e", p=128),
                       reads=[T_xdst], writes=[Cst_T])
                kb.op(DVE, lambda e: e.tensor_scalar(out=Cst, in0=Cst, scalar1=flag[:, 0:1], scalar2=None,
                                                     op0=ALU.mult), reads=[Cst_T, T_const], writes=[Cst_T])
                kb.op(ACT, lambda e: e.copy(out=Cbs[0][:, :, 0:257], in_=Cst), reads=[Cst_T], writes=[Cb_Ts[0]])


            def proj_gen(h, plo, phi):
                def fm(view, vT, evac):
                    for ec in range(2):
                        for (t0, n) in token_tiles(NT):
                            pi = pick(plo, phi)
                            kb.mm(psum[pi][:, :n],
                                  [(view[:, k, ec * 128:(ec + 1) * 128], xnT[:, k, t0:t0 + n]) for k in range(KD)],
                                  reads=[xnTk(k, t0) for k in range(KD)] + [vT], writes=[psum_T[pi]])
                            evac(ec, t0, n, psum[pi][:, :n], psum_T[pi])
                            yield

                def tm(view, vT, evac):
                    for r in range(9):
                        rows = min(128, NT - r * 128)
                        pi = pick(plo, phi)
                        kb.mm(psum[pi][:rows, :256],
                              [(xnT[:, k, r * 128:r * 128 + rows], view[:, k, 0:256]) for k in range(KD)],
                              reads=[xnTk(k, tts_of(r * 128)) for k in range(KD)] + [vT], writes=[psum_T[pi]])
                        evac(r, rows, psum[pi][:rows, :256], psum_T[pi])
                        yield

                qv, qvT = ring.load(w["w_in"], 0, KD, h * 256, 256)

                def evq(ec, t0, n, ps, psT):
                    kb.op(ACT, lambda e: e.copy(out=qT[:, ec, t0:t0 + n], in_=ps), reads=[psT], writes=[q_T])

                yield from fm(qv, qvT, evq)
                kv, kvT = ring.load(w["w_in"], 0, KD, 1024 + h * 256, 256)

                def evk(ec, t0, n, ps, psT):
                    kb.op(ACT, lambda e: e.activation(out=kT[:, ec, t0:t0 + n], in_=ps, func=AF.Copy, scale=0.0625),
                          reads=[psT], writes=[k_T])

                yield from fm(kv, kvT, evk)

                def evkt(r, rows, ps, psT):
                    kb.op(DVE, lambda e: e.tensor_scalar(out=ktok[:rows, r, :], in0=ps, scalar1=0.0625, scalar2=None,
                                                         op0=ALU.mult), reads=[psT], writes=[ktok_T])

                for r in range(9):
                    rows = min(128, NT - r * 128)
                    pi = pick(plo, phi)
                    pb = psum[pi].bitcast(BF16)
                    outs = [pb[:rows, dc * 128:(dc + 1) * 128] for dc in range(2)]
                    pairs = [(kT[:, dc, r * 128:r * 128 + rows], identb) for dc in range(2)]
                    kb.mm(outs, pairs, reads=[k_T, T_const], writes=[psum_T[pi]], transpose=True)
                    kb.op(DVE, lambda e: e.tensor_copy(out=ktok[:rows, r, :], in_=pb[:rows, 0:256]),
                          reads=[psum_T[pi]], writes=[ktok_T])
                    yield
                vv, vvT = ring.load(w["w_in"], 0, KD, 2048 + h * 256, 256)

                def evv(r, rows, ps, psT):
                    kb.op(ACT, lambda e: e.copy(out=vext[:rows, r, 0:256], in_=ps), reads=[psT], writes=[vext_T])

                yield from tm(vv, vvT, evv)
                ov, ovT = ring.load(w["w_in"], 0, KD, 3072 + h * 256, 256)
                kb.dma(SP, nmb, nmb_d[:, h * 256:(h + 1) * 256], writes=[nmb_T])

                def evo(r, rows, ps, psT):
                    kb.op(ACT, lambda e: e.activation(out=nmso[:rows, r, :], in_=ps, func=AF.Sigmoid),
                          reads=[psT], writes=[nmso_T])
                    kb.op(DVE, lambda e: e.tensor_tensor(out=nmso[:rows, r, :], in0=nmso[:rows, r, :],
                                                         in1=nmb[:rows, :], op=ALU.mult),
                          reads=[nmso_T, nmb_T], writes=[nmso_T])

                yield from tm(ov, ovT, evo)

            def load_state(h, j):
                b = j % NB
                idx = j * 4 + h
                kb.dma(SP, Cs_[b][:, :, 0:256], sC_d[j, h].rearrange("(dc p) e -> p dc e", p=128),
                       writes=[Cs_T[b]])
                kb.op(POOL, lambda e: e.tensor_copy(out=Cs_[b][:, :, 256], in_=nTt[:, :, idx]),
                      reads=[nTt_T], writes=[Cs_T[b]])

            def chunks(h, gen):
                for j in range(NB):
                    load_state(h, j)
                def scores(c):
                    t0 = c * 128
                    tsl = slice(t0, t0 + 128)
                    b = c % 2
                    kb.mm(psum[b][:, 0:128], [(kT[:, dc, tsl], qT[:, dc, tsl]) for dc in range(2)],
                          reads=[k_T, q_T], writes=[psum_T[b]])
                    kb.op(POOL, lambda e: e.tensor_tensor(out=arg[b], in0=negMb[:, 1 + t0:1 + t0 + 128], in1=maskneg,
                                                          op=ALU.add), reads=[negMb_T, T_const], writes=[arg_T[b]])
                    kb.op(ACT, lambda e: e.activation(out=Dm[b], in_=arg[b], func=AF.Exp, scale=1.0,
                                                      bias=Gcol[:, c, 0, h:h + 1]),
                          reads=[arg_T[b], Gcol_T], writes=[Dm_T[b]])
                    kb.op(DVE, lambda e: e.tensor_tensor(out=SpT[b], in0=psum[b][:, 0:128], in1=Dm[b], op=ALU.mult),
                          reads=[psum_T[b], Dm_T[b]], writes=[SpT_T[b]])
                    kb.op(ACT, lambda e: e.activation(out=wkc[b], in_=ktok[:, c, :], func=AF.Copy,
                                                      scale=cs[:, 1, c:c + 1]),
                          reads=[ktok_T, cs_T], writes=[wkc_T[b]])

                scores(0)
                for c in range(8):
                    t0 = c * 128
                    tsl = slice(t0, t0 + 128)
                    b = c % 2
                    cbn, cbo = Cbs[(c + 1) % 2], Cbs[c % 2]
                    cbn_T, cbo_T = Cb_Ts[(c + 1) % 2], Cb_Ts[c % 2]
                    for dc in range(2):
                        pu = 4 + dc
                        kb.mm(psum[pu][:, 0:257], [(wkc[b][:, dc * 128:(dc + 1) * 128], vext[:, c, 0:257])],
                              reads=[wkc_T[b], vext_T], writes=[psum_T[pu]])
                    kb.mm(psum[3][:, 0:257], [(qT[:, dc, tsl], cbo[:, dc, 0:257]) for dc in range(2)],
                          reads=[q_T, cbo_T], writes=[psum_T[3]])
                    for dc in range(2):
                        pu = 4 + dc
                        kb.op(DVE, lambda e: e.scalar_tensor_tensor(out=Cst[:, dc, :], in0=Cst[:, dc, :],
                                                                    scalar=cs[:, 2, c:c + 1], in1=psum[pu][:, 0:257],
                                                                    op0=ALU.mult, op1=ALU.add),
                              reads=[Cst_T, cs_T, psum_T[pu]], writes=[Cst_T])
                    kb.op(ACT, lambda e: e.copy(out=cbn[:, :, 0:257], in_=Cst), reads=[Cst_T], writes=[cbn_T])
                    if c + 1 < 8:
                        scores(c + 1)
                    kb.mm(psum[2][:, 0:257], [(SpT[b], vext[:, c, 0:257])], reads=[SpT_T[b], vext_T],
                          writes=[psum_T[2]])
                    kb.op(ACT, lambda e: e.activation(out=tmp1, in_=psum[3][:, 0:257], func=AF.Copy,
                                                      scale=cs[:, 0, c:c + 1]), reads=[psum_T[3], cs_T],
                          writes=[tmp1_T])
                    kb.op(DVE, lambda e: e.tensor_tensor(out=num[:, c, :], in0=tmp1, in1=psum[2][:, 0:257], op=ALU.add),
                          reads=[tmp1_T, psum_T[2]], writes=[num_T])
                kb.dma(ACT, Cp_o[h].rearrange("(dc p) e -> p dc e", p=128), Cst[:, :, 0:256], reads=[Cst_T])
                with nc.allow_non_contiguous_dma(reason="n state column"):
                    kb.dma(ACT, np_o[h].rearrange("(dc p) -> p dc", p=128), Cst[:, :, 256], reads=[Cst_T])
                snapshot(h)
                finish_h(128, 8, num, num_T, Gcol[:, :, 2, h], Gcol_T, lambda c: nmso[:, c, :], nmso_T, h,
                         lambda c: c * 128, fin_tiles, gen=gen, npre=4, nper=1)

            def snapshot(h):
                kb.op(DVE, lambda e: e.tensor_copy(out=ks_s, in_=ktok[0:NS, 8, :]), reads=[ktok_T], writes=[ks_sT])
                kb.op(DVE, lambda e: e.tensor_copy(out=vs_s, in_=vext[0:NS, 8, :]), reads=[vext_T], writes=[vs_sT])
                kb.op(DVE, lambda e: e.tensor_copy(out=nm_s, in_=nmso[0:NS, 8, :]), reads=[nmso_T], writes=[nm_sT])
                kb.op(DVE, lambda e: e.tensor_scalar(out=Wdg, in0=ident[0:NS, 0:NS], scalar1=swt[:, h:h + 1],
                                                     scalar2=None, op0=ALU.mult),
                      reads=[T_const, sgb_T], writes=[Wdg_T])
                for dc in range(2):
                    kb.op(DVE, lambda e: e.tensor_tensor(out=qsel[:, dc, :, :],
                                                         in0=qT[:, dc, NPR:NT].unsqueeze(1).broadcast_to([128, NS, NS]),
                                                         in1=eyeb, op=ALU.mult),
                          reads=[q_T, T_const], writes=[qsel_T])

            def sample(h, gen):
                def mk_vsel(j):
                    kb.op(ACT, lambda e: e.activation(out=vsel[j % 2], in_=vs_s, func=AF.Copy,
                                                      scale=Wdg[:, j:j + 1]),
                          reads=[vs_sT, Wdg_T], writes=[vsel_T[j % 2]])

                def matvec(j):
                    b2 = j % 2
                    E = kb.pe
                    kb._deps(E, [qsel_T, Cnb_T[b2]], [psum_T[3]] if j == 0 else [])
                    for dc in range(2):
                        inst = nc.tensor.matmul(psum[3][0:NS, 0:257], lhsT=qsel[:, dc, j, :], rhs=Cnb[b2][:, dc, 0:257],
                                                start=(j == 0 and dc == 0), stop=(j == NS - 1 and dc == 1))
                    E.seq += 1
                    inst.then_inc(E.sem, 1)
                    tok = (E.sem, E.seq, E)
                    kb._mark(tok, [qsel_T, Cnb_T[b2]], [psum_T[3]] if j == NS - 1 else [])
                    if j != NS - 1:
                        psum_T[3].w = tok

                mk_vsel(0)
                for j in range(NS):
                    b = j % NB
                    b2 = j % 2
                    idx = j * 4 + h
                    if j >= NB:
                        load_state(h, j)
                    for dc in range(2):
                        po = 4 + dc
                        kb.mm(psum[po][:, 0:257],
                              [(ks_s[:, dc * 128:(dc + 1) * 128], vsel[b2][:, 0:257])],
                              reads=[ks_sT, vsel_T[b2]], writes=[psum_T[po]])
                    if j > 0:
                        matvec(j - 1)
                    if j + 1 < NS:
                        mk_vsel(j + 1)
                    for dc in range(2):
                        po = 4 + dc
                        kb.op(DVE, lambda e: e.scalar_tensor_tensor(out=Cs_[b][:, dc, :], in0=Cs_[b][:, dc, :],
                                                                    scalar=sgb[:, h * 48 + j:h * 48 + j + 1],
                                                                    in1=psum[po][:, 0:257], op0=ALU.mult, op1=ALU.add),
                              reads=[Cs_T[b], sgb_T, psum_T[po]], writes=[Cs_T[b]])
                    kb.op(ACT, lambda e: e.copy(out=Cnb[b2][:, :, 0:257], in_=Cs_[b]), reads=[Cs_T[b]],
                          writes=[Cnb_T[b2]])
                    kb.dma(ACT, Cs_o[j, h].rearrange("(dc p) e -> p dc e", p=128), Cs_[b][:, :, 0:256],
                           reads=[Cs_T[b]])
                    kb.op(ACT, lambda e: e.copy(out=nTn[:, :, idx], in_=Cs_[b][:, :, 256]),
                          reads=[Cs_T[b]], writes=[nTn_T])
                    if gen is not None:
                        for _ in range(2 if j < 7 else 1):
                            next(gen, None)
                matvec(NS - 1)
                kb.op(ACT, lambda e: e.copy(out=hs[0:NS, 0, 0:257], in_=psum[3][0:NS, 0:257]), reads=[psum_T[3]],
                      writes=[hs_T])
                finish_h(NS, 1, hs, hs_T, semt[:, h:h + 1], sgb_T, lambda c: nm_s, nm_sT, h,
                         lambda c: NPR, fin_tiles, gen=gen, npre=4, nper=0)


            prep(0)
            for _ in proj_gen(0, 0, 6):
                pass
            for h in range(H):
                gen = proj_gen(h + 1, 0, 3) if h + 1 < H else None
                with nc.named_scope("h%d_chunks" % h):
                    chunks(h, gen)
                if h + 1 < H:
                    prep(h + 1)
                with nc.named_scope("h%d_sample" % h):
                    sample(h, gen)
                    if gen is not None:
                        for _ in gen:
                            pass
            for dc in range(2):
                pi = 6 + dc
                kb.mm([psum[pi][0:64, 0:128]], [(nTn[:, dc, :], ident)], reads=[nTn_T, T_const],
                      writes=[psum_T[pi]], transpose=True)
                kb.op(DVE, lambda e: e.tensor_copy(out=snrow[:, dc * 128:(dc + 1) * 128], in_=psum[pi][0:64, 0:128]),
                      reads=[psum_T[pi]], writes=[snrow_T])
            kb.dma(SP, ns_o, snrow, reads=[snrow_T])

        def mix_conv(stack):
            ring = Ring(self, stack, 3)
            gcs = [self.scr(stack, "gcs%d" % i, [128, NT], F32) for i in range(2)]
            gbs = [self.scr(stack, "gbs%d" % i, [128, NT], F32) for i in range(2)]
            uext = [self.scr(stack, "uext%d" % i, [128, NPR + 2], F32) for i in range(2)]
            us = [self.scr(stack, "us%d" % i, [128, NS], F32) for i in range(2)]
            y1 = [self.scr(stack, "y1%d" % i, [128, NT], F32) for i in range(2)]
            sq = [self.scr(stack, "csq%d" % i, [128, 512], BF16) for i in range(2)]
            rs = self.scr(stack, "crs", [128, 512], F32)
            scrow = self.scr(stack, "scrow", [32, 1024], F32)
            cbT = self.scr(stack, "cbT", [128, 8, 32], F32)
            crow = self.scr(stack, "crow", [2, 1024], F32)
            csrow = scrow[0:NS, :]
            rs_T, scrow_T, cbT_T, crow_T = [Tk() for _ in range(4)]
            csrow_T = scrow_T
            gcs_T, gbs_T, uext_T, us_T, y1_T, sq_T = [[Tk(), Tk()] for _ in range(6)]
            kb.dma(SP, scrow, sconv_d, writes=[scrow_T])
            for cch in range(8):
                pi = 6 + cch % 2
                kb.mm([psum[pi][:, 0:32]], [(scrow[:, cch * 128:(cch + 1) * 128], ident[:32, :32])],
                      reads=[scrow_T, T_const], writes=[psum_T[pi]], transpose=True)
                kb.op(DVE, lambda e: e.tensor_copy(out=cbT[:, cch, :], in_=psum[pi][:, 0:32]), reads=[psum_T[pi]],
                      writes=[cbT_T])
            views = {}

            def stage_a(cch):
                blk, ec = cch // 2, cch % 2
                b = cch % 2
                if ec == 0:
                    views["gc"] = ring.load(w["w_in"], 0, KD, 5128 + blk * 256, 256)
                    views["xc"] = ring.load(w["w_in"], 0, KD, 6152 + blk * 256, 256)
                    views["gb"] = ring.load(w["w_in"], 0, KD, 4104 + blk * 256, 256)
                esl = slice(ec * 128, (ec + 1) * 128)
                gcv, gcT = views["gc"]
                xcv, xcT = views["xc"]
                gbv, gbT = views["gb"]
                for (t0, n) in token_tiles(NT):
                    pi = pick()
                    kb.mm(psum[pi][:, :n], [(gcv[:, k, esl], xnT[:, k, t0:t0 + n]) for k in range(KD)],
                          reads=[xnTk(k, t0) for k in range(KD)] + [gcT], writes=[psum_T[pi]])
                    kb.op(ACT, lambda e: e.copy(out=gcs[b][:, t0:t0 + n], in_=psum[pi][:, :n]), reads=[psum_T[pi]],
                          writes=[gcs_T[b]])
                pi = pick()
                kb.mm(psum[pi][:, 0:2], [(gcv[:, k, esl], xnpre[:, k, :]) for k in range(KD)],
                      reads=[T_convinit, gcT], writes=[psum_T[pi]])
                kb.op(ACT, lambda e: e.activation(out=g2, in_=psum[pi][:, 0:2], func=AF.Copy, scale=flag[:, 0:1]),
                      reads=[psum_T[pi], T_const], writes=[T_g2])
                pi = pick()
                kb.mm(psum[pi][:, 0:2], [(xcv[:, k, esl], xnpre[:, k, :]) for k in range(KD)],
                      reads=[T_convinit, xcT], writes=[psum_T[pi]])
                kb.op(DVE, lambda e: e.tensor_tensor(out=uext[b][:, 0:2], in0=g2, in1=psum[pi][:, 0:2], op=ALU.mult),
                      reads=[psum_T[pi], T_g2], writes=[uext_T[b]])
                for (t0, n) in token_tiles(NT):
                    pi = pick()
                    kb.mm(psum[pi][:, :n], [(xcv[:, k, esl], xnT[:, k, t0:t0 + n]) for k in range(KD)],
                          reads=[xnTk(k, t0) for k in range(KD)] + [xcT], writes=[psum_T[pi]])
                    if t0 < NPR:
                        kb.op(DVE, lambda e: e.tensor_tensor(out=uext[b][:, 2 + t0:2 + t0 + n], in0=gcs[b][:, t0:t0 + n],
                                                             in1=psum[pi][:, :n], op=ALU.mult),
                              reads=[gcs_T[b], psum_T[pi]], writes=[uext_T[b]])
                    else:
                        kb.op(DVE, lambda e: e.tensor_tensor(out=us[b], in0=gcs[b][:, t0:t0 + n], in1=psum[pi][:, :n],
                                                             op=ALU.mult),
                              reads=[gcs_T[b], psum_T[pi]], writes=[us_T[b]])
                for (t0, n) in token_tiles(NT):
                    pi = pick()
                    kb.mm(psum[pi][:, :n], [(gbv[:, k, esl], xnT[:, k, t0:t0 + n]) for k in range(KD)],
                          reads=[xnTk(k, t0) for k in range(KD)] + [gbT], writes=[psum_T[pi]])
                    kb.op(ACT, lambda e: e.copy(out=gbs[b][:, t0:t0 + n], in_=psum[pi][:, :n]), reads=[psum_T[pi]],
                          writes=[gbs_T[b]])

            def stage_b(cch):
                b = cch % 2
                cw = [pcol[:, 64 + 8 * jx + cch:65 + 8 * jx + cch] for jx in range(3)]
                cbias = pcol[:, 88 + cch:89 + cch]
                ncol = pcol[:, 96 + cch:97 + cch]
                yy, yT_ = y1[b], y1_T[b]
                ue, ueT = uext[b], uext_T[b]
                kb.op(DVE, lambda e: e.tensor_scalar(out=yy[:, 0:NPR], in0=ue[:, 0:NPR], scalar1=cw[0],
                                                     scalar2=cbias, op0=ALU.mult, op1=ALU.add),
                      reads=[ueT, T_const], writes=[yT_])
                kb.op(DVE, lambda e: e.scalar_tensor_tensor(out=yy[:, 0:NPR], in0=ue[:, 1:NPR + 1], scalar=cw[1],
                                                            in1=yy[:, 0:NPR], op0=ALU.mult, op1=ALU.add),
                      reads=[ueT, yT_], writes=[yT_])
                kb.op(DVE, lambda e: e.scalar_tensor_tensor(out=yy[:, 0:NPR], in0=ue[:, 2:NPR + 2], scalar=cw[2],
                                                            in1=yy[:, 0:NPR], op0=ALU.mult, op1=ALU.add),
                      reads=[ueT, yT_], writes=[yT_])
                cb3 = cbT[:, cch, :].rearrange("p (s j) -> p s j", j=2)
                kb.op(DVE, lambda e: e.tensor_scalar(out=yy[:, NPR:NT], in0=cb3[:, :, 0], scalar1=cw[0],
                                                     scalar2=cbias, op0=ALU.mult, op1=ALU.add),
                      reads=[cbT_T, T_const, yT_], writes=[yT_])
                kb.op(DVE, lambda e: e.scalar_tensor_tensor(out=yy[:, NPR:NT], in0=cb3[:, :, 1], scalar=cw[1],
                                                            in1=yy[:, NPR:NT], op0=ALU.mult, op1=ALU.add),
                      reads=[cbT_T, yT_], writes=[yT_])
                kb.op(DVE, lambda e: e.scalar_tensor_tensor(out=yy[:, NPR:NT], in0=us[b], scalar=cw[2],
                                                            in1=yy[:, NPR:NT], op0=ALU.mult, op1=ALU.add),
                      reads=[us_T[b], yT_], writes=[yT_])
                kb.op(DVE, lambda e: e.tensor_tensor(out=yy, in0=yy, in1=gbs[b], op=ALU.mult),
                      reads=[yT_, gbs_T[b]], writes=[yT_])
                for (t0, n) in token_tiles(NT):
                    rstd_from([(yy[:, t0:t0 + n], [yT_])], n, 1.0 / 128.0, sq, sq_T, rs[:, :n], rs_T, 7)
                    kb.op(DVE, lambda e: e.scalar_tensor_tensor(out=self.mixT[:, 8 + cch, t0:t0 + n],
                                                                in0=yy[:, t0:t0 + n], scalar=ncol, in1=rs[:, :n],
                                                                op0=ALU.mult, op1=ALU.mult),
                          reads=[yT_, rs_T, T_const], writes=[self.mix_T[(8 + cch, t0)]])
                pi = 6
                kb.mm([psum[pi][0:2, 0:128]], [(ue[:, NPR:NPR + 2], ident)], reads=[ueT, T_const],
                      writes=[psum_T[pi]], transpose=True)
                kb.op(ACT, lambda e: e.copy(out=crow[:, cch * 128:(cch + 1) * 128], in_=psum[pi][0:2, 0:128]),
                      reads=[psum_T[pi]], writes=[crow_T])
                kb.mm([psum[pi][0:NS, 0:128]], [(us[b], ident)], reads=[us_T[b], T_const], writes=[psum_T[pi]],
                      transpose=True)
                kb.op(ACT, lambda e: e.copy(out=csrow[:, cch * 128:(cch + 1) * 128], in_=psum[pi][0:NS, 0:128]),
                      reads=[psum_T[pi]], writes=[csrow_T])

            stage_a(0)
            for cch in range(8):
                if cch + 1 < 8:
                    stage_a(cch + 1)
                stage_b(cch)
            kb.dma(SP, convp_o, crow, reads=[crow_T])
            kb.dma(SP, convs_o[:, 1, :], csrow, reads=[csrow_T])
            kb.dma(SP, convs_o[:, 0, :], sconv_d.rearrange("(s j) c -> s j c", j=2)[:, 1, :])

        def mix_out(stack):
            ring = Ring(self, stack, 3)
            for blk in range(8):
                wv, wT = ring.load(w["w_out"], 0, KD, blk * 256, 256)
                for ec in range(2):
                    i = blk * 2 + ec
                    for (t0, n) in token_tiles(NT):
                        pi = pick()
                        kb.mm(psum[pi][:, :n],
                              [(wv[:, k, ec * 128:(ec + 1) * 128], self.mixT[:, k, t0:t0 + n]) for k in range(KD)],
                              reads=[self.mix_T[(k, t0)] for k in range(KD)] + [wT], writes=[psum_T[pi]])
                        kb.op(DVE, lambda e: e.tensor_tensor(out=hT[:, i, t0:t0 + n], in0=hT[:, i, t0:t0 + n],
                                                             in1=psum[pi][:, :n], op=ALU.add),
                              reads=[psum_T[pi], hTk(i, t0)], writes=[hTk(i, t0)])

        def phase(fn, *a):
            self.phase_id = getattr(self, "phase_id", 0) + 1
            with nc.named_scope("p%02d_%s" % (self.phase_id, fn.__name__)):
                with ExitStack() as s:
                    fn(*a, s)
                    kb.barrier()

        def phase2(*fns):
            self.phase_id = getattr(self, "phase_id", 0) + 1
            with nc.named_scope("p%02d_%s" % (self.phase_id, fns[-1][0].__name__)):
                with ExitStack() as s:
                    for f in fns:
                        f[0](*f[1:], s)
                    kb.barrier()

        F1 = (w["ffn1_gate"], w["ffn1_up"], w["ffn1_down"])
        F2 = (w["ffn2_gate"], w["ffn2_up"], w["ffn2_down"])
        phase(load_T, xm, NT)
        if ALL or "ffn1" in st:
            phase2((rmsnorm, NT, 0), (ffn, NT) + F1)
        if ALL or "mix" in st:
            with ExitStack() as sm:
                G = {"Gcol": self.scr(sm, "Gcol", [128, 8, 3, 4], F32), "Gcol_T": Tk(),
                     "sgb": self.scr(sm, "sgb", [128, 192], F32), "sgb_T": Tk(),
                     "semt": self.scr(sm, "semt", [NS, 4], F32), "swt": self.scr(sm, "swt", [NS, 4], F32),
                     "negM_dT": Tk(), "sgs_dT": Tk(),
                     "nTt": self.scr(sm, "nTt", [128, 2, 64], F32), "nTt_T": Tk()}
                phase2((rmsnorm, NT, 16), (prefix_state, G))
                self.mixT = self.scr(sm, "mixT", [128, KD, NT], BF16)
                self.mix_T = {(k, t0): Tk() for k in range(KD) for (t0, n) in token_tiles(NT)}
                phase(mix_heads, G)
                phase(mix_conv)
                phase(mix_out)
        if ALL or "ffn2" in st:
            phase2((rmsnorm, NT, 32), (ffn, NT) + F2)
        self.phase_id += 1
        with nc.named_scope("p%02d_final" % self.phase_id):
            with ExitStack() as s:
                final_out(y_out, NT, s)
        kb.finish()
        return nc


def make_pcol(inp):
    pc = np.zeros((128, 128), np.float32)

    def put(c0, v):
        v = np.asarray(v, np.float32).reshape(-1, 128)
        pc[:, c0:c0 + v.shape[0]] = v.T

    put(0, inp["norm_ffn1"][0])
    put(16, inp["norm_mix"][0])
    put(32, inp["norm_ffn2"][0])
    put(48, inp["norm_final"])
    put(64, inp["conv_w"][0, 0])
    put(72, inp["conv_w"][0, 1])
    put(80, inp["conv_w"][0, 2])
    put(88, inp["conv_b"][0])
    put(96, inp["norm_conv"][0])
    return pc


def make_cst():
    c = np.zeros((128, 512), np.float32)
    c[:, 0:128] = np.eye(128, dtype=np.float32)
    c[:, 256:512] = np.eye(16, dtype=np.float32).reshape(1, 256)
    s_i = np.arange(128)[:, None]
    t_i = np.arange(128)[None, :]
    c[:, 128:256] = np.where(s_i <= t_i, 0.0, -30000.0)
    return c


_CACHE = {}

W_NAMES = ["ffn1_gate", "ffn1_up", "ffn1_down", "w_in", "w_out", "ffn2_gate", "ffn2_up", "ffn2_down"]


def core_inputs(inp, c, shared):
    b, half = c // 2, c % 2
    f32 = np.float32
    xm = np.concatenate([inp["x_prompt"][b, half * NPR:(half + 1) * NPR], inp["x_sample"][c * NS:(c + 1) * NS, 0]], 0)
    m = dict(shared)
    m["xm"] = np.ascontiguousarray(xm, f32)
    m["flag"] = np.full((128, 1), float(half), f32)
    sl = slice(c * NS, (c + 1) * NS)
    m["sC"] = np.ascontiguousarray(inp["state_mlstm_C"][0, sl], f32)
    m["sn"] = np.ascontiguousarray(inp["state_mlstm_n"][0, sl], f32).reshape(NS * H, DK)
    m["sm"] = np.ascontiguousarray(inp["state_mlstm_m"][0, sl], f32)
    m["sconv"] = np.ascontiguousarray(inp["state_conv"][0, sl], f32).reshape(NS * 2, 1024)
    return m


def shared_inputs(inp):
    f32 = np.float32
    sh = {"pcol": make_pcol(inp), "cst": make_cst(),
          "nmb": np.ascontiguousarray(np.broadcast_to(np.asarray(inp["norm_mlstm"][0], f32)[None, :], (128, 1024))),
          "gfb": np.ascontiguousarray(np.broadcast_to(np.asarray(inp["norm_final"], f32)[None, :], (128, D))),
          "bg": np.ascontiguousarray(np.asarray(inp["b_gates"][0], f32).reshape(2, 4).T)}
    for nm in W_NAMES:
        sh[nm] = np.ascontiguousarray(inp[nm][0], f32)
    return sh


def assemble(res):
    f32 = np.float32
    y_p = np.zeros((4, 2048, D), f32)
    y_s = np.zeros((128, 1, D), f32)
    C_p = np.zeros((1, 4, H, DK, DK), f32)
    n_p = np.zeros((1, 4, H, DK), f32)
    m_p = np.zeros((1, 4, H), f32)
    conv_p = np.zeros((1, 4, 2, 1024), f32)
    C_s = np.zeros((1, 128, H, DK, DK), f32)
    n_s = np.zeros((1, 128, H, DK), f32)
    m_s = np.zeros((1, 128, H), f32)
    conv_s = np.zeros((1, 128, 2, 1024), f32)
    for c in range(8):
        r = res[c]
        b, half = c // 2, c % 2
        y_p[b, half * NPR:(half + 1) * NPR] = r["y"][:NPR]
        sl = slice(c * NS, (c + 1) * NS)
        y_s[sl, 0] = r["y"][NPR:NT]
        if half == 1:
            C_p[0, b] = r["Cp"]
            n_p[0, b] = r["np"]
            m_p[0, b] = r["mp"][:, 0]
            conv_p[0, b] = r["convp"]
        C_s[0, sl] = r["Cs"]
        n_s[0, sl] = r["ns"].reshape(NS, H, DK)
        m_s[0, sl] = r["ms"]
        conv_s[0, sl] = r["convs"]
    return (y_p, y_s, C_p, n_p, m_p, conv_p, C_s, n_s, m_s, conv_s)


def kernel(**inputs):
    if "prog" not in _CACHE:
        p = Prog(("all",))
        p.build()
        _CACHE["prog"] = p
    p = _CACHE["prog"]
    inp = {k: np.asarray(v) for k, v in inputs.items()}
    sh = shared_inputs(inp)
    in_maps = [core_inputs(inp, c, sh) for c in range(8)]
    res = run_bass_kernel_spmd(p.nc, in_maps, core_ids=list(range(8)))
    return assemble(res.results)
```

```python
from contextlib import ExitStack

import numpy as np
import concourse.bass as bass
import concourse.mybir as mybir
from concourse.bass_utils import run_bass_kernel_spmd

F32 = mybir.dt.float32
BF16 = mybir.dt.bfloat16
AF = mybir.ActivationFunctionType
ALU = mybir.AluOpType

D = 2048
DFF = 5504
NFF = DFF // 128
KD = D // 128
H = 4
DK = 256
DIN = 7176
NPR = 1024
NS = 16
NT = NPR + NS
EPS = 1e-6
WINDOW = 4
SLOT_ELEMS = 4096
NSLOT = 5


class Tk:
    __slots__ = ("w", "r")

    def __init__(self):
        self.w = None
        self.r = {}


class Eng:
    def __init__(self, nc, name, eng, ndma=0):
        self.name = name
        self.eng = eng
        self.sem = nc.alloc_semaphore("s_" + name)
        self.seq = 0
        self.waited = {}
        self.slots = [[nc.alloc_semaphore("d_%s%d" % (name, i)), 0] for i in range(ndma)]
        self.nxt = 0


class KB:
    def __init__(self, nc):
        self.nc = nc
        self.pe = Eng(nc, "pe", nc.tensor)
        self.act = Eng(nc, "act", nc.scalar, ndma=6)
        self.dve = Eng(nc, "dve", nc.vector)
        self.pool = Eng(nc, "pool", nc.gpsimd, ndma=10)
        self.sp = Eng(nc, "sp", nc.sync, ndma=12)
        self.engs = [self.pe, self.act, self.dve, self.pool, self.sp]

    def _deps(self, E, reads, writes):
        need = {}

        def add(tok):
            sem, val, owner = tok
            if owner is E:
                if E is self.pe or E.seq - val >= WINDOW:
                    return
            k = id(sem)
            cur = need.get(k)
            if cur is None or cur[1] < val:
                need[k] = (sem, val)

        for t in reads:
            if t.w is not None:
                add(t.w)
        for t in writes:
            if t.w is not None:
                add(t.w)
            for tok in t.r.values():
                add(tok)
        for k, (sem, val) in need.items():
            if E.waited.get(k, 0) >= val:
                continue
            E.eng.wait_ge(sem, val)
            E.waited[k] = val

    def _mark(self, tok, reads, writes):
        k = id(tok[0])
        for t in reads:
            t.r[k] = tok
        for t in writes:
            t.w = tok
            t.r = {}

    def op(self, E, fn, reads=(), writes=()):
        self._deps(E, reads, writes)
        inst = fn(E.eng)
        E.seq += 1
        inst.then_inc(E.sem, 1)
        self._mark((E.sem, E.seq, E), reads, writes)

    def mm(self, out_ap, pairs, reads=(), writes=(), transpose=False):
        E = self.pe
        self._deps(E, reads, writes)
        n = len(pairs)
        inst = None
        for i, (l, r) in enumerate(pairs):
            if transpose:
                inst = self.nc.tensor.transpose(out_ap[i], l, r)
            else:
                inst = self.nc.tensor.matmul(out_ap, lhsT=l, rhs=r, start=(i == 0), stop=(i == n - 1))
        E.seq += 1
        inst.then_inc(E.sem, 1)
        self._mark((E.sem, E.seq, E), reads, writes)

    def dma(self, Q, out_ap, in_ap, reads=(), writes=(), **kw):
        self._deps(Q, reads, writes)
        slot = Q.slots[Q.nxt]
        Q.nxt = (Q.nxt + 1) % len(Q.slots)
        sem, cnt = slot
        k = id(sem)
        if cnt > 0 and Q.waited.get(k, 0) < cnt:
            Q.eng.wait_ge(sem, cnt)
            Q.waited[k] = cnt
        inst = Q.eng.dma_start(out=out_ap, in_=in_ap, **kw)
        slot[1] = cnt + 16
        inst.then_inc(sem, 16)
        self._mark((sem, cnt + 16, None), reads, writes)

    def barrier(self):
        for E in self.engs:
            for Fe in self.engs:
                if Fe is not E and Fe.seq > 0 and E.waited.get(id(Fe.sem), 0) < Fe.seq:
                    E.eng.wait_ge(Fe.sem, Fe.seq)
                    E.waited[id(Fe.sem)] = Fe.seq
                for sem, cnt in Fe.slots:
                    if cnt > 0 and E.waited.get(id(sem), 0) < cnt:
                        E.eng.wait_ge(sem, cnt)
                        E.waited[id(sem)] = cnt

    def finish(self):
        for Q in (self.sp, self.pool, self.act):
            for sem, cnt in Q.slots:
                if cnt > 0 and Q.waited.get(id(sem), 0) < cnt:
                    Q.eng.wait_ge(sem, cnt)
                    Q.waited[id(sem)] = cnt


def token_tiles(nt):
    out = []
    t = 0
    while t < nt:
        n = min(512, nt - t)
        out.append((t, n))
        t += n
    return out


class Ring:
    def __init__(self, prog, stack, nslots):
        self.prog = prog
        self.slots = [prog.scr(stack, "wring", [128, SLOT_ELEMS], BF16) for _ in range(nslots)]
        self.T = [Tk() for _ in range(nslots)]
        self.nxt = 0

    def load(self, wap, r0, nrow_chunks, c0, ncols):
        assert nrow_chunks * ncols <= SLOT_ELEMS
        i = self.nxt
        self.nxt = (i + 1) % len(self.slots)
        view = self.slots[i][:, 0:nrow_chunks * ncols].rearrange("p (k c) -> p k c", k=nrow_chunks)
        src = wap[r0 * 128:(r0 + nrow_chunks) * 128, c0:c0 + ncols].rearrange("(k p) c -> p k c", p=128)
        kb = self.prog.kb
        if ncols * 4 < 512:
            with self.prog.nc.allow_non_contiguous_dma(reason="narrow gate columns"):
                kb.dma(kb.pool, view, src, writes=[self.T[i]])
        else:
            kb.dma(kb.pool, view, src, writes=[self.T[i]])
        return view, self.T[i]


class Prog:
    def __init__(self, stages=("all",)):
        self.stages = stages
        nc = bass.Bass("TRN2", target_bir_lowering=False)
        self.nc = nc
        self.kb = KB(nc)
        self.din = {}
        self.dout = {}
        self.uid = 0

    def inp(self, name, shape, dt=F32):
        t = self.nc.dram_tensor(name, list(shape), dt, kind="ExternalInput")
        self.din[name] = t
        return t.ap()

    def outp(self, name, shape, dt=F32):
        t = self.nc.dram_tensor(name, list(shape), dt, kind="ExternalOutput")
        self.dout[name] = t
        return t.ap()

    def dscr(self, name, shape, dt=F32):
        return self.nc.dram_tensor(name, list(shape), dt).ap()

    def sb(self, name, shape, dt):
        return self.nc.alloc_sbuf_tensor(name, list(shape), dt).ap()

    def scr(self, stack, name, shape, dt):
        self.uid += 1
        return stack.enter_context(self.nc.sbuf_tensor("%s_u%d" % (name, self.uid), list(shape), dt)).ap()

    def build(self):
        nc, kb = self.nc, self.kb
        PE, ACT, DVE, POOL, SP = kb.pe, kb.act, kb.dve, kb.pool, kb.sp
        st = self.stages
        ALL = "all" in st

        xm = self.inp("xm", [NT, D])
        pcol_d = self.inp("pcol", [128, 128])
        cst_d = self.inp("cst", [128, 512])
        flag_d = self.inp("flag", [128, 1])
        nmb_d = self.inp("nmb", [128, 1024])
        gfb_d = self.inp("gfb", [128, D])
        bg_d = self.inp("bg", [4, 2])
        sC_d = self.inp("sC", [NS, H, DK, DK])
        sn_d = self.inp("sn", [NS * H, DK])
        sm_d = self.inp("sm", [NS, H])
        sconv_d = self.inp("sconv", [NS * 2, 1024])
        w = {}
        for nm, shp in [("ffn1_gate", [D, DFF]), ("ffn1_up", [D, DFF]), ("ffn1_down", [DFF, D]),
                        ("w_in", [D, DIN]), ("w_out", [D, D]),
                        ("ffn2_gate", [D, DFF]), ("ffn2_up", [D, DFF]), ("ffn2_down", [DFF, D])]:
            w[nm] = self.inp(nm, shp)
        y_out = self.outp("y", [NT, D])
        Cp_o = self.outp("Cp", [H, DK, DK])
        np_o = self.outp("np", [H, DK])
        mp_o = self.outp("mp", [H, 1])
        convp_o = self.outp("convp", [2, 1024])
        Cs_o = self.outp("Cs", [NS, H, DK, DK])
        ns_o = self.outp("ns", [NS * H, DK])
        ms_o = self.outp("ms", [NS, H])
        convs_o = self.outp("convs", [NS, 2, 1024])
        XR, XC = 1024, 264
        xsrc = self.dscr("cc_src", [XR, XC])
        xdst = self.dscr("cc_dst", [2 * XR, XC])
        xsa = self.dscr("cc_src_a", [32, XC])
        xda = self.dscr("cc_dst_a", [64, XC])
        cc_sem = nc.alloc_semaphore("cc_sem")
        cc_sem_a = nc.alloc_semaphore("cc_sem_a")
        T_xdst = Tk()
        T_xda = Tk()
        negM_d = self.dscr("scr_negM", [H, 1025])
        sgs_d = self.dscr("scr_sgs", [H, 48])

        onesb = self.sb("onesb", [128, 128], BF16)
        identb = self.sb("identb", [128, 128], BF16)
        pcol = self.sb("pcol_sb", [128, 128], F32)
        cst = self.sb("cst_sb", [128, 512], F32)
        ident = cst[:, 0:128]
        epsc = self.sb("epsc", [128, 1], F32)
        onec = self.sb("onec", [128, 1], F32)
        flag = self.sb("flag_sb", [128, 1], F32)
        minit = self.sb("minit", [4, 1], F32)
        xnpre = self.sb("xnpre", [128, KD, 2], BF16)
        g2 = self.sb("g2", [128, 2], F32)
        T_g2 = Tk()
        hT = self.sb("hT", [128, KD, NT], F32)
        xnT = self.sb("xnT", [128, KD, NT], BF16)
        psum = [nc.alloc_psum_tensor("ps%d" % i, [128, 512], F32).ap() for i in range(8)]
        psum_T = [Tk() for _ in range(8)]
        T_const = Tk()
        T_minit = Tk()
        T_convinit = Tk()
        T_cinit = Tk()
        hT_T = {}
        xnT_T = {}
        self.pcnt = 0

        def hTk(k, t0):
            return hT_T.setdefault((k, t0), Tk())

        def xnTk(k, t0):
            return xnT_T.setdefault((k, t0), Tk())

        def pick(lo=0, hi=6):
            self.pcnt += 1
            return lo + (self.pcnt % (hi - lo))

        kb.dma(SP, cst, cst_d, writes=[T_const])
        kb.dma(SP, pcol, pcol_d, writes=[T_const])
        kb.dma(SP, flag, flag_d, writes=[T_const])
        kb.op(POOL, lambda e: e.memset(onesb, 1.0), writes=[T_const])
        kb.op(POOL, lambda e: e.tensor_copy(out=identb, in_=cst[:, 0:128]), reads=[T_const], writes=[T_const])
        kb.op(POOL, lambda e: e.memset(epsc, EPS), writes=[T_const])
        kb.op(POOL, lambda e: e.memset(onec, 1.0), writes=[T_const])
        maskneg = cst[:, 128:256]

        def tts_of(t0):
            return (t0 // 512) * 512

        def load_T(x_dram, nt, stack):
            xrow = [self.scr(stack, "xrow%d" % i, [128, D], F32) for i in range(2)]
            xrow_T = [Tk(), Tk()]
            nrt = (nt + 127) // 128
            cnt = 0
            for r in range(nrt):
                rows = min(128, nt - r * 128)
                b = r % 2
                kb.dma(SP, xrow[b][:rows, :], x_dram[r * 128:r * 128 + rows, :], writes=[xrow_T[b]])
                tt0 = tts_of(r * 128)
                for kg in range(4):
                    pi = 6 + (cnt % 2)
                    outs = [psum[pi][:, i * 128:i * 128 + rows] for i in range(4)]
                    pairs = [(xrow[b][:rows, (kg * 4 + i) * 128:(kg * 4 + i + 1) * 128], ident[:rows, :rows])
                             for i in range(4)]
                    kb.mm(outs, pairs, reads=[xrow_T[b], T_const], writes=[psum_T[pi]], transpose=True)
                    src = psum[pi].rearrange("p (a b) -> p a b", a=4)[:, :, 0:rows]
                    dst = hT[:, kg * 4:(kg + 1) * 4, r * 128:r * 128 + rows]
                    wr = [hTk(kg * 4 + i, tt0) for i in range(4)]
                    if cnt % 2 == 0:
                        kb.op(ACT, lambda e: e.copy(out=dst, in_=src), reads=[psum_T[pi]], writes=wr)
                    else:
                        kb.op(DVE, lambda e: e.tensor_copy(out=dst, in_=src), reads=[psum_T[pi]], writes=wr)
                    cnt += 1

        def rstd_from(srcs, n, inv_count, sq, sq_T, rs_ap, rs_T, pi):
            ns_ = len(srcs)
            for k, (ap, tks) in enumerate(srcs):
                b = k % 2
                kb.op(ACT, lambda e: e.activation(out=sq[b][:, :n], in_=ap, func=AF.Square),
                      reads=tks, writes=[sq_T[b]])
                E = kb.pe
                kb._deps(E, [sq_T[b], T_const], [psum_T[pi]] if k == 0 else [])
                inst = nc.tensor.matmul(psum[pi][:, :n], lhsT=onesb, rhs=sq[b][:, :n], start=(k == 0),
                                        stop=(k == ns_ - 1))
                E.seq += 1
                inst.then_inc(E.sem, 1)
                tok = (E.sem, E.seq, E)
                kb._mark(tok, [sq_T[b]], [psum_T[pi]] if k == ns_ - 1 else [])
                if k != ns_ - 1:
                    psum_T[pi].w = tok
            kb.op(ACT, lambda e: e.activation(out=rs_ap, in_=psum[pi][:, :n], func=AF.Ln,
                                              scale=inv_count, bias=epsc),
                  reads=[psum_T[pi], T_const], writes=[rs_T])
            kb.op(ACT, lambda e: e.activation(out=rs_ap, in_=rs_ap, func=AF.Exp, scale=-0.5),
                  reads=[rs_T], writes=[rs_T])

        def rmsnorm(nt, gcol0, stack, out_bf16=True):
            sq = [self.scr(stack, "nsq%d" % i, [128, 512], BF16) for i in range(2)]
            sq_T = [Tk(), Tk()]
            rs = self.scr(stack, "nrs", [128, 512], F32)
            rs_T = Tk()
            for (t0, n) in token_tiles(nt):
                rstd_from([(hT[:, k, t0:t0 + n], [hTk(k, t0)]) for k in range(KD)], n, 1.0 / D,
                          sq, sq_T, rs[:, :n], rs_T, 6)
                for k in range(KD):
                    if out_bf16:
                        o_ap, o_T = xnT[:, k, t0:t0 + n], [xnTk(k, t0)]
                    else:
                        o_ap, o_T = hT[:, k, t0:t0 + n], [hTk(k, t0)]
                    kb.op(DVE, lambda e: e.scalar_tensor_tensor(out=o_ap, in0=hT[:, k, t0:t0 + n],
                                                                scalar=pcol[:, gcol0 + k:gcol0 + k + 1],
                                                                in1=rs[:, :n], op0=ALU.mult, op1=ALU.mult),
                          reads=[hTk(k, t0), rs_T, T_const], writes=o_T)

        def ffn(nt, wg, wu, wd, stack):
            tts = token_tiles(nt)
            NH = 22
            ring = Ring(self, stack, 5)
            actT = self.scr(stack, "actT", [128, NH, nt], BF16)
            act_T = {}
            sg = [self.scr(stack, "sg%d" % i, [128, 512], F32) for i in range(3)]
            sg_T = [Tk(), Tk(), Tk()]
            PG, PU, PD = [0, 1, 4], [2, 3, 5], [4, 5, 0, 1]
            cnt = 0
            for (j0, j1) in [(0, NH), (NH, NFF)]:
                j = j0
                while j < j1:
                    nj = min(2, j1 - j)
                    gv, gT = ring.load(wg, 0, KD, j * 128, nj * 128)
                    uv, uT = ring.load(wu, 0, KD, j * 128, nj * 128)
                    for jj in range(nj):
                        jl = j + jj - j0
                        for (t0, n) in tts:
                            b = cnt % 3
                            cnt += 1
                            pg, pu = PG[b], PU[b]
                            rd = [xnTk(k, t0) for k in range(KD)]
                            kb.mm(psum[pg][:, :n],
                                  [(gv[:, k, jj * 128:(jj + 1) * 128], xnT[:, k, t0:t0 + n]) for k in range(KD)],
                                  reads=rd + [gT], writes=[psum_T[pg]])
                            kb.mm(psum[pu][:, :n],
                                  [(uv[:, k, jj * 128:(jj + 1) * 128], xnT[:, k, t0:t0 + n]) for k in range(KD)],
                                  reads=rd + [uT], writes=[psum_T[pu]])
                            kb.op(ACT, lambda e: e.activation(out=sg[b][:, :n], in_=psum[pg][:, :n], func=AF.Silu),
                                  reads=[psum_T[pg]], writes=[sg_T[b]])
                            aT = act_T.setdefault((jl, t0), Tk())
                            kb.op(DVE, lambda e: e.tensor_tensor(out=actT[:, jl, t0:t0 + n], in0=sg[b][:, :n],
                                                                 in1=psum[pu][:, :n], op=ALU.mult),
                                  reads=[sg_T[b], psum_T[pu]], writes=[aT])
                    j += nj
                njh = j1 - j0
                for i in range(KD):
                    dv, dT = ring.load(wd, j0, njh, i * 128, 128)
                    for (t0, n) in tts:
                        pd = PD[cnt % 4]
                        cnt += 1
                        kb.mm(psum[pd][:, :n],
                              [(dv[:, jl, :], actT[:, jl, t0:t0 + n]) for jl in range(njh)],
                              reads=[act_T[(jl, t0)] for jl in range(njh)] + [dT], writes=[psum_T[pd]])
                        kb.op(DVE, lambda e: e.scalar_tensor_tensor(out=hT[:, i, t0:t0 + n], in0=psum[pd][:, :n],
                                                                    scalar=0.5, in1=hT[:, i, t0:t0 + n],
                                                                    op0=ALU.mult, op1=ALU.add),
                              reads=[psum_T[pd], hTk(i, t0)], writes=[hTk(i, t0)])

        def final_out(y_dram, nt, stack):
            sq = [self.scr(stack, "fsq%d" % i, [128, 512], BF16) for i in range(2)]
            sq_T = [Tk(), Tk()]
            rs = self.scr(stack, "frs", [128, 512], F32)
            rs_T = Tk()
            gfb = self.scr(stack, "gfb_sb", [128, D], F32)
            gfb_T = Tk()
            rcol = self.scr(stack, "rcol", [128, 16], F32)
            rcol_T = Tk()
            yrow = [self.scr(stack, "yrow%d" % i, [128, D], F32) for i in range(2)]
            yrow_T = [Tk(), Tk()]
            kb.dma(SP, gfb, gfb_d, writes=[gfb_T])
            cnt = 0
            for (t0, n) in token_tiles(nt):
                rstd_from([(hT[:, k, t0:t0 + n], [hTk(k, t0)]) for k in range(KD)], n, 1.0 / D,
                          sq, sq_T, rs[:, :n], rs_T, 6)
                rts = [(r, min(128, nt - r * 128)) for r in range((nt + 127) // 128)
                       if t0 <= r * 128 < t0 + n]
                for (r, rows) in rts:
                    off = r * 128 - t0
                    kb.mm([psum[7][:rows, 0:1]], [(rs[0:1, off:off + rows], ident[0:1, 0:1])],
                          reads=[rs_T, T_const], writes=[psum_T[7]], transpose=True)
                    kb.op(ACT, lambda e: e.copy(out=rcol[:rows, r:r + 1], in_=psum[7][:rows, 0:1]),
                          reads=[psum_T[7]], writes=[rcol_T])
                for (r, rows) in rts:
                    b = r % 2
                    for kg in range(4):
                        pi = 6 + (cnt % 2)
                        cnt += 1
                        outs = [psum[pi][:rows, i * 128:(i + 1) * 128] for i in range(4)]
                        pairs = [(hT[:, kg * 4 + i, r * 128:r * 128 + rows], ident) for i in range(4)]
                        kb.mm(outs, pairs, reads=[hTk(kg * 4 + i, t0) for i in range(4)] + [T_const],
                              writes=[psum_T[pi]], transpose=True)
                        kb.op(DVE, lambda e: e.scalar_tensor_tensor(out=yrow[b][:rows, kg * 512:(kg + 1) * 512],
                                                                    in0=psum[pi][:rows, :],
                                                                    scalar=rcol[:rows, r:r + 1],
                                                                    in1=gfb[:rows, kg * 512:(kg + 1) * 512],
                                                                    op0=ALU.mult, op1=ALU.mult),
                              reads=[psum_T[pi], rcol_T, gfb_T], writes=[yrow_T[b]])
                    kb.dma(SP, y_dram[r * 128:r * 128 + rows, :], yrow[b][:rows, :], reads=[yrow_T[b]])

        def store_T(y_dram, nt, stack):
            yrow = [self.scr(stack, "yrow%d" % i, [128, D], F32) for i in range(2)]
            yrow_T = [Tk(), Tk()]
            nrt = (nt + 127) // 128
            cnt = 0
            for r in range(nrt):
                rows = min(128, nt - r * 128)
                b = r % 2
                tt0 = tts_of(r * 128)
                for kg in range(4):
                    pi = 6 + (cnt % 2)
                    outs = [psum[pi][:rows, i * 128:(i + 1) * 128] for i in range(4)]
                    pairs = [(hT[:, kg * 4 + i, r * 128:r * 128 + rows], ident) for i in range(4)]
                    kb.mm(outs, pairs, reads=[hTk(kg * 4 + i, tt0) for i in range(4)] + [T_const],
                          writes=[psum_T[pi]], transpose=True)
                    dst = yrow[b][:rows, kg * 512:(kg + 1) * 512]
                    src = psum[pi][:rows, :]
                    if cnt % 2 == 0:
                        kb.op(ACT, lambda e: e.copy(out=dst, in_=src), reads=[psum_T[pi]], writes=[yrow_T[b]])
                    else:
                        kb.op(DVE, lambda e: e.tensor_copy(out=dst, in_=src), reads=[psum_T[pi]], writes=[yrow_T[b]])
                    cnt += 1
                kb.dma(SP, y_dram[r * 128:r * 128 + rows, :], yrow[b][:rows, :], reads=[yrow_T[b]])

        def dense_fm(view, vT, ncols, nt, evac, m_rows=128):
            for ec in range((ncols + m_rows - 1) // m_rows):
                m = min(m_rows, ncols - ec * m_rows)
                for (t0, n) in token_tiles(nt):
                    pi = pick()
                    kb.mm(psum[pi][:m, :n],
                          [(view[:, k, ec * m_rows:ec * m_rows + m], xnT[:, k, t0:t0 + n]) for k in range(KD)],
                          reads=[xnTk(k, t0) for k in range(KD)] + [vT], writes=[psum_T[pi]])
                    evac(ec, t0, n, psum[pi][:m, :n], psum_T[pi])

        def dense_tm(view, vT, ncols, nt, evac):
            for r in range((nt + 127) // 128):
                rows = min(128, nt - r * 128)
                pi = pick()
                kb.mm(psum[pi][:rows, :ncols],
                      [(xnT[:, k, r * 128:r * 128 + rows], view[:, k, 0:ncols]) for k in range(KD)],
                      reads=[xnTk(k, tts_of(r * 128)) for k in range(KD)] + [vT], writes=[psum_T[pi]])
                evac(r, rows, psum[pi][:rows, :ncols], psum_T[pi])

        def gate_rows(nt, nscan, m0, m0_reads, stack, ring):
            R = {}

            def row(name, p=4, n=nt):
                R[name] = self.scr(stack, "r_" + name, [p, n], F32)
                R[name + "_T"] = Tk()
                return R[name], R[name + "_T"]

            bg = self.scr(stack, "bg_sb", [4, 2], F32)
            bg_T = Tk()
            kb.dma(SP, bg, bg_d, writes=[bg_T])
            kb.op(DVE, lambda e: e.tensor_scalar(out=bg, in0=bg, scalar1=1.0 / 15.0, scalar2=None, op0=ALU.mult),
                  reads=[bg_T], writes=[bg_T])
            gv, gT = ring.load(w["w_in"], 0, KD, 4096, 8)
            graw, graw_T = row("graw", 8)
            fgraw, fgraw_T = row("fgraw")

            def ev(ec, t0, n, ps, psT):
                kb.op(ACT, lambda e: e.copy(out=graw[:, t0:t0 + n], in_=ps), reads=[psT], writes=[graw_T])

            dense_fm(gv, gT, 8, nt, ev, m_rows=8)
            kb.dma(SP, fgraw, graw[4:8, :], reads=[graw_T], writes=[fgraw_T])
            yield
            logi, logi_T = row("logi")
            logf, logf_T = row("logf")
            kb.op(ACT, lambda e: e.activation(out=logi, in_=graw[0:4, :], func=AF.Tanh, scale=1.0 / 15.0,
                                              bias=bg[:, 0:1]), reads=[graw_T, bg_T], writes=[logi_T])
            kb.op(ACT, lambda e: e.activation(out=logf, in_=fgraw, func=AF.Tanh, scale=1.0 / 15.0,
                                              bias=bg[:, 1:2]), reads=[fgraw_T, bg_T], writes=[logf_T])
            kb.op(ACT, lambda e: e.activation(out=logf, in_=logf, func=AF.Exp, scale=-15.0),
                  reads=[logf_T], writes=[logf_T])
            kb.op(ACT, lambda e: e.activation(out=logf, in_=logf, func=AF.Ln, scale=1.0, bias=onec[0:4, :]),
                  reads=[logf_T, T_const], writes=[logf_T])
            kb.op(DVE, lambda e: e.tensor_scalar(out=logf, in0=logf, scalar1=-1.0, scalar2=None, op0=ALU.mult),
                  reads=[logf_T], writes=[logf_T])
            kb.op(DVE, lambda e: e.tensor_scalar(out=logi, in0=logi, scalar1=15.0, scalar2=None, op0=ALU.mult),
                  reads=[logi_T], writes=[logi_T])
            zeros, zeros_T = row("zeros", 4, nscan)
            kb.op(POOL, lambda e: e.memset(zeros, 0.0), writes=[zeros_T])
            Bc, Bc_T = row("B", 4, nscan)
            mr, mr_T = row("m", 4, nscan)
            kb.op(DVE, lambda e: e.tensor_tensor_scan(out=Bc, data0=logf[:, 0:nscan], data1=zeros, initial=0.0,
                                                      op0=ALU.add, op1=ALU.add),
                  reads=[logf_T, zeros_T], writes=[Bc_T])
            kb.op(DVE, lambda e: e.tensor_tensor_scan(out=mr, data0=logf[:, 0:nscan], data1=logi[:, 0:nscan],
                                                      initial=m0, op0=ALU.add, op1=ALU.max),
                  reads=[logf_T, logi_T] + m0_reads, writes=[mr_T])
            return R

        def run_gen(g):
            try:
                while True:
                    next(g)
            except StopIteration as stop:
                return stop.value

        def transpose_rows(row_ap, rowT, nrow, nchunks, dst_ap, dst_T, pi):
            outs = [psum[pi][:, c * nrow:(c + 1) * nrow] for c in range(nchunks)]
            pairs = [(row_ap[0:nrow, c * 128:(c + 1) * 128], ident[:nrow, :nrow]) for c in range(nchunks)]
            kb.mm(outs, pairs, reads=[rowT, T_const], writes=[psum_T[pi]], transpose=True)
            src = psum[pi][:, 0:nchunks * nrow].rearrange("p (c h) -> p c h", c=nchunks)
            kb.op(DVE, lambda e: e.tensor_copy(out=dst_ap, in_=src), reads=[psum_T[pi]], writes=[dst_T])

        def prefix_state(G, stack):
            ring = Ring(self, stack, 4)
            ktok = self.scr(stack, "pktok", [128, 8, 256], BF16)
            ktok_T = Tk()
            vext = self.scr(stack, "pvext", [128, 8, 258], BF16)
            vext_T = Tk()
            kb.op(POOL, lambda e: e.memset(vext[:, :, 256:258], 1.0), writes=[vext_T])

            def kv_proj(h):
                kv, kT_ = ring.load(w["w_in"], 0, KD, 1024 + h * 256, 256)

                def evk(r, rows, ps, psT):
                    kb.op(ACT, lambda e: e.activation(out=ktok[:rows, r, :], in_=ps, func=AF.Copy, scale=0.0625),
                          reads=[psT], writes=[ktok_T])

                dense_tm(kv, kT_, 256, NPR, evk)
                vv, vT_ = ring.load(w["w_in"], 0, KD, 2048 + h * 256, 256)

                def evv(r, rows, ps, psT):
                    kb.op(ACT, lambda e: e.copy(out=vext[:rows, r, 0:256], in_=ps), reads=[psT], writes=[vext_T])

                dense_tm(vv, vT_, 256, NPR, evv)

            gg = gate_rows(NT, NPR, 0.0, [], stack, ring)
            next(gg)
            kv_proj(0)
            R = run_gen(gg)
            logi, logf, Bc, mr = R["logi"], R["logf"], R["B"], R["m"]
            a = self.scr(stack, "pa", [4, NPR], F32)
            a_T = Tk()
            kb.op(DVE, lambda e: e.tensor_tensor(out=a, in0=logi[:, 0:NPR], in1=Bc, op=ALU.subtract),
                  reads=[R["logi_T"], R["B_T"]], writes=[a_T])
            nME = self.scr(stack, "pnME", [4, 1], F32)
            nME_T = Tk()
            kb.op(DVE, lambda e: e.tensor_tensor(out=nME, in0=Bc[:, NPR - 1:NPR], in1=mr[:, NPR - 1:NPR],
                                                 op=ALU.subtract),
                  reads=[R["B_T"], R["m_T"]], writes=[nME_T])
            kb.op(ACT, lambda e: e.activation(out=a, in_=a, func=AF.Exp, scale=1.0, bias=nME),
                  reads=[a_T, nME_T], writes=[a_T])
            wcol = self.scr(stack, "pwcol", [128, 8, 4], F32)
            wcol_T = Tk()
            transpose_rows(a, a_T, 4, 8, wcol, wcol_T, 7)
            T_src = []

            def src_tk():
                T_src.append(Tk())
                return T_src[-1]

            T_srca = []

            def srca_tk():
                T_srca.append(Tk())
                return T_srca[-1]

            with nc.allow_non_contiguous_dma(reason="tiny m row"):
                kb.dma(SP, xsa[16:17, 0:4].rearrange("o h -> h o"), mr[:, NPR - 1:NPR], reads=[R["m_T"]],
                       writes=[srca_tk()])
            xnp32 = self.scr(stack, "xnp32", [128, KD, 2], F32)
            x32_T = Tk()
            kb.op(DVE, lambda e: e.tensor_copy(out=xnp32, in_=xnT[:, :, NPR - 2:NPR]),
                  reads=[xnTk(k, 512) for k in range(KD)], writes=[x32_T])
            with nc.allow_non_contiguous_dma(reason="2-token boundary block"):
                kb.dma(SP, xsa[0:16, 0:256].rearrange("k (p j) -> p k j", j=2), xnp32, reads=[x32_T],
                       writes=[srca_tk()])
            wk = self.scr(stack, "pwk", [128, 8, 256], BF16)
            wk_T = Tk()
            cst_t = self.scr(stack, "pcst", [128, 2, 257], F32)
            cst_T = Tk()
            for h in range(H):
                if h > 0:
                    kv_proj(h)
                for c in range(8):
                    kb.op(DVE, lambda e: e.tensor_scalar(out=wk[:, c, :], in0=ktok[:, c, :],
                                                         scalar1=wcol[:, c, h:h + 1], scalar2=None, op0=ALU.mult),
                          reads=[ktok_T, wcol_T], writes=[wk_T])
                for dc in range(2):
                    pi = pick()
                    kb.mm(psum[pi][:, 0:257],
                          [(wk[:, c, dc * 128:(dc + 1) * 128], vext[:, c, 0:257]) for c in range(8)],
                          reads=[wk_T, vext_T], writes=[psum_T[pi]])
                    kb.op(ACT, lambda e: e.copy(out=cst_t[:, dc, :], in_=psum[pi][:, 0:257]),
                          reads=[psum_T[pi]], writes=[cst_T])
                kb.dma(SP, xsrc[h * 256:(h + 1) * 256, 0:257].rearrange("(dc p) e -> p dc e", p=128), cst_t,
                       reads=[cst_T], writes=[src_tk()])
                if h == 0:
                    kb._deps(POOL, T_srca, [T_xda])
                    inst = nc.gpsimd.collective_compute("AllGather", ALU.bypass,
                                                        replica_groups=[[0, 1], [2, 3], [4, 5], [6, 7]],
                                                        ins=[xsa], outs=[xda])
                    inst.then_inc(cc_sem_a)
                    kb._mark((cc_sem_a, 1, None), T_srca, [T_xda])
                    gates_pre(G, stack, R)
                    snrow0 = self.scr(stack, "snrow0", [64, 256], F32)
                    snrow0_T = Tk()
                    kb.dma(SP, snrow0, sn_d, writes=[snrow0_T])
                    for dc in range(2):
                        pi = 6 + dc
                        kb.mm([psum[pi][:, 0:64]], [(snrow0[:, dc * 128:(dc + 1) * 128], ident[:64, :64])],
                              reads=[snrow0_T, T_const], writes=[psum_T[pi]], transpose=True)
                        kb.op(DVE, lambda e: e.tensor_copy(out=G["nTt"][:, dc, :], in_=psum[pi][:, 0:64]),
                              reads=[psum_T[pi]], writes=[G["nTt_T"]])

                if h == 2:
                    mraw = self.scr(stack, "mraw", [4, 1], F32)
                    mraw_T = Tk()
                    with nc.allow_non_contiguous_dma(reason="tiny m row"):
                        kb.dma(SP, mraw, xda[16:17, 0:4].rearrange("o h -> h o"), reads=[T_xda], writes=[mraw_T])
                    kb.op(DVE, lambda e: e.tensor_scalar(out=minit, in0=mraw, scalar1=flag[0:4, :], scalar2=None,
                                                         op0=ALU.mult),
                          reads=[mraw_T, T_const], writes=[T_minit])
                    with nc.allow_non_contiguous_dma(reason="2-token boundary block"):
                        kb.dma(SP, xnp32, xda[0:16, 0:256].rearrange("k (p j) -> p k j", j=2), reads=[T_xda],
                               writes=[x32_T])
                    kb.op(DVE, lambda e: e.tensor_copy(out=xnpre, in_=xnp32), reads=[x32_T], writes=[T_convinit])
                    mr2 = self.scr(stack, "mr2", [4, NPR], F32)
                    mr2_T = Tk()
                    kb.op(DVE, lambda e: e.tensor_tensor_scan(out=mr2, data0=logf[:, 0:NPR], data1=logi[:, 0:NPR],
                                                              initial=minit, op0=ALU.add, op1=ALU.max),
                          reads=[R["logf_T"], R["logi_T"], T_minit], writes=[mr2_T])
                    gates_post(G, stack, R, mr2, mr2_T)

            kb._deps(POOL, T_src, [T_xdst])
            inst = nc.gpsimd.collective_compute("AllGather", ALU.bypass,
                                                replica_groups=[[0, 1], [2, 3], [4, 5], [6, 7]],
                                                ins=[xsrc], outs=[xdst])
            inst.then_inc(cc_sem)
            kb._mark((cc_sem, 1, None), T_src, [T_xdst])

        def gates_post(G, stack, R, mr, mr_T):
            logi, logf, Bc = R["logi"], R["logf"], R["B"]
            a = self.scr(stack, "ma", [4, NPR], F32)
            a_T = Tk()
            negMx = self.scr(stack, "mnegMx", [4, 1025], F32)
            negMx_T = Tk()
            negm = self.scr(stack, "mnegm", [4, NPR], F32)
            negm_T = Tk()
            kb.op(DVE, lambda e: e.tensor_tensor(out=a, in0=logi[:, 0:NPR], in1=Bc, op=ALU.subtract),
                  reads=[R["logi_T"], R["B_T"]], writes=[a_T])
            kb.op(DVE, lambda e: e.tensor_tensor(out=negMx[:, 1:1025], in0=Bc, in1=mr, op=ALU.subtract),
                  reads=[R["B_T"], mr_T], writes=[negMx_T])
            kb.op(DVE, lambda e: e.tensor_scalar(out=negMx[:, 0:1], in0=minit, scalar1=-1.0, scalar2=None,
                                                 op0=ALU.mult),
                  reads=[T_minit], writes=[negMx_T])
            kb.op(DVE, lambda e: e.tensor_scalar(out=negm, in0=mr, scalar1=-1.0, scalar2=None, op0=ALU.mult),
                  reads=[mr_T], writes=[negm_T])
            kb.dma(SP, negM_d, negMx, reads=[negMx_T], writes=[G["negM_dT"]])
            kb.dma(SP, mp_o, mr[:, NPR - 1:NPR], reads=[mr_T])
            Gcol = G["Gcol"]
            transpose_rows(a, a_T, 4, 8, Gcol[:, :, 0, :], G["Gcol_T"], 7)
            transpose_rows(negMx[:, 1:1025], negMx_T, 4, 8, Gcol[:, :, 1, :], G["Gcol_T"], 6)
            transpose_rows(negm, negm_T, 4, 8, Gcol[:, :, 2, :], G["Gcol_T"], 7)
            kb.op(ACT, lambda e: e.activation(out=Gcol[:, :, 2, :], in_=Gcol[:, :, 2, :], func=AF.Exp),
                  reads=[G["Gcol_T"]], writes=[G["Gcol_T"]])

        def gates_pre(G, stack, R):
            logi, logf = R["logi"], R["logf"]
            mold = self.scr(stack, "smold", [4, NS], F32)
            mold_T = Tk()
            with nc.allow_non_contiguous_dma(reason="tiny state transpose"):
                kb.dma(SP, mold, sm_d.rearrange("s h -> h s"), writes=[mold_T])
            srow = self.scr(stack, "srow", [4, 48], F32)
            srow_T = Tk()
            mnew = self.scr(stack, "smnew", [4, NS], F32)
            mnew_T = Tk()
            lfs, lis = logf[:, NPR:NT], logi[:, NPR:NT]
            kb.op(DVE, lambda e: e.tensor_tensor(out=mold, in0=mold, in1=lfs, op=ALU.add),
                  reads=[mold_T, R["logf_T"]], writes=[mold_T])
            kb.op(DVE, lambda e: e.tensor_tensor(out=mnew, in0=mold, in1=lis, op=ALU.max),
                  reads=[mold_T, R["logi_T"]], writes=[mnew_T])
            kb.op(DVE, lambda e: e.tensor_tensor(out=srow[:, 0:16], in0=mold, in1=mnew, op=ALU.subtract),
                  reads=[mold_T, mnew_T], writes=[srow_T])
            kb.op(DVE, lambda e: e.tensor_tensor(out=srow[:, 16:32], in0=lis, in1=mnew, op=ALU.subtract),
                  reads=[R["logi_T"], mnew_T], writes=[srow_T])
            kb.op(DVE, lambda e: e.tensor_scalar(out=srow[:, 32:48], in0=mnew, scalar1=-1.0, scalar2=None,
                                                 op0=ALU.mult),
                  reads=[mnew_T], writes=[srow_T])
            kb.op(ACT, lambda e: e.activation(out=srow, in_=srow, func=AF.Exp), reads=[srow_T], writes=[srow_T])
            kb.dma(SP, sgs_d, srow, reads=[srow_T], writes=[G["sgs_dT"]])
            with nc.allow_non_contiguous_dma(reason="tiny state transpose"):
                kb.dma(SP, ms_o.rearrange("s h -> h s"), mnew, reads=[mnew_T])
            kb.dma(SP, G["sgb"], sgs_d.rearrange("h c -> (h c)").partition_broadcast(128),
                   reads=[G["sgs_dT"]], writes=[G["sgb_T"]])
            with nc.allow_non_contiguous_dma(reason="tiny state transpose"):
                kb.dma(SP, G["semt"], sgs_d[:, 32:48].rearrange("h s -> s h"), reads=[G["sgs_dT"]],
                       writes=[G["sgb_T"]])
                kb.dma(SP, G["swt"], sgs_d[:, 16:32].rearrange("h s -> s h"), reads=[G["sgs_dT"]],
                       writes=[G["sgb_T"]])


        def mix_gates(G, stack):
            ring = Ring(self, stack, 2)
            R = run_gen(gate_rows(NT, NPR, minit, [T_minit], stack, ring))
            gates_post(G, stack, R, R["m"], R["m_T"])
            gates_pre(G, stack, R)

        def finish_h(P_, NC_, num, num_T, emt, emt_T, nmso_fn, nmso_T, h, col_fn, tiles, gen=None, npre=0, nper=0):
            sc, sc_T, junk, junk_T, ha, ha_T = tiles
            S = lambda q: sc[:P_, q, 0:NC_]
            kb.op(ACT, lambda e: e.activation(out=S(0), in_=num[:P_, :, 256], func=AF.Abs),
                  reads=[num_T], writes=[sc_T])
            kb.op(DVE, lambda e: e.tensor_tensor(out=S(0), in0=S(0), in1=emt, op=ALU.max),
                  reads=[sc_T, emt_T], writes=[sc_T])
            kb.op(DVE, lambda e: e.reciprocal(out=S(1), in_=S(0)), reads=[sc_T], writes=[sc_T])
            for c in range(NC_):
                kb.op(ACT, lambda e: e.activation(out=junk[:P_, :], in_=num[:P_, c, 0:256], func=AF.Square,
                                                  accum_out=sc[:P_, 2, c:c + 1]),
                      reads=[num_T, sc_T], writes=[junk_T, sc_T])
            kb.op(DVE, lambda e: e.tensor_tensor(out=S(3), in0=S(2), in1=S(1), op=ALU.mult),
                  reads=[sc_T], writes=[sc_T])
            kb.op(DVE, lambda e: e.tensor_tensor(out=S(3), in0=S(3), in1=S(1), op=ALU.mult),
                  reads=[sc_T], writes=[sc_T])
            kb.op(ACT, lambda e: e.activation(out=S(4), in_=S(3), func=AF.Ln, scale=1.0 / 256.0,
                                              bias=epsc[:P_, :]), reads=[sc_T, T_const], writes=[sc_T])
            kb.op(ACT, lambda e: e.activation(out=S(4), in_=S(4), func=AF.Exp, scale=-0.5),
                  reads=[sc_T], writes=[sc_T])
            kb.op(DVE, lambda e: e.tensor_tensor(out=S(5), in0=S(4), in1=S(1), op=ALU.mult),
                  reads=[sc_T], writes=[sc_T])
            if gen is not None:
                for _ in range(npre):
                    next(gen, None)
            for c in range(NC_):
                b = c % 2
                col0 = col_fn(c)
                if gen is not None:
                    for _ in range(nper):
                        next(gen, None)
                kb.op(DVE, lambda e: e.scalar_tensor_tensor(out=ha[b][:P_, :], in0=num[:P_, c, 0:256],
                                                            scalar=sc[:P_, 5, c:c + 1], in1=nmso_fn(c),
                                                            op0=ALU.mult, op1=ALU.mult),
                      reads=[num_T, sc_T, nmso_T], writes=[ha_T[b]])
                pi = 6 + b
                outs = [psum[pi][:, ec * 128:ec * 128 + P_] for ec in range(2)]
                pairs = [(ha[b][:P_, ec * 128:(ec + 1) * 128], ident[:P_, :P_]) for ec in range(2)]
                kb.mm(outs, pairs, reads=[ha_T[b], T_const], writes=[psum_T[pi]], transpose=True)
                src = psum[pi][:, 0:256].rearrange("p (a b) -> p a b", a=2)[:, :, 0:P_]
                dst = self.mixT[:, 2 * h:2 * h + 2, col0:col0 + P_]
                kb.op(ACT, lambda e: e.copy(out=dst, in_=src), reads=[psum_T[pi]],
                      writes=[self.mix_T[(2 * h, tts_of(col0))], self.mix_T[(2 * h + 1, tts_of(col0))]])

        def mix_heads(G, stack):
            ring = Ring(self, stack, 2)
            Gcol, Gcol_T = G["Gcol"], G["Gcol_T"]
            qT = self.scr(stack, "qT", [128, 2, NT], BF16)
            kT = self.scr(stack, "kT", [128, 2, NT], BF16)
            ktok = self.scr(stack, "ktok", [128, 9, 256], BF16)
            vext = self.scr(stack, "vext", [128, 9, 258], BF16)
            nmso = self.scr(stack, "nmso", [128, 9, 256], BF16)
            nmb = self.scr(stack, "nmb", [128, 256], F32)
            negMb = self.scr(stack, "negMb", [128, 1025], F32)
            Mend = self.scr(stack, "Mend", [128, 9], F32)
            Cst = self.scr(stack, "Cst", [128, 2, 257], F32)
            Cbs = [self.scr(stack, "Cb%d" % i, [128, 2, 258], BF16) for i in range(2)]
            Cb_Ts = [Tk(), Tk()]
            arg = [self.scr(stack, "arg%d" % i, [128, 128], F32) for i in range(2)]
            Dm = arg
            SpT = [self.scr(stack, "SpT%d" % i, [128, 128], BF16) for i in range(2)]
            tmp1f = self.scr(stack, "tmp1", [128, 1, 258], F32)
            tmp1 = tmp1f[:, 0, 0:257]
            num = self.scr(stack, "num", [128, 8, 257], F32)
            sc = self.scr(stack, "sc", [128, 6, 8], F32)
            junk = self.scr(stack, "junk", [128, 256], BF16)
            ha0 = self.scr(stack, "ha0", [128, 256], F32)
            ha = [ha0, ha0]
            wkc = [self.scr(stack, "wkc%d" % i, [128, 256], BF16) for i in range(2)]
            cs = self.scr(stack, "cs", [128, 3, 8], F32)
            q_T, k_T, ktok_T, vext_T, nmso_T, nmb_T, negMb_T, Mend_T = [Tk() for _ in range(8)]
            Cst_T, Cb_T, tmp1_T, num_T, sc_T, junk_T, cs_T = [Tk() for _ in range(7)]
            arg_T, SpT_T, wkc_T = [[Tk(), Tk()] for _ in range(3)]
            ha_T0 = Tk()
            ha_T = [ha_T0, ha_T0]
            Dm_T = arg_T
            fin_tiles = (sc, sc_T, junk, junk_T, ha, ha_T)
            NB = 3
            snrow = ha[0][0:64, :]
            nTt, nTt_T = G["nTt"], G["nTt_T"]
            nTn = self.scr(stack, "nTn", [128, 2, 64], F32)
            Cs_ = [self.scr(stack, "Cs%d" % i, [128, 2, 257], F32) for i in range(NB)]
            Cnb = [self.scr(stack, "Cnb%d" % i, [128, 2, 258], BF16) for i in range(2)]
            vsel = [self.scr(stack, "vsel%d" % i, [NS, 258], BF16) for i in range(2)]
            qsel = self.scr(stack, "qsel", [128, 2, NS, NS], BF16)
            Wdg = self.scr(stack, "Wdg", [NS, NS], F32)
            hs = tmp1f
            snrow_T, nTn_T, qsel_T, Wdg_T = [Tk() for _ in range(4)]
            hs_T = tmp1_T
            Cs_T = [Tk() for _ in range(NB)]
            Cnb_T = [Tk(), Tk()]
            vsel_T = [Tk(), Tk()]
            sgb, sgb_T, semt, swt = G["sgb"], G["sgb_T"], G["semt"], G["swt"]
            eyeb = cst[:, 256:512].rearrange("p (a b) -> p a b", a=NS)

            kb.op(POOL, lambda e: e.memset(vext[:, :, 256:258], 1.0), writes=[vext_T])

            ks_s = self.scr(stack, "ks_s", [NS, 256], BF16)
            vs_s = self.scr(stack, "vs_s", [NS, 258], BF16)
            nm_s = self.scr(stack, "nm_s", [NS, 256], BF16)
            ks_sT, vs_sT, nm_sT = Tk(), Tk(), Tk()

            def prep(h):
                kb.dma(SP, negMb, negM_d[h:h + 1, :].partition_broadcast(128), reads=[G["negM_dT"]],
                       writes=[negMb_T])
                kb.op(DVE, lambda e: e.tensor_scalar(out=Mend[:, 0:8],
                                                     in0=negMb[:, 0:1024].rearrange("p (c t) -> p c t", t=128)[:, :, 0],
                                                     scalar1=-1.0, scalar2=None, op0=ALU.mult),
                      reads=[negMb_T], writes=[Mend_T])
                kb.op(DVE, lambda e: e.tensor_scalar(out=Mend[:, 8:9], in0=negMb[:, 1024:1025], scalar1=-1.0,
                                                     scalar2=None, op0=ALU.mult), reads=[negMb_T], writes=[Mend_T])
                kb.op(DVE, lambda e: e.tensor_tensor(out=cs[:, 0, :], in0=Gcol[:, :, 1, h], in1=Mend[:, 0:8], op=ALU.add),
                      reads=[Gcol_T, Mend_T], writes=[cs_T])
                kb.op(DVE, lambda e: e.tensor_tensor(out=cs[:, 1, :], in0=Gcol[:, :, 0, h], in1=Mend[:, 1:9],
                                                     op=ALU.subtract), reads=[Gcol_T, Mend_T], writes=[cs_T])
                kb.op(DVE, lambda e: e.tensor_tensor(out=cs[:, 2, :], in0=Mend[:, 0:8], in1=Mend[:, 1:9],
                                                     op=ALU.subtract), reads=[Mend_T], writes=[cs_T])
                kb.op(ACT, lambda e: e.activation(out=cs, in_=cs, func=AF.Exp), reads=[cs_T], writes=[cs_T])
                kb.dma(SP, Cst, xdst[h * 256:(h + 1) * 256, 0:257].rearrange("(dc p) e -> p dc e", p=128),
                       reads=[T_xdst], writes=[Cst_T])
                kb.op(DVE, lambda e: e.tensor_scalar(out=Cst, in0=Cst, scalar1=flag[:, 0:1], scalar2=None,
                                                     op0=ALU.mult), reads=[Cst_T, T_const], writes=[Cst_T])
                kb.op(ACT, lambda e: e.copy(out=Cbs[0][:, :, 0:257], in_=Cst), reads=[Cst_T], writes=[Cb_Ts[0]])


            def proj_gen(h, plo, phi):
                def fm(view, vT, evac):
                    for ec in range(2):
                        for (t0, n) in token_tiles(NT):
                            pi = pick(plo, phi)
                            kb.mm(psum[pi][:, :n],
                                  [(view[:, k, ec * 128:(ec + 1) * 128], xnT[:, k, t0:t0 + n]) for k in range(KD)],
                                  reads=[xnTk(k, t0) for k in range(KD)] + [vT], writes=[psum_T[pi]])
                            evac(ec, t0, n, psum[pi][:, :n], psum_T[pi])
                            yield

                def tm(view, vT, evac):
                    for r in range(9):
                        rows = min(128, NT - r * 128)
                        pi = pick(plo, phi)
                        kb.mm(psum[pi][:rows, :256],
                              [(xnT[:, k, r * 128:r * 128 + rows], view[:, k, 0:256]) for k in range(KD)],
                              reads=[xnTk(k, tts_of(r * 128)) for k in range(KD)] + [vT], writes=[psum_T[pi]])
                        evac(r, rows, psum[pi][:rows, :256], psum_T[pi])
                        yield

                qv, qvT = ring.load(w["w_in"], 0, KD, h * 256, 256)

                def evq(ec, t0, n, ps, psT):
                    kb.op(ACT, lambda e: e.copy(out=qT[:, ec, t0:t0 + n], in_=ps), reads=[psT], writes=[q_T])

                yield from fm(qv, qvT, evq)
                kv, kvT = ring.load(w["w_in"], 0, KD, 1024 + h * 256, 256)

                def evk(ec, t0, n, ps, psT):
                    kb.op(ACT, lambda e: e.activation(out=kT[:, ec, t0:t0 + n], in_=ps, func=AF.Copy, scale=0.0625),
                          reads=[psT], writes=[k_T])

                yield from fm(kv, kvT, evk)

                def evkt(r, rows, ps, psT):
                    kb.op(DVE, lambda e: e.tensor_scalar(out=ktok[:rows, r, :], in0=ps, scalar1=0.0625, scalar2=None,
                                                         op0=ALU.mult), reads=[psT], writes=[ktok_T])

                for r in range(9):
                    rows = min(128, NT - r * 128)
                    pi = pick(plo, phi)
                    pb = psum[pi].bitcast(BF16)
                    outs = [pb[:rows, dc * 128:(dc + 1) * 128] for dc in range(2)]
                    pairs = [(kT[:, dc, r * 128:r * 128 + rows], identb) for dc in range(2)]
                    kb.mm(outs, pairs, reads=[k_T, T_const], writes=[psum_T[pi]], transpose=True)
                    kb.op(DVE, lambda e: e.tensor_copy(out=ktok[:rows, r, :], in_=pb[:rows, 0:256]),
                          reads=[psum_T[pi]], writes=[ktok_T])
                    yield
                vv, vvT = ring.load(w["w_in"], 0, KD, 2048 + h * 256, 256)

                def evv(r, rows, ps, psT):
                    kb.op(ACT, lambda e: e.copy(out=vext[:rows, r, 0:256], in_=ps), reads=[psT], writes=[vext_T])

                yield from tm(vv, vvT, evv)
                ov, ovT = ring.load(w["w_in"], 0, KD, 3072 + h * 256, 256)
                kb.dma(SP, nmb, nmb_d[:, h * 256:(h + 1) * 256], writes=[nmb_T])

                def evo(r, rows, ps, psT):
                    kb.op(ACT, lambda e: e.activation(out=nmso[:rows, r, :], in_=ps, func=AF.Sigmoid),
                          reads=[psT], writes=[nmso_T])
                    kb.op(DVE, lambda e: e.tensor_tensor(out=nmso[:rows, r, :], in0=nmso[:rows, r, :],
                                                         in1=nmb[:rows, :], op=ALU.mult),
                          reads=[nmso_T, nmb_T], writes=[nmso_T])

                yield from tm(ov, ovT, evo)

            def load_state(h, j):
                b = j % NB
                idx = j * 4 + h
                kb.dma(SP, Cs_[b][:, :, 0:256], sC_d[j, h].rearrange("(dc p) e -> p dc e", p=128),
                       writes=[Cs_T[b]])
                kb.op(POOL, lambda e: e.tensor_copy(out=Cs_[b][:, :, 256], in_=nTt[:, :, idx]),
                      reads=[nTt_T], writes=[Cs_T[b]])

            def chunks(h, gen):
                for j in range(NB):
                    load_state(h, j)
                def scores(c):
                    t0 = c * 128
                    tsl = slice(t0, t0 + 128)
                    b = c % 2
                    kb.mm(psum[b][:, 0:128], [(kT[:, dc, tsl], qT[:, dc, tsl]) for dc in range(2)],
                          reads=[k_T, q_T], writes=[psum_T[b]])
                    kb.op(POOL, lambda e: e.tensor_tensor(out=arg[b], in0=negMb[:, 1 + t0:1 + t0 + 128], in1=maskneg,
                                                          op=ALU.add), reads=[negMb_T, T_const], writes=[arg_T[b]])
                    kb.op(ACT, lambda e: e.activation(out=Dm[b], in_=arg[b], func=AF.Exp, scale=1.0,
                                                      bias=Gcol[:, c, 0, h:h + 1]),
                          reads=[arg_T[b], Gcol_T], writes=[Dm_T[b]])
                    kb.op(DVE, lambda e: e.tensor_tensor(out=SpT[b], in0=psum[b][:, 0:128], in1=Dm[b], op=ALU.mult),
                          reads=[psum_T[b], Dm_T[b]], writes=[SpT_T[b]])
                    kb.op(ACT, lambda e: e.activation(out=wkc[b], in_=ktok[:, c, :], func=AF.Copy,
                                                      scale=cs[:, 1, c:c + 1]),
                          reads=[ktok_T, cs_T], writes=[wkc_T[b]])

                scores(0)
                for c in range(8):
                    t0 = c * 128
                    tsl = slice(t0, t0 + 128)
                    b = c % 2
                    cbn, cbo = Cbs[(c + 1) % 2], Cbs[c % 2]
                    cbn_T, cbo_T = Cb_Ts[(c + 1) % 2], Cb_Ts[c % 2]
                    for dc in range(2):
                        pu = 4 + dc
                        kb.mm(psum[pu][:, 0:257], [(wkc[b][:, dc * 128:(dc + 1) * 128], vext[:, c, 0:257])],
                              reads=[wkc_T[b], vext_T], writes=[psum_T[pu]])
                    kb.mm(psum[3][:, 0:257], [(qT[:, dc, tsl], cbo[:, dc, 0:257]) for dc in range(2)],
                          reads=[q_T, cbo_T], writes=[psum_T[3]])
                    for dc in range(2):
                        pu = 4 + dc
                        kb.op(DVE, lambda e: e.scalar_tensor_tensor(out=Cst[:, dc, :], in0=Cst[:, dc, :],
                                                                    scalar=cs[:, 2, c:c + 1], in1=psum[pu][:, 0:257],
                                                                    op0=ALU.mult, op1=ALU.add),
                              reads=[Cst_T, cs_T, psum_T[pu]], writes=[Cst_T])
                    kb.op(ACT, lambda e: e.copy(out=cbn[:, :, 0:257], in_=Cst), reads=[Cst_T], writes=[cbn_T])
                    if c + 1 < 8:
                        scores(c + 1)
                    kb.mm(psum[2][:, 0:257], [(SpT[b], vext[:, c, 0:257])], reads=[SpT_T[b], vext_T],
                          writes=[psum_T[2]])
                    kb.op(ACT, lambda e: e.activation(out=tmp1, in_=psum[3][:, 0:257], func=AF.Copy,
                                                      scale=cs[:, 0, c:c + 1]), reads=[psum_T[3], cs_T],
                          writes=[tmp1_T])
                    kb.op(DVE, lambda e: e.tensor_tensor(out=num[:, c, :], in0=tmp1, in1=psum[2][:, 0:257], op=ALU.add),
                          reads=[tmp1_T, psum_T[2]], writes=[num_T])
                kb.dma(ACT, Cp_o[h].rearrange("(dc p) e -> p dc e", p=128), Cst[:, :, 0:256], reads=[Cst_T])
                with nc.allow_non_contiguous_dma(reason="n state column"):
                    kb.dma(ACT, np_o[h].rearrange("(dc p) -> p dc", p=128), Cst[:, :, 256], reads=[Cst_T])
                snapshot(h)
                finish_h(128, 8, num, num_T, Gcol[:, :, 2, h], Gcol_T, lambda c: nmso[:, c, :], nmso_T, h,
                         lambda c: c * 128, fin_tiles, gen=gen, npre=4, nper=1)

            def snapshot(h):
                kb.op(DVE, lambda e: e.tensor_copy(out=ks_s, in_=ktok[0:NS, 8, :]), reads=[ktok_T], writes=[ks_sT])
                kb.op(DVE, lambda e: e.tensor_copy(out=vs_s, in_=vext[0:NS, 8, :]), reads=[vext_T], writes=[vs_sT])
                kb.op(DVE, lambda e: e.tensor_copy(out=nm_s, in_=nmso[0:NS, 8, :]), reads=[nmso_T], writes=[nm_sT])
                kb.op(DVE, lambda e: e.tensor_scalar(out=Wdg, in0=ident[0:NS, 0:NS], scalar1=swt[:, h:h + 1],
                                                     scalar2=None, op0=ALU.mult),
                      reads=[T_const, sgb_T], writes=[Wdg_T])
                for dc in range(2):
                    kb.op(DVE, lambda e: e.tensor_tensor(out=qsel[:, dc, :, :],
                                                         in0=qT[:, dc, NPR:NT].unsqueeze(1).broadcast_to([128, NS, NS]),
                                                         in1=eyeb, op=ALU.mult),
                          reads=[q_T, T_const], writes=[qsel_T])

            def sample(h, gen):
                def mk_vsel(j):
                    kb.op(ACT, lambda e: e.activation(out=vsel[j % 2], in_=vs_s, func=AF.Copy,
                                                      scale=Wdg[:, j:j + 1]),
                          reads=[vs_sT, Wdg_T], writes=[vsel_T[j % 2]])

                def matvec(j):
                    b2 = j % 2
                    E = kb.pe
                    kb._deps(E, [qsel_T, Cnb_T[b2]], [psum_T[3]] if j == 0 else [])
                    for dc in range(2):
                        inst = nc.tensor.matmul(psum[3][0:NS, 0:257], lhsT=qsel[:, dc, j, :], rhs=Cnb[b2][:, dc, 0:257],
                                                start=(j == 0 and dc == 0), stop=(j == NS - 1 and dc == 1))
                    E.seq += 1
                    inst.then_inc(E.sem, 1)
                    tok = (E.sem, E.seq, E)
                    kb._mark(tok, [qsel_T, Cnb_T[b2]], [psum_T[3]] if j == NS - 1 else [])
                    if j != NS - 1:
                        psum_T[3].w = tok

                mk_vsel(0)
                for j in range(NS):
                    b = j % NB
                    b2 = j % 2
                    idx = j * 4 + h
                    if j >= NB:
                        load_state(h, j)
                    for dc in range(2):
                        po = 4 + dc
                        kb.mm(psum[po][:, 0:257],
                              [(ks_s[:, dc * 128:(dc + 1) * 128], vsel[b2][:, 0:257])],
                              reads=[ks_sT, vsel_T[b2]], writes=[psum_T[po]])
                    if j > 0:
                        matvec(j - 1)
                    if j + 1 < NS:
                        mk_vsel(j + 1)
                    for dc in range(2):
                        po = 4 + dc
                        kb.op(DVE, lambda e: e.scalar_tensor_tensor(out=Cs_[b][:, dc, :], in0=Cs_[b][:, dc, :],
                                                                    scalar=sgb[:, h * 48 + j:h * 48 + j + 1],
                                                                    in1=psum[po][:, 0:257], op0=ALU.mult, op1=ALU.add),
                              reads=[Cs_T[b], sgb_T, psum_T[po]], writes=[Cs_T[b]])
                    kb.op(ACT, lambda e: e.copy(out=Cnb[b2][:, :, 0:257], in_=Cs_[b]), reads=[Cs_T[b]],
                          writes=[Cnb_T[b2]])
                    kb.dma(ACT, Cs_o[j, h].rearrange("(dc p) e -> p dc e", p=128), Cs_[b][:, :, 0:256],
                           reads=[Cs_T[b]])
                    kb.op(ACT, lambda e: e.copy(out=nTn[:, :, idx], in_=Cs_[b][:, :, 256]),
                          reads=[Cs_T[b]], writes=[nTn_T])
                    if gen is not None:
                        for _ in range(2 if j < 7 else 1):
                            next(gen, None)
                matvec(NS - 1)
                kb.op(ACT, lambda e: e.copy(out=hs[0:NS, 0, 0:257], in_=psum[3][0:NS, 0:257]), reads=[psum_T[3]],
                      writes=[hs_T])
                finish_h(NS, 1, hs, hs_T, semt[:, h:h + 1], sgb_T, lambda c: nm_s, nm_sT, h,
                         lambda c: NPR, fin_tiles, gen=gen, npre=4, nper=0)


            prep(0)
            for _ in proj_gen(0, 0, 6):
                pass
            for h in range(H):
                gen = proj_gen(h + 1, 0, 3) if h + 1 < H else None
                with nc.named_scope("h%d_chunks" % h):
                    chunks(h, gen)
                if h + 1 < H:
                    prep(h + 1)
                with nc.named_scope("h%d_sample" % h):
                    sample(h, gen)
                    if gen is not None:
                        for _ in gen:
                            pass
            for dc in range(2):
                pi = 6 + dc
                kb.mm([psum[pi][0:64, 0:128]], [(nTn[:, dc, :], ident)], reads=[nTn_T, T_const],
                      writes=[psum_T[pi]], transpose=True)
                kb.op(DVE, lambda e: e.tensor_copy(out=snrow[:, dc * 128:(dc + 1) * 128], in_=psum[pi][0:64, 0:128]),
                      reads=[psum_T[pi]], writes=[snrow_T])
            kb.dma(SP, ns_o, snrow, reads=[snrow_T])

        def mix_conv(stack):
            ring = Ring(self, stack, 3)
            gcs = [self.scr(stack, "gcs%d" % i, [128, NT], F32) for i in range(2)]
            gbs = [self.scr(stack, "gbs%d" % i, [128, NT], F32) for i in range(2)]
            uext = [self.scr(stack, "uext%d" % i, [128, NPR + 2], F32) for i in range(2)]
            us = [self.scr(stack, "us%d" % i, [128, NS], F32) for i in range(2)]
            y1 = [self.scr(stack, "y1%d" % i, [128, NT], F32) for i in range(2)]
            sq = [self.scr(stack, "csq%d" % i, [128, 512], BF16) for i in range(2)]
            rs = self.scr(stack, "crs", [128, 512], F32)
            scrow = self.scr(stack, "scrow", [32, 1024], F32)
            cbT = self.scr(stack, "cbT", [128, 8, 32], F32)
            crow = self.scr(stack, "crow", [2, 1024], F32)
            csrow = scrow[0:NS, :]
            rs_T, scrow_T, cbT_T, crow_T = [Tk() for _ in range(4)]
            csrow_T = scrow_T
            gcs_T, gbs_T, uext_T, us_T, y1_T, sq_T = [[Tk(), Tk()] for _ in range(6)]
            kb.dma(SP, scrow, sconv_d, writes=[scrow_T])
            for cch in range(8):
                pi = 6 + cch % 2
                kb.mm([psum[pi][:, 0:32]], [(scrow[:, cch * 128:(cch + 1) * 128], ident[:32, :32])],
                      reads=[scrow_T, T_const], writes=[psum_T[pi]], transpose=True)
                kb.op(DVE, lambda e: e.tensor_copy(out=cbT[:, cch, :], in_=psum[pi][:, 0:32]), reads=[psum_T[pi]],
                      writes=[cbT_T])
            views = {}

            def stage_a(cch):
                blk, ec = cch // 2, cch % 2
                b = cch % 2
                if ec == 0:
                    views["gc"] = ring.load(w["w_in"], 0, KD, 5128 + blk * 256, 256)
                    views["xc"] = ring.load(w["w_in"], 0, KD, 6152 + blk * 256, 256)
                    views["gb"] = ring.load(w["w_in"], 0, KD, 4104 + blk * 256, 256)
                esl = slice(ec * 128, (ec + 1) * 128)
                gcv, gcT = views["gc"]
                xcv, xcT = views["xc"]
                gbv, gbT = views["gb"]
                for (t0, n) in token_tiles(NT):
                    pi = pick()
                    kb.mm(psum[pi][:, :n], [(gcv[:, k, esl], xnT[:, k, t0:t0 + n]) for k in range(KD)],
                          reads=[xnTk(k, t0) for k in range(KD)] + [gcT], writes=[psum_T[pi]])
                    kb.op(ACT, lambda e: e.copy(out=gcs[b][:, t0:t0 + n], in_=psum[pi][:, :n]), reads=[psum_T[pi]],
                          writes=[gcs_T[b]])
                pi = pick()
                kb.mm(psum[pi][:, 0:2], [(gcv[:, k, esl], xnpre[:, k, :]) for k in range(KD)],
                      reads=[T_convinit, gcT], writes=[psum_T[pi]])
                kb.op(ACT, lambda e: e.activation(out=g2, in_=psum[pi][:, 0:2], func=AF.Copy, scale=flag[:, 0:1]),
                      reads=[psum_T[pi], T_const], writes=[T_g2])
                pi = pick()
                kb.mm(psum[pi][:, 0:2], [(xcv[:, k, esl], xnpre[:, k, :]) for k in range(KD)],
                      reads=[T_convinit, xcT], writes=[psum_T[pi]])
                kb.op(DVE, lambda e: e.tensor_tensor(out=uext[b][:, 0:2], in0=g2, in1=psum[pi][:, 0:2], op=ALU.mult),
                      reads=[psum_T[pi], T_g2], writes=[uext_T[b]])
                for (t0, n) in token_tiles(NT):
                    pi = pick()
                    kb.mm(psum[pi][:, :n], [(xcv[:, k, esl], xnT[:, k, t0:t0 + n]) for k in range(KD)],
                          reads=[xnTk(k, t0) for k in range(KD)] + [xcT], writes=[psum_T[pi]])
                    if t0 < NPR:
                        kb.op(DVE, lambda e: e.tensor_tensor(out=uext[b][:, 2 + t0:2 + t0 + n], in0=gcs[b][:, t0:t0 + n],
                                                             in1=psum[pi][:, :n], op=ALU.mult),
                              reads=[gcs_T[b], psum_T[pi]], writes=[uext_T[b]])
                    else:
                        kb.op(DVE, lambda e: e.tensor_tensor(out=us[b], in0=gcs[b][:, t0:t0 + n], in1=psum[pi][:, :n],
                                                             op=ALU.mult),
                              reads=[gcs_T[b], psum_T[pi]], writes=[us_T[b]])
                for (t0, n) in token_tiles(NT):
                    pi = pick()
                    kb.mm(psum[pi][:, :n], [(gbv[:, k, esl], xnT[:, k, t0:t0 + n]) for k in range(KD)],
                          reads=[xnTk(k, t0) for k in range(KD)] + [gbT], writes=[psum_T[pi]])
                    kb.op(ACT, lambda e: e.copy(out=gbs[b][:, t0:t0 + n], in_=psum[pi][:, :n]), reads=[psum_T[pi]],
                          writes=[gbs_T[b]])

            def stage_b(cch):
                b = cch % 2
                cw = [pcol[:, 64 + 8 * jx + cch:65 + 8 * jx + cch] for jx in range(3)]
                cbias = pcol[:, 88 + cch:89 + cch]
                ncol = pcol[:, 96 + cch:97 + cch]
                yy, yT_ = y1[b], y1_T[b]
                ue, ueT = uext[b], uext_T[b]
                kb.op(DVE, lambda e: e.tensor_scalar(out=yy[:, 0:NPR], in0=ue[:, 0:NPR], scalar1=cw[0],
                                                     scalar2=cbias, op0=ALU.mult, op1=ALU.add),
                      reads=[ueT, T_const], writes=[yT_])
                kb.op(DVE, lambda e: e.scalar_tensor_tensor(out=yy[:, 0:NPR], in0=ue[:, 1:NPR + 1], scalar=cw[1],
                                                            in1=yy[:, 0:NPR], op0=ALU.mult, op1=ALU.add),
                      reads=[ueT, yT_], writes=[yT_])
                kb.op(DVE, lambda e: e.scalar_tensor_tensor(out=yy[:, 0:NPR], in0=ue[:, 2:NPR + 2], scalar=cw[2],
                                                            in1=yy[:, 0:NPR], op0=ALU.mult, op1=ALU.add),
                      reads=[ueT, yT_], writes=[yT_])
                cb3 = cbT[:, cch, :].rearrange("p (s j) -> p s j", j=2)
                kb.op(DVE, lambda e: e.tensor_scalar(out=yy[:, NPR:NT], in0=cb3[:, :, 0], scalar1=cw[0],
                                                     scalar2=cbias, op0=ALU.mult, op1=ALU.add),
                      reads=[cbT_T, T_const, yT_], writes=[yT_])
                kb.op(DVE, lambda e: e.scalar_tensor_tensor(out=yy[:, NPR:NT], in0=cb3[:, :, 1], scalar=cw[1],
                                                            in1=yy[:, NPR:NT], op0=ALU.mult, op1=ALU.add),
                      reads=[cbT_T, yT_], writes=[yT_])
                kb.op(DVE, lambda e: e.scalar_tensor_tensor(out=yy[:, NPR:NT], in0=us[b], scalar=cw[2],
                                                            in1=yy[:, NPR:NT], op0=ALU.mult, op1=ALU.add),
                      reads=[us_T[b], yT_], writes=[yT_])
                kb.op(DVE, lambda e: e.tensor_tensor(out=yy, in0=yy, in1=gbs[b], op=ALU.mult),
                      reads=[yT_, gbs_T[b]], writes=[yT_])
                for (t0, n) in token_tiles(NT):
                    rstd_from([(yy[:, t0:t0 + n], [yT_])], n, 1.0 / 128.0, sq, sq_T, rs[:, :n], rs_T, 7)
                    kb.op(DVE, lambda e: e.scalar_tensor_tensor(out=self.mixT[:, 8 + cch, t0:t0 + n],
                                                                in0=yy[:, t0:t0 + n], scalar=ncol, in1=rs[:, :n],
                                                                op0=ALU.mult, op1=ALU.mult),
                          reads=[yT_, rs_T, T_const], writes=[self.mix_T[(8 + cch, t0)]])
                pi = 6
                kb.mm([psum[pi][0:2, 0:128]], [(ue[:, NPR:NPR + 2], ident)], reads=[ueT, T_const],
                      writes=[psum_T[pi]], transpose=True)
                kb.op(ACT, lambda e: e.copy(out=crow[:, cch * 128:(cch + 1) * 128], in_=psum[pi][0:2, 0:128]),
                      reads=[psum_T[pi]], writes=[crow_T])
                kb.mm([psum[pi][0:NS, 0:128]], [(us[b], ident)], reads=[us_T[b], T_const], writes=[psum_T[pi]],
                      transpose=True)
                kb.op(ACT, lambda e: e.copy(out=csrow[:, cch * 128:(cch + 1) * 128], in_=psum[pi][0:NS, 0:128]),
                      reads=[psum_T[pi]], writes=[csrow_T])

            stage_a(0)
            for cch in range(8):
                if cch + 1 < 8:
                    stage_a(cch + 1)
                stage_b(cch)
            kb.dma(SP, convp_o, crow, reads=[crow_T])
            kb.dma(SP, convs_o[:, 1, :], csrow, reads=[csrow_T])
            kb.dma(SP, convs_o[:, 0, :], sconv_d.rearrange("(s j) c -> s j c", j=2)[:, 1, :])

        def mix_out(stack):
            ring = Ring(self, stack, 3)
            for blk in range(8):
                wv, wT = ring.load(w["w_out"], 0, KD, blk * 256, 256)
                for ec in range(2):
                    i = blk * 2 + ec
                    for (t0, n) in token_tiles(NT):
                        pi = pick()
                        kb.mm(psum[pi][:, :n],
                              [(wv[:, k, ec * 128:(ec + 1) * 128], self.mixT[:, k, t0:t0 + n]) for k in range(KD)],
                              reads=[self.mix_T[(k, t0)] for k in range(KD)] + [wT], writes=[psum_T[pi]])
                        kb.op(DVE, lambda e: e.tensor_tensor(out=hT[:, i, t0:t0 + n], in0=hT[:, i, t0:t0 + n],
                                                             in1=psum[pi][:, :n], op=ALU.add),
                              reads=[psum_T[pi], hTk(i, t0)], writes=[hTk(i, t0)])

        def phase(fn, *a):
            self.phase_id = getattr(self, "phase_id", 0) + 1
            with nc.named_scope("p%02d_%s" % (self.phase_id, fn.__name__)):
                with ExitStack() as s:
                    fn(*a, s)
                    kb.barrier()

        def phase2(*fns):
            self.phase_id = getattr(self, "phase_id", 0) + 1
            with nc.named_scope("p%02d_%s" % (self.phase_id, fns[-1][0].__name__)):
                with ExitStack() as s:
                    for f in fns:
                        f[0](*f[1:], s)
                    kb.barrier()

        F1 = (w["ffn1_gate"], w["ffn1_up"], w["ffn1_down"])
        F2 = (w["ffn2_gate"], w["ffn2_up"], w["ffn2_down"])
        phase(load_T, xm, NT)
        if ALL or "ffn1" in st:
            phase2((rmsnorm, NT, 0), (ffn, NT) + F1)
        if ALL or "mix" in st:
            with ExitStack() as sm:
                G = {"Gcol": self.scr(sm, "Gcol", [128, 8, 3, 4], F32), "Gcol_T": Tk(),
                     "sgb": self.scr(sm, "sgb", [128, 192], F32), "sgb_T": Tk(),
                     "semt": self.scr(sm, "semt", [NS, 4], F32), "swt": self.scr(sm, "swt", [NS, 4], F32),
                     "negM_dT": Tk(), "sgs_dT": Tk(),
                     "nTt": self.scr(sm, "nTt", [128, 2, 64], F32), "nTt_T": Tk()}
                phase2((rmsnorm, NT, 16), (prefix_state, G))
                self.mixT = self.scr(sm, "mixT", [128, KD, NT], BF16)
                self.mix_T = {(k, t0): Tk() for k in range(KD) for (t0, n) in token_tiles(NT)}
                phase(mix_heads, G)
                phase(mix_conv)
                phase(mix_out)
        if ALL or "ffn2" in st:
            phase2((rmsnorm, NT, 32), (ffn, NT) + F2)
        self.phase_id += 1
        with nc.named_scope("p%02d_final" % self.phase_id):
            with ExitStack() as s:
                final_out(y_out, NT, s)
        kb.finish()
        return nc


def make_pcol(inp):
    pc = np.zeros((128, 128), np.float32)

    def put(c0, v):
        v = np.asarray(v, np.float32).reshape(-1, 128)
        pc[:, c0:c0 + v.shape[0]] = v.T

    put(0, inp["norm_ffn1"][0])
    put(16, inp["norm_mix"][0])
    put(32, inp["norm_ffn2"][0])
    put(48, inp["norm_final"])
    put(64, inp["conv_w"][0, 0])
    put(72, inp["conv_w"][0, 1])
    put(80, inp["conv_w"][0, 2])
    put(88, inp["conv_b"][0])
    put(96, inp["norm_conv"][0])
    return pc


def make_cst():
    c = np.zeros((128, 512), np.float32)
    c[:, 0:128] = np.eye(128, dtype=np.float32)
    c[:, 256:512] = np.eye(16, dtype=np.float32).reshape(1, 256)
    s_i = np.arange(128)[:, None]
    t_i = np.arange(128)[None, :]
    c[:, 128:256] = np.where(s_i <= t_i, 0.0, -30000.0)
    return c


_CACHE = {}

W_NAMES = ["ffn1_gate", "ffn1_up", "ffn1_down", "w_in", "w_out", "ffn2_gate", "ffn2_up", "ffn2_down"]


def core_inputs(inp, c, shared):
    b, half = c // 2, c % 2
    f32 = np.float32
    xm = np.concatenate([inp["x_prompt"][b, half * NPR:(half + 1) * NPR], inp["x_sample"][c * NS:(c + 1) * NS, 0]], 0)
    m = dict(shared)
    m["xm"] = np.ascontiguousarray(xm, f32)
    m["flag"] = np.full((128, 1), float(half), f32)
    sl = slice(c * NS, (c + 1) * NS)
    m["sC"] = np.ascontiguousarray(inp["state_mlstm_C"][0, sl], f32)
    m["sn"] = np.ascontiguousarray(inp["state_mlstm_n"][0, sl], f32).reshape(NS * H, DK)
    m["sm"] = np.ascontiguousarray(inp["state_mlstm_m"][0, sl], f32)
    m["sconv"] = np.ascontiguousarray(inp["state_conv"][0, sl], f32).reshape(NS * 2, 1024)
    return m


def shared_inputs(inp):
    f32 = np.float32
    sh = {"pcol": make_pcol(inp), "cst": make_cst(),
          "nmb": np.ascontiguousarray(np.broadcast_to(np.asarray(inp["norm_mlstm"][0], f32)[None, :], (128, 1024))),
          "gfb": np.ascontiguousarray(np.broadcast_to(np.asarray(inp["norm_final"], f32)[None, :], (128, D))),
          "bg": np.ascontiguousarray(np.asarray(inp["b_gates"][0], f32).reshape(2, 4).T)}
    for nm in W_NAMES:
        sh[nm] = np.ascontiguousarray(inp[nm][0], f32)
    return sh


def assemble(res):
    f32 = np.float32
    y_p = np.zeros((4, 2048, D), f32)
    y_s = np.zeros((128, 1, D), f32)
    C_p = np.zeros((1, 4, H, DK, DK), f32)
    n_p = np.zeros((1, 4, H, DK), f32)
    m_p = np.zeros((1, 4, H), f32)
    conv_p = np.zeros((1, 4, 2, 1024), f32)
    C_s = np.zeros((1, 128, H, DK, DK), f32)
    n_s = np.zeros((1, 128, H, DK), f32)
    m_s = np.zeros((1, 128, H), f32)
    conv_s = np.zeros((1, 128, 2, 1024), f32)
    for c in range(8):
        r = res[c]
        b, half = c // 2, c % 2
        y_p[b, half * NPR:(half + 1) * NPR] = r["y"][:NPR]
        sl = slice(c * NS, (c + 1) * NS)
        y_s[sl, 0] = r["y"][NPR:NT]
        if half == 1:
            C_p[0, b] = r["Cp"]
            n_p[0, b] = r["np"]
            m_p[0, b] = r["mp"][:, 0]
            conv_p[0, b] = r["convp"]
        C_s[0, sl] = r["Cs"]
        n_s[0, sl] = r["ns"].reshape(NS, H, DK)
        m_s[0, sl] = r["ms"]
        conv_s[0, sl] = r["convs"]
    return (y_p, y_s, C_p, n_p, m_p, conv_p, C_s, n_s, m_s, conv_s)


def kernel(**inputs):
    if "prog" not in _CACHE:
        p = Prog(("all",))
        p.build()
        _CACHE["prog"] = p
    p = _CACHE["prog"]
    inp = {k: np.asarray(v) for k, v in inputs.items()}
    sh = shared_inputs(inp)
    in_maps = [core_inputs(inp, c, sh) for c in range(8)]
    res = run_bass_kernel_spmd(p.nc, in_maps, core_ids=list(range(8)))
    return assemble(res.results)
```

```python
from contextlib import ExitStack

import numpy as np
import concourse.bass as bass
import concourse.mybir as mybir
from concourse.bass_utils import run_bass_kernel_spmd

F32 = mybir.dt.float32
BF16 = mybir.dt.bfloat16
AF = mybir.ActivationFunctionType
ALU = mybir.AluOpType

D = 2048
DFF = 5504
NFF = DFF // 128
KD = D // 128
H = 4
DK = 256
DIN = 7176
NPR = 1024
NS = 16
NT = NPR + NS
EPS = 1e-6
WINDOW = 4
SLOT_ELEMS = 4096
NSLOT = 5


class Tk:
    __slots__ = ("w", "r")

    def __init__(self):
        self.w = None
        self.r = {}


class Eng:
    def __init__(self, nc, name, eng, ndma=0):
        self.name = name
        self.eng = eng
        self.sem = nc.alloc_semaphore("s_" + name)
        self.seq = 0
        self.waited = {}
        self.slots = [[nc.alloc_semaphore("d_%s%d" % (name, i)), 0] for i in range(ndma)]
        self.nxt = 0


class KB:
    def __init__(self, nc):
        self.nc = nc
        self.pe = Eng(nc, "pe", nc.tensor)
        self.act = Eng(nc, "act", nc.scalar, ndma=6)
        self.dve = Eng(nc, "dve", nc.vector)
        self.pool = Eng(nc, "pool", nc.gpsimd, ndma=10)
        self.sp = Eng(nc, "sp", nc.sync, ndma=12)
        self.engs = [self.pe, self.act, self.dve, self.pool, self.sp]

    def _deps(self, E, reads, writes):
        need = {}

        def add(tok):
            sem, val, owner = tok
            if owner is E:
                if E is self.pe or E.seq - val >= WINDOW:
                    return
            k = id(sem)
            cur = need.get(k)
            if cur is None or cur[1] < val:
                need[k] = (sem, val)

        for t in reads:
            if t.w is not None:
                add(t.w)
        for t in writes:
            if t.w is not None:
                add(t.w)
            for tok in t.r.values():
                add(tok)
        for k, (sem, val) in need.items():
            if E.waited.get(k, 0) >= val:
                continue
            E.eng.wait_ge(sem, val)
            E.waited[k] = val

    def _mark(self, tok, reads, writes):
        k = id(tok[0])
        for t in reads:
            t.r[k] = tok
        for t in writes:
            t.w = tok
            t.r = {}

    def op(self, E, fn, reads=(), writes=()):
        self._deps(E, reads, writes)
        inst = fn(E.eng)
        E.seq += 1
        inst.then_inc(E.sem, 1)
        self._mark((E.sem, E.seq, E), reads, writes)

    def mm(self, out_ap, pairs, reads=(), writes=(), transpose=False):
        E = self.pe
        self._deps(E, reads, writes)
        n = len(pairs)
        inst = None
        for i, (l, r) in enumerate(pairs):
            if transpose:
                inst = self.nc.tensor.transpose(out_ap[i], l, r)
            else:
                inst = self.nc.tensor.matmul(out_ap, lhsT=l, rhs=r, start=(i == 0), stop=(i == n - 1))
        E.seq += 1
        inst.then_inc(E.sem, 1)
        self._mark((E.sem, E.seq, E), reads, writes)

    def dma(self, Q, out_ap, in_ap, reads=(), writes=(), **kw):
        self._deps(Q, reads, writes)
        slot = Q.slots[Q.nxt]
        Q.nxt = (Q.nxt + 1) % len(Q.slots)
        sem, cnt = slot
        k = id(sem)
        if cnt > 0 and Q.waited.get(k, 0) < cnt:
            Q.eng.wait_ge(sem, cnt)
            Q.waited[k] = cnt
        inst = Q.eng.dma_start(out=out_ap, in_=in_ap, **kw)
        slot[1] = cnt + 16
        inst.then_inc(sem, 16)
        self._mark((sem, cnt + 16, None), reads, writes)

    def barrier(self):
        for E in self.engs:
            for Fe in self.engs:
                if Fe is not E and Fe.seq > 0 and E.waited.get(id(Fe.sem), 0) < Fe.seq:
                    E.eng.wait_ge(Fe.sem, Fe.seq)
                    E.waited[id(Fe.sem)] = Fe.seq
                for sem, cnt in Fe.slots:
                    if cnt > 0 and E.waited.get(id(sem), 0) < cnt:
                        E.eng.wait_ge(sem, cnt)
                        E.waited[id(sem)] = cnt

    def finish(self):
        for Q in (self.sp, self.pool, self.act):
            for sem, cnt in Q.slots:
                if cnt > 0 and Q.waited.get(id(sem), 0) < cnt:
                    Q.eng.wait_ge(sem, cnt)
                    Q.waited[id(sem)] = cnt


def token_tiles(nt):
    out = []
    t = 0
    while t < nt:
        n = min(512, nt - t)
        out.append((t, n))
        t += n
    return out


class Ring:
    def __init__(self, prog, stack, nslots):
        self.prog = prog
        self.slots = [prog.scr(stack, "wring", [128, SLOT_ELEMS], BF16) for _ in range(nslots)]
        self.T = [Tk() for _ in range(nslots)]
        self.nxt = 0

    def load(self, wap, r0, nrow_chunks, c0, ncols):
        assert nrow_chunks * ncols <= SLOT_ELEMS
        i = self.nxt
        self.nxt = (i + 1) % len(self.slots)
        view = self.slots[i][:, 0:nrow_chunks * ncols].rearrange("p (k c) -> p k c", k=nrow_chunks)
        src = wap[r0 * 128:(r0 + nrow_chunks) * 128, c0:c0 + ncols].rearrange("(k p) c -> p k c", p=128)
        kb = self.prog.kb
        if ncols * 4 < 512:
            with self.prog.nc.allow_non_contiguous_dma(reason="narrow gate columns"):
                kb.dma(kb.pool, view, src, writes=[self.T[i]])
        else:
            kb.dma(kb.pool, view, src, writes=[self.T[i]])
        return view, self.T[i]


class Prog:
    def __init__(self, stages=("all",)):
        self.stages = stages
        nc = bass.Bass("TRN2", target_bir_lowering=False)
        self.nc = nc
        self.kb = KB(nc)
        self.din = {}
        self.dout = {}
        self.uid = 0

    def inp(self, name, shape, dt=F32):
        t = self.nc.dram_tensor(name, list(shape), dt, kind="ExternalInput")
        self.din[name] = t
        return t.ap()

    def outp(self, name, shape, dt=F32):
        t = self.nc.dram_tensor(name, list(shape), dt, kind="ExternalOutput")
        self.dout[name] = t
        return t.ap()

    def dscr(self, name, shape, dt=F32):
        return self.nc.dram_tensor(name, list(shape), dt).ap()

    def sb(self, name, shape, dt):
        return self.nc.alloc_sbuf_tensor(name, list(shape), dt).ap()

    def scr(self, stack, name, shape, dt):
        self.uid += 1
        return stack.enter_context(self.nc.sbuf_tensor("%s_u%d" % (name, self.uid), list(shape), dt)).ap()

    def build(self):
        nc, kb = self.nc, self.kb
        PE, ACT, DVE, POOL, SP = kb.pe, kb.act, kb.dve, kb.pool, kb.sp
        st = self.stages
        ALL = "all" in st

        xm = self.inp("xm", [NT, D])
        pcol_d = self.inp("pcol", [128, 128])
        cst_d = self.inp("cst", [128, 512])
        flag_d = self.inp("flag", [128, 1])
        nmb_d = self.inp("nmb", [128, 1024])
        gfb_d = self.inp("gfb", [128, D])
        bg_d = self.inp("bg", [4, 2])
        sC_d = self.inp("sC", [NS, H, DK, DK])
        sn_d = self.inp("sn", [NS * H, DK])
        sm_d = self.inp("sm", [NS, H])
        sconv_d = self.inp("sconv", [NS * 2, 1024])
        w = {}
        for nm, shp in [("ffn1_gate", [D, DFF]), ("ffn1_up", [D, DFF]), ("ffn1_down", [DFF, D]),
                        ("w_in", [D, DIN]), ("w_out", [D, D]),
                        ("ffn2_gate", [D, DFF]), ("ffn2_up", [D, DFF]), ("ffn2_down", [DFF, D])]:
            w[nm] = self.inp(nm, shp)
        y_out = self.outp("y", [NT, D])
        Cp_o = self.outp("Cp", [H, DK, DK])
        np_o = self.outp("np", [H, DK])
        mp_o = self.outp("mp", [H, 1])
        convp_o = self.outp("convp", [2, 1024])
        Cs_o = self.outp("Cs", [NS, H, DK, DK])
        ns_o = self.outp("ns", [NS * H, DK])
        ms_o = self.outp("ms", [NS, H])
        convs_o = self.outp("convs", [NS, 2, 1024])
        XR, XC = 1024, 264
        xsrc = self.dscr("cc_src", [XR, XC])
        xdst = self.dscr("cc_dst", [2 * XR, XC])
        xsa = self.dscr("cc_src_a", [32, XC])
        xda = self.dscr("cc_dst_a", [64, XC])
        cc_sem = nc.alloc_semaphore("cc_sem")
        cc_sem_a = nc.alloc_semaphore("cc_sem_a")
        T_xdst = Tk()
        T_xda = Tk()
        negM_d = self.dscr("scr_negM", [H, 1025])
        sgs_d = self.dscr("scr_sgs", [H, 48])

        onesb = self.sb("onesb", [128, 128], BF16)
        identb = self.sb("identb", [128, 128], BF16)
        pcol = self.sb("pcol_sb", [128, 128], F32)
        cst = self.sb("cst_sb", [128, 512], F32)
        ident = cst[:, 0:128]
        epsc = self.sb("epsc", [128, 1], F32)
        onec = self.sb("onec", [128, 1], F32)
        flag = self.sb("flag_sb", [128, 1], F32)
        minit = self.sb("minit", [4, 1], F32)
        xnpre = self.sb("xnpre", [128, KD, 2], BF16)
        g2 = self.sb("g2", [128, 2], F32)
        T_g2 = Tk()
        hT = self.sb("hT", [128, KD, NT], F32)
        xnT = self.sb("xnT", [128, KD, NT], BF16)
        psum = [nc.alloc_psum_tensor("ps%d" % i, [128, 512], F32).ap() for i in range(8)]
        psum_T = [Tk() for _ in range(8)]
        T_const = Tk()
        T_minit = Tk()
        T_convinit = Tk()
        T_cinit = Tk()
        hT_T = {}
        xnT_T = {}
        self.pcnt = 0

        def hTk(k, t0):
            return hT_T.setdefault((k, t0), Tk())

        def xnTk(k, t0):
            return xnT_T.setdefault((k, t0), Tk())

        def pick(lo=0, hi=6):
            self.pcnt += 1
            return lo + (self.pcnt % (hi - lo))

        kb.dma(SP, cst, cst_d, writes=[T_const])
        kb.dma(SP, pcol, pcol_d, writes=[T_const])
        kb.dma(SP, flag, flag_d, writes=[T_const])
        kb.op(POOL, lambda e: e.memset(onesb, 1.0), writes=[T_const])
        kb.op(POOL, lambda e: e.tensor_copy(out=identb, in_=cst[:, 0:128]), reads=[T_const], writes=[T_const])
        kb.op(POOL, lambda e: e.memset(epsc, EPS), writes=[T_const])
        kb.op(POOL, lambda e: e.memset(onec, 1.0), writes=[T_const])
        maskneg = cst[:, 128:256]

        def tts_of(t0):
            return (t0 // 512) * 512

        def load_T(x_dram, nt, stack):
            xrow = [self.scr(stack, "xrow%d" % i, [128, D], F32) for i in range(2)]
            xrow_T = [Tk(), Tk()]
            nrt = (nt + 127) // 128
            cnt = 0
            for r in range(nrt):
                rows = min(128, nt - r * 128)
                b = r % 2
                kb.dma(SP, xrow[b][:rows, :], x_dram[r * 128:r * 128 + rows, :], writes=[xrow_T[b]])
                tt0 = tts_of(r * 128)
                for kg in range(4):
                    pi = 6 + (cnt % 2)
                    outs = [psum[pi][:, i * 128:i * 128 + rows] for i in range(4)]
                    pairs = [(xrow[b][:rows, (kg * 4 + i) * 128:(kg * 4 + i + 1) * 128], ident[:rows, :rows])
                             for i in range(4)]
                    kb.mm(outs, pairs, reads=[xrow_T[b], T_const], writes=[psum_T[pi]], transpose=True)
                    src = psum[pi].rearrange("p (a b) -> p a b", a=4)[:, :, 0:rows]
                    dst = hT[:, kg * 4:(kg + 1) * 4, r * 128:r * 128 + rows]
                    wr = [hTk(kg * 4 + i, tt0) for i in range(4)]
                    if cnt % 2 == 0:
                        kb.op(ACT, lambda e: e.copy(out=dst, in_=src), reads=[psum_T[pi]], writes=wr)
                    else:
                        kb.op(DVE, lambda e: e.tensor_copy(out=dst, in_=src), reads=[psum_T[pi]], writes=wr)
                    cnt += 1

        def rstd_from(srcs, n, inv_count, sq, sq_T, rs_ap, rs_T, pi):
            ns_ = len(srcs)
            for k, (ap, tks) in enumerate(srcs):
                b = k % 2
                kb.op(ACT, lambda e: e.activation(out=sq[b][:, :n], in_=ap, func=AF.Square),
                      reads=tks, writes=[sq_T[b]])
                E = kb.pe
                kb._deps(E, [sq_T[b], T_const], [psum_T[pi]] if k == 0 else [])
                inst = nc.tensor.matmul(psum[pi][:, :n], lhsT=onesb, rhs=sq[b][:, :n], start=(k == 0),
                                        stop=(k == ns_ - 1))
                E.seq += 1
                inst.then_inc(E.sem, 1)
                tok = (E.sem, E.seq, E)
                kb._mark(tok, [sq_T[b]], [psum_T[pi]] if k == ns_ - 1 else [])
                if k != ns_ - 1:
                    psum_T[pi].w = tok
            kb.op(ACT, lambda e: e.activation(out=rs_ap, in_=psum[pi][:, :n], func=AF.Ln,
                                              scale=inv_count, bias=epsc),
                  reads=[psum_T[pi], T_const], writes=[rs_T])
            kb.op(ACT, lambda e: e.activation(out=rs_ap, in_=rs_ap, func=AF.Exp, scale=-0.5),
                  reads=[rs_T], writes=[rs_T])

        def rmsnorm(nt, gcol0, stack, out_bf16=True):
            sq = [self.scr(stack, "nsq%d" % i, [128, 512], BF16) for i in range(2)]
            sq_T = [Tk(), Tk()]
            rs = self.scr(stack, "nrs", [128, 512], F32)
            rs_T = Tk()
            for ti, (t0, n) in enumerate(token_tiles(nt)):
                rstd_from([(hT[:, k, t0:t0 + n], [hTk(k, t0)]) for k in range(KD)], n, 1.0 / D,
                          sq, sq_T, rs[:, :n], rs_T, 6 + ti % 2)
                for k in range(KD):
                    if out_bf16:
                        o_ap, o_T = xnT[:, k, t0:t0 + n], [xnTk(k, t0)]
                    else:
                        o_ap, o_T = hT[:, k, t0:t0 + n], [hTk(k, t0)]
                    kb.op(DVE, lambda e: e.scalar_tensor_tensor(out=o_ap, in0=hT[:, k, t0:t0 + n],
                                                                scalar=pcol[:, gcol0 + k:gcol0 + k + 1],
                                                                in1=rs[:, :n], op0=ALU.mult, op1=ALU.mult),
                          reads=[hTk(k, t0), rs_T, T_const], writes=o_T)

        def ffn(nt, wg, wu, wd, stack):
            tts = token_tiles(nt)
            NH = 22
            ring = Ring(self, stack, 5)
            actT = self.scr(stack, "actT", [128, NH, nt], BF16)
            act_T = {}
            sg = [self.scr(stack, "sg%d" % i, [128, 512], F32) for i in range(3)]
            sg_T = [Tk(), Tk(), Tk()]
            PG, PU, PD = [0, 1, 4], [2, 3, 5], [4, 5, 0, 1]
            cnt = 0
            for (j0, j1) in [(0, NH), (NH, NFF)]:
                j = j0
                while j < j1:
                    nj = min(2, j1 - j)
                    gv, gT = ring.load(wg, 0, KD, j * 128, nj * 128)
                    uv, uT = ring.load(wu, 0, KD, j * 128, nj * 128)
                    for jj in range(nj):
                        jl = j + jj - j0
                        for (t0, n) in tts:
                            b = cnt % 3
                            cnt += 1
                            pg, pu = PG[b], PU[b]
                            rd = [xnTk(k, t0) for k in range(KD)]
                            kb.mm(psum[pg][:, :n],
                                  [(gv[:, k, jj * 128:(jj + 1) * 128], xnT[:, k, t0:t0 + n]) for k in range(KD)],
                                  reads=rd + [gT], writes=[psum_T[pg]])
                            kb.mm(psum[pu][:, :n],
                                  [(uv[:, k, jj * 128:(jj + 1) * 128], xnT[:, k, t0:t0 + n]) for k in range(KD)],
                                  reads=rd + [uT], writes=[psum_T[pu]])
                            kb.op(ACT, lambda e: e.activation(out=sg[b][:, :n], in_=psum[pg][:, :n], func=AF.Silu),
                                  reads=[psum_T[pg]], writes=[sg_T[b]])
                            aT = act_T.setdefault((jl, t0), Tk())
                            kb.op(DVE, lambda e: e.tensor_tensor(out=actT[:, jl, t0:t0 + n], in0=sg[b][:, :n],
                                                                 in1=psum[pu][:, :n], op=ALU.mult),
                                  reads=[sg_T[b], psum_T[pu]], writes=[aT])
                    j += nj
                njh = j1 - j0
                for i in range(KD):
                    dv, dT = ring.load(wd, j0, njh, i * 128, 128)
                    for (t0, n) in tts:
                        pd = PD[cnt % 4]
                        cnt += 1
                        kb.mm(psum[pd][:, :n],
                              [(dv[:, jl, :], actT[:, jl, t0:t0 + n]) for jl in range(njh)],
                              reads=[act_T[(jl, t0)] for jl in range(njh)] + [dT], writes=[psum_T[pd]])
                        kb.op(DVE, lambda e: e.scalar_tensor_tensor(out=hT[:, i, t0:t0 + n], in0=psum[pd][:, :n],
                                                                    scalar=0.5, in1=hT[:, i, t0:t0 + n],
                                                                    op0=ALU.mult, op1=ALU.add),
                              reads=[psum_T[pd], hTk(i, t0)], writes=[hTk(i, t0)])

        def final_out(y_dram, nt, stack):
            sq = [self.scr(stack, "fsq%d" % i, [128, 512], BF16) for i in range(2)]
            sq_T = [Tk(), Tk()]
            rs = self.scr(stack, "frs", [128, 512], F32)
            rs_T = Tk()
            gfb = self.scr(stack, "gfb_sb", [128, D], F32)
            gfb_T = Tk()
            rcol = self.scr(stack, "rcol", [128, 16], F32)
            rcol_T = Tk()
            yrow = [self.scr(stack, "yrow%d" % i, [128, D], F32) for i in range(2)]
            yrow_T = [Tk(), Tk()]
            kb.dma(SP, gfb, gfb_d, writes=[gfb_T])
            cnt = 0
            for (t0, n) in token_tiles(nt):
                rstd_from([(hT[:, k, t0:t0 + n], [hTk(k, t0)]) for k in range(KD)], n, 1.0 / D,
                          sq, sq_T, rs[:, :n], rs_T, 6)
                rts = [(r, min(128, nt - r * 128)) for r in range((nt + 127) // 128)
                       if t0 <= r * 128 < t0 + n]
                for (r, rows) in rts:
                    off = r * 128 - t0
                    kb.mm([psum[7][:rows, 0:1]], [(rs[0:1, off:off + rows], ident[0:1, 0:1])],
                          reads=[rs_T, T_const], writes=[psum_T[7]], transpose=True)
                    kb.op(ACT, lambda e: e.copy(out=rcol[:rows, r:r + 1], in_=psum[7][:rows, 0:1]),
                          reads=[psum_T[7]], writes=[rcol_T])
                for (r, rows) in rts:
                    b = r % 2
                    for kg in range(4):
                        pi = 6 + (cnt % 2)
                        cnt += 1
                        outs = [psum[pi][:rows, i * 128:(i + 1) * 128] for i in range(4)]
                        pairs = [(hT[:, kg * 4 + i, r * 128:r * 128 + rows], ident) for i in range(4)]
                        kb.mm(outs, pairs, reads=[hTk(kg * 4 + i, t0) for i in range(4)] + [T_const],
                              writes=[psum_T[pi]], transpose=True)
                        kb.op(DVE, lambda e: e.scalar_tensor_tensor(out=yrow[b][:rows, kg * 512:(kg + 1) * 512],
                                                                    in0=psum[pi][:rows, :],
                                                                    scalar=rcol[:rows, r:r + 1],
                                                                    in1=gfb[:rows, kg * 512:(kg + 1) * 512],
                                                                    op0=ALU.mult, op1=ALU.mult),
                              reads=[psum_T[pi], rcol_T, gfb_T], writes=[yrow_T[b]])
                    kb.dma(SP, y_dram[r * 128:r * 128 + rows, :], yrow[b][:rows, :], reads=[yrow_T[b]])

        def store_T(y_dram, nt, stack):
            yrow = [self.scr(stack, "yrow%d" % i, [128, D], F32) for i in range(2)]
            yrow_T = [Tk(), Tk()]
            nrt = (nt + 127) // 128
            cnt = 0
            for r in range(nrt):
                rows = min(128, nt - r * 128)
                b = r % 2
                tt0 = tts_of(r * 128)
                for kg in range(4):
                    pi = 6 + (cnt % 2)
                    outs = [psum[pi][:rows, i * 128:(i + 1) * 128] for i in range(4)]
                    pairs = [(hT[:, kg * 4 + i, r * 128:r * 128 + rows], ident) for i in range(4)]
                    kb.mm(outs, pairs, reads=[hTk(kg * 4 + i, tt0) for i in range(4)] + [T_const],
                          writes=[psum_T[pi]], transpose=True)
                    dst = yrow[b][:rows, kg * 512:(kg + 1) * 512]
                    src = psum[pi][:rows, :]
                    if cnt % 2 == 0:
                        kb.op(ACT, lambda e: e.copy(out=dst, in_=src), reads=[psum_T[pi]], writes=[yrow_T[b]])
                    else:
                        kb.op(DVE, lambda e: e.tensor_copy(out=dst, in_=src), reads=[psum_T[pi]], writes=[yrow_T[b]])
                    cnt += 1
                kb.dma(SP, y_dram[r * 128:r * 128 + rows, :], yrow[b][:rows, :], reads=[yrow_T[b]])

        def dense_fm(view, vT, ncols, nt, evac, m_rows=128):
            for ec in range((ncols + m_rows - 1) // m_rows):
                m = min(m_rows, ncols - ec * m_rows)
                for (t0, n) in token_tiles(nt):
                    pi = pick()
                    kb.mm(psum[pi][:m, :n],
                          [(view[:, k, ec * m_rows:ec * m_rows + m], xnT[:, k, t0:t0 + n]) for k in range(KD)],
                          reads=[xnTk(k, t0) for k in range(KD)] + [vT], writes=[psum_T[pi]])
                    evac(ec, t0, n, psum[pi][:m, :n], psum_T[pi])

        def dense_tm(view, vT, ncols, nt, evac):
            for r in range((nt + 127) // 128):
                rows = min(128, nt - r * 128)
                pi = pick()
                kb.mm(psum[pi][:rows, :ncols],
                      [(xnT[:, k, r * 128:r * 128 + rows], view[:, k, 0:ncols]) for k in range(KD)],
                      reads=[xnTk(k, tts_of(r * 128)) for k in range(KD)] + [vT], writes=[psum_T[pi]])
                evac(r, rows, psum[pi][:rows, :ncols], psum_T[pi])

        def gate_rows(nt, nscan, m0, m0_reads, stack, ring):
            R = {}

            def row(name, p=4, n=nt):
                R[name] = self.scr(stack, "r_" + name, [p, n], F32)
                R[name + "_T"] = Tk()
                return R[name], R[name + "_T"]

            bg = self.scr(stack, "bg_sb", [4, 2], F32)
            bg_T = Tk()
            kb.dma(SP, bg, bg_d, writes=[bg_T])
            kb.op(DVE, lambda e: e.tensor_scalar(out=bg, in0=bg, scalar1=1.0 / 15.0, scalar2=None, op0=ALU.mult),
                  reads=[bg_T], writes=[bg_T])
            gv, gT = ring.load(w["w_in"], 0, KD, 4096, 8)
            graw, graw_T = row("graw", 8)
            fgraw, fgraw_T = row("fgraw")

            def ev(ec, t0, n, ps, psT):
                kb.op(ACT, lambda e: e.copy(out=graw[:, t0:t0 + n], in_=ps), reads=[psT], writes=[graw_T])

            dense_fm(gv, gT, 8, nt, ev, m_rows=8)
            kb.dma(SP, fgraw, graw[4:8, :], reads=[graw_T], writes=[fgraw_T])
            logi, logi_T = row("logi")
            logf, logf_T = row("logf")
            kb.op(ACT, lambda e: e.activation(out=logi, in_=graw[0:4, :], func=AF.Tanh, scale=1.0 / 15.0,
                                              bias=bg[:, 0:1]), reads=[graw_T, bg_T], writes=[logi_T])
            kb.op(ACT, lambda e: e.activation(out=logf, in_=fgraw, func=AF.Tanh, scale=1.0 / 15.0,
                                              bias=bg[:, 1:2]), reads=[fgraw_T, bg_T], writes=[logf_T])
            kb.op(ACT, lambda e: e.activation(out=logf, in_=logf, func=AF.Exp, scale=-15.0),
                  reads=[logf_T], writes=[logf_T])
            kb.op(ACT, lambda e: e.activation(out=logf, in_=logf, func=AF.Ln, scale=1.0, bias=onec[0:4, :]),
                  reads=[logf_T, T_const], writes=[logf_T])
            kb.op(DVE, lambda e: e.tensor_scalar(out=logf, in0=logf, scalar1=-1.0, scalar2=None, op0=ALU.mult),
                  reads=[logf_T], writes=[logf_T])
            kb.op(DVE, lambda e: e.tensor_scalar(out=logi, in0=logi, scalar1=15.0, scalar2=None, op0=ALU.mult),
                  reads=[logi_T], writes=[logi_T])
            zeros, zeros_T = row("zeros", 4, nscan)
            kb.op(POOL, lambda e: e.memset(zeros, 0.0), writes=[zeros_T])
            Bc, Bc_T = row("B", 4, nscan)
            mr, mr_T = row("m", 4, nscan)
            kb.op(DVE, lambda e: e.tensor_tensor_scan(out=Bc, data0=logf[:, 0:nscan], data1=zeros, initial=0.0,
                                                      op0=ALU.add, op1=ALU.add),
                  reads=[logf_T, zeros_T], writes=[Bc_T])
            kb.op(DVE, lambda e: e.tensor_tensor_scan(out=mr, data0=logf[:, 0:nscan], data1=logi[:, 0:nscan],
                                                      initial=m0, op0=ALU.add, op1=ALU.max),
                  reads=[logf_T, logi_T] + m0_reads, writes=[mr_T])
            return R

        def transpose_rows(row_ap, rowT, nrow, nchunks, dst_ap, dst_T, pi):
            outs = [psum[pi][:, c * nrow:(c + 1) * nrow] for c in range(nchunks)]
            pairs = [(row_ap[0:nrow, c * 128:(c + 1) * 128], ident[:nrow, :nrow]) for c in range(nchunks)]
            kb.mm(outs, pairs, reads=[rowT, T_const], writes=[psum_T[pi]], transpose=True)
            src = psum[pi][:, 0:nchunks * nrow].rearrange("p (c h) -> p c h", c=nchunks)
            kb.op(DVE, lambda e: e.tensor_copy(out=dst_ap, in_=src), reads=[psum_T[pi]], writes=[dst_T])

        def prefix_state(G, stack):
            ring = Ring(self, stack, 4)
            R = gate_rows(NT, NPR, 0.0, [], stack, ring)
            logi, logf, Bc, mr = R["logi"], R["logf"], R["B"], R["m"]
            a = self.scr(stack, "pa", [4, NPR], F32)
            a_T = Tk()
            kb.op(DVE, lambda e: e.tensor_tensor(out=a, in0=logi[:, 0:NPR], in1=Bc, op=ALU.subtract),
                  reads=[R["logi_T"], R["B_T"]], writes=[a_T])
            nME = self.scr(stack, "pnME", [4, 1], F32)
            nME_T = Tk()
            kb.op(DVE, lambda e: e.tensor_tensor(out=nME, in0=Bc[:, NPR - 1:NPR], in1=mr[:, NPR - 1:NPR],
                                                 op=ALU.subtract),
                  reads=[R["B_T"], R["m_T"]], writes=[nME_T])
            kb.op(ACT, lambda e: e.activation(out=a, in_=a, func=AF.Exp, scale=1.0, bias=nME),
                  reads=[a_T, nME_T], writes=[a_T])
            wcol = self.scr(stack, "pwcol", [128, 8, 4], F32)
            wcol_T = Tk()
            transpose_rows(a, a_T, 4, 8, wcol, wcol_T, 7)
            T_src = []

            def src_tk():
                T_src.append(Tk())
                return T_src[-1]

            T_srca = []

            def srca_tk():
                T_srca.append(Tk())
                return T_srca[-1]

            with nc.allow_non_contiguous_dma(reason="tiny m row"):
                kb.dma(SP, xsa[16:17, 0:4].rearrange("o h -> h o"), mr[:, NPR - 1:NPR], reads=[R["m_T"]],
                       writes=[srca_tk()])
            xnp32 = self.scr(stack, "xnp32", [128, KD, 2], F32)
            x32_T = Tk()
            kb.op(DVE, lambda e: e.tensor_copy(out=xnp32, in_=xnT[:, :, NPR - 2:NPR]),
                  reads=[xnTk(k, 512) for k in range(KD)], writes=[x32_T])
            with nc.allow_non_contiguous_dma(reason="2-token boundary block"):
                kb.dma(SP, xsa[0:16, 0:256].rearrange("k (p j) -> p k j", j=2), xnp32, reads=[x32_T],
                       writes=[srca_tk()])
            ktok = self.scr(stack, "pktok", [128, 8, 256], BF16)
            ktok_T = Tk()
            vext = self.scr(stack, "pvext", [128, 8, 258], BF16)
            vext_T = Tk()
            wk = self.scr(stack, "pwk", [128, 8, 256], BF16)
            wk_T = Tk()
            cst_t = self.scr(stack, "pcst", [128, 2, 257], F32)
            cst_T = Tk()
            kb.op(POOL, lambda e: e.memset(vext[:, :, 256:258], 1.0), writes=[vext_T])
            for h in range(H):
                kv, kT_ = ring.load(w["w_in"], 0, KD, 1024 + h * 256, 256)

                def evk(r, rows, ps, psT):
                    kb.op(ACT, lambda e: e.activation(out=ktok[:rows, r, :], in_=ps, func=AF.Copy, scale=0.0625),
                          reads=[psT], writes=[ktok_T])

                dense_tm(kv, kT_, 256, NPR, evk)
                vv, vT_ = ring.load(w["w_in"], 0, KD, 2048 + h * 256, 256)

                def evv(r, rows, ps, psT):
                    kb.op(ACT, lambda e: e.copy(out=vext[:rows, r, 0:256], in_=ps), reads=[psT], writes=[vext_T])

                dense_tm(vv, vT_, 256, NPR, evv)
                for c in range(8):
                    kb.op(DVE, lambda e: e.tensor_scalar(out=wk[:, c, :], in0=ktok[:, c, :],
                                                         scalar1=wcol[:, c, h:h + 1], scalar2=None, op0=ALU.mult),
                          reads=[ktok_T, wcol_T], writes=[wk_T])
                for dc in range(2):
                    pi = pick()
                    kb.mm(psum[pi][:, 0:257],
                          [(wk[:, c, dc * 128:(dc + 1) * 128], vext[:, c, 0:257]) for c in range(8)],
                          reads=[wk_T, vext_T], writes=[psum_T[pi]])
                    kb.op(ACT, lambda e: e.copy(out=cst_t[:, dc, :], in_=psum[pi][:, 0:257]),
                          reads=[psum_T[pi]], writes=[cst_T])
                kb.dma(SP, xsrc[h * 256:(h + 1) * 256, 0:257].rearrange("(dc p) e -> p dc e", p=128), cst_t,
                       reads=[cst_T], writes=[src_tk()])
                if h == 0:
                    kb._deps(POOL, T_srca, [T_xda])
                    inst = nc.gpsimd.collective_compute("AllGather", ALU.bypass,
                                                        replica_groups=[[0, 1], [2, 3], [4, 5], [6, 7]],
                                                        ins=[xsa], outs=[xda])
                    inst.then_inc(cc_sem_a)
                    kb._mark((cc_sem_a, 1, None), T_srca, [T_xda])
                    gates_pre(G, stack, R)
                    snrow0 = self.scr(stack, "snrow0", [64, 256], F32)
                    snrow0_T = Tk()
                    kb.dma(SP, snrow0, sn_d, writes=[snrow0_T])
                    for dc in range(2):
                        pi = 6 + dc
                        kb.mm([psum[pi][:, 0:64]], [(snrow0[:, dc * 128:(dc + 1) * 128], ident[:64, :64])],
                              reads=[snrow0_T, T_const], writes=[psum_T[pi]], transpose=True)
                        kb.op(DVE, lambda e: e.tensor_copy(out=G["nTt"][:, dc, :], in_=psum[pi][:, 0:64]),
                              reads=[psum_T[pi]], writes=[G["nTt_T"]])

                    mraw = self.scr(stack, "mraw", [4, 1], F32)
                    mraw_T = Tk()
                    with nc.allow_non_contiguous_dma(reason="tiny m row"):
                        kb.dma(SP, mraw, xda[16:17, 0:4].rearrange("o h -> h o"), reads=[T_xda], writes=[mraw_T])
                    kb.op(DVE, lambda e: e.tensor_scalar(out=minit, in0=mraw, scalar1=flag[0:4, :], scalar2=None,
                                                         op0=ALU.mult),
                          reads=[mraw_T, T_const], writes=[T_minit])
                    with nc.allow_non_contiguous_dma(reason="2-token boundary block"):
                        kb.dma(SP, xnp32, xda[0:16, 0:256].rearrange("k (p j) -> p k j", j=2), reads=[T_xda],
                               writes=[x32_T])
                    kb.op(DVE, lambda e: e.tensor_copy(out=xnpre, in_=xnp32), reads=[x32_T], writes=[T_convinit])
                    mr2 = self.scr(stack, "mr2", [4, NPR], F32)
                    mr2_T = Tk()
                    kb.op(DVE, lambda e: e.tensor_tensor_scan(out=mr2, data0=logf[:, 0:NPR], data1=logi[:, 0:NPR],
                                                              initial=minit, op0=ALU.add, op1=ALU.max),
                          reads=[R["logf_T"], R["logi_T"], T_minit], writes=[mr2_T])
                    gates_post(G, stack, R, mr2, mr2_T)

            kb._deps(POOL, T_src, [T_xdst])
            inst = nc.gpsimd.collective_compute("AllGather", ALU.bypass,
                                                replica_groups=[[0, 1], [2, 3], [4, 5], [6, 7]],
                                                ins=[xsrc], outs=[xdst])
            inst.then_inc(cc_sem)
            kb._mark((cc_sem, 1, None), T_src, [T_xdst])

        def gates_post(G, stack, R, mr, mr_T):
            logi, logf, Bc = R["logi"], R["logf"], R["B"]
            a = self.scr(stack, "ma", [4, NPR], F32)
            a_T = Tk()
            negMx = self.scr(stack, "mnegMx", [4, 1025], F32)
            negMx_T = Tk()
            negm = self.scr(stack, "mnegm", [4, NPR], F32)
            negm_T = Tk()
            kb.op(DVE, lambda e: e.tensor_tensor(out=a, in0=logi[:, 0:NPR], in1=Bc, op=ALU.subtract),
                  reads=[R["logi_T"], R["B_T"]], writes=[a_T])
            kb.op(DVE, lambda e: e.tensor_tensor(out=negMx[:, 1:1025], in0=Bc, in1=mr, op=ALU.subtract),
                  reads=[R["B_T"], mr_T], writes=[negMx_T])
            kb.op(DVE, lambda e: e.tensor_scalar(out=negMx[:, 0:1], in0=minit, scalar1=-1.0, scalar2=None,
                                                 op0=ALU.mult),
                  reads=[T_minit], writes=[negMx_T])
            kb.op(DVE, lambda e: e.tensor_scalar(out=negm, in0=mr, scalar1=-1.0, scalar2=None, op0=ALU.mult),
                  reads=[mr_T], writes=[negm_T])
            kb.dma(SP, negM_d, negMx, reads=[negMx_T], writes=[G["negM_dT"]])
            kb.dma(SP, mp_o, mr[:, NPR - 1:NPR], reads=[mr_T])
            Gcol = G["Gcol"]
            transpose_rows(a, a_T, 4, 8, Gcol[:, :, 0, :], G["Gcol_T"], 7)
            transpose_rows(negMx[:, 1:1025], negMx_T, 4, 8, Gcol[:, :, 1, :], G["Gcol_T"], 6)
            transpose_rows(negm, negm_T, 4, 8, Gcol[:, :, 2, :], G["Gcol_T"], 7)
            kb.op(ACT, lambda e: e.activation(out=Gcol[:, :, 2, :], in_=Gcol[:, :, 2, :], func=AF.Exp),
                  reads=[G["Gcol_T"]], writes=[G["Gcol_T"]])

        def gates_pre(G, stack, R):
            logi, logf = R["logi"], R["logf"]
            mold = self.scr(stack, "smold", [4, NS], F32)
            mold_T = Tk()
            with nc.allow_non_contiguous_dma(reason="tiny state transpose"):
                kb.dma(SP, mold, sm_d.rearrange("s h -> h s"), writes=[mold_T])
            srow = self.scr(stack, "srow", [4, 48], F32)
            srow_T = Tk()
            mnew = self.scr(stack, "smnew", [4, NS], F32)
            mnew_T = Tk()
            lfs, lis = logf[:, NPR:NT], logi[:, NPR:NT]
            kb.op(DVE, lambda e: e.tensor_tensor(out=mold, in0=mold, in1=lfs, op=ALU.add),
                  reads=[mold_T, R["logf_T"]], writes=[mold_T])
            kb.op(DVE, lambda e: e.tensor_tensor(out=mnew, in0=mold, in1=lis, op=ALU.max),
                  reads=[mold_T, R["logi_T"]], writes=[mnew_T])
            kb.op(DVE, lambda e: e.tensor_tensor(out=srow[:, 0:16], in0=mold, in1=mnew, op=ALU.subtract),
                  reads=[mold_T, mnew_T], writes=[srow_T])
            kb.op(DVE, lambda e: e.tensor_tensor(out=srow[:, 16:32], in0=lis, in1=mnew, op=ALU.subtract),
                  reads=[R["logi_T"], mnew_T], writes=[srow_T])
            kb.op(DVE, lambda e: e.tensor_scalar(out=srow[:, 32:48], in0=mnew, scalar1=-1.0, scalar2=None,
                                                 op0=ALU.mult),
                  reads=[mnew_T], writes=[srow_T])
            kb.op(ACT, lambda e: e.activation(out=srow, in_=srow, func=AF.Exp), reads=[srow_T], writes=[srow_T])
            kb.dma(SP, sgs_d, srow, reads=[srow_T], writes=[G["sgs_dT"]])
            with nc.allow_non_contiguous_dma(reason="tiny state transpose"):
                kb.dma(SP, ms_o.rearrange("s h -> h s"), mnew, reads=[mnew_T])
            kb.dma(SP, G["sgb"], sgs_d.rearrange("h c -> (h c)").partition_broadcast(128),
                   reads=[G["sgs_dT"]], writes=[G["sgb_T"]])
            with nc.allow_non_contiguous_dma(reason="tiny state transpose"):
                kb.dma(SP, G["semt"], sgs_d[:, 32:48].rearrange("h s -> s h"), reads=[G["sgs_dT"]],
                       writes=[G["sgb_T"]])
                kb.dma(SP, G["swt"], sgs_d[:, 16:32].rearrange("h s -> s h"), reads=[G["sgs_dT"]],
                       writes=[G["sgb_T"]])


        def mix_gates(G, stack):
            ring = Ring(self, stack, 2)
            R = run_gen(gate_rows(NT, NPR, minit, [T_minit], stack, ring)) if False else gate_rows(NT, NPR, minit, [T_minit], stack, ring)
            gates_post(G, stack, R, R["m"], R["m_T"])
            gates_pre(G, stack, R)

        def finish_h(P_, NC_, num, num_T, emt, emt_T, nmso_fn, nmso_T, h, col_fn, tiles, gen=None, npre=0, nper=0):
            sc, sc_T, junk, junk_T, ha, ha_T = tiles
            S = lambda q: sc[:P_, q, 0:NC_]
            kb.op(ACT, lambda e: e.activation(out=S(0), in_=num[:P_, :, 256], func=AF.Abs),
                  reads=[num_T], writes=[sc_T])
            kb.op(DVE, lambda e: e.tensor_tensor(out=S(0), in0=S(0), in1=emt, op=ALU.max),
                  reads=[sc_T, emt_T], writes=[sc_T])
            kb.op(DVE, lambda e: e.reciprocal(out=S(1), in_=S(0)), reads=[sc_T], writes=[sc_T])
            for c in range(NC_):
                kb.op(ACT, lambda e: e.activation(out=junk[:P_, :], in_=num[:P_, c, 0:256], func=AF.Square,
                                                  accum_out=sc[:P_, 2, c:c + 1]),
                      reads=[num_T, sc_T], writes=[junk_T, sc_T])
            kb.op(DVE, lambda e: e.tensor_tensor(out=S(3), in0=S(2), in1=S(1), op=ALU.mult),
                  reads=[sc_T], writes=[sc_T])
            kb.op(DVE, lambda e: e.tensor_tensor(out=S(3), in0=S(3), in1=S(1), op=ALU.mult),
                  reads=[sc_T], writes=[sc_T])
            kb.op(ACT, lambda e: e.activation(out=S(4), in_=S(3), func=AF.Ln, scale=1.0 / 256.0,
                                              bias=epsc[:P_, :]), reads=[sc_T, T_const], writes=[sc_T])
            kb.op(ACT, lambda e: e.activation(out=S(4), in_=S(4), func=AF.Exp, scale=-0.5),
                  reads=[sc_T], writes=[sc_T])
            kb.op(DVE, lambda e: e.tensor_tensor(out=S(5), in0=S(4), in1=S(1), op=ALU.mult),
                  reads=[sc_T], writes=[sc_T])
            if gen is not None:
                for _ in range(npre):
                    next(gen, None)
            for c in range(NC_):
                b = c % 2
                col0 = col_fn(c)
                if gen is not None:
                    for _ in range(nper):
                        next(gen, None)
                kb.op(DVE, lambda e: e.scalar_tensor_tensor(out=ha[b][:P_, :], in0=num[:P_, c, 0:256],
                                                            scalar=sc[:P_, 5, c:c + 1], in1=nmso_fn(c),
                                                            op0=ALU.mult, op1=ALU.mult),
                      reads=[num_T, sc_T, nmso_T], writes=[ha_T[b]])
                pi = 6 + b
                outs = [psum[pi][:, ec * 128:ec * 128 + P_] for ec in range(2)]
                pairs = [(ha[b][:P_, ec * 128:(ec + 1) * 128], ident[:P_, :P_]) for ec in range(2)]
                kb.mm(outs, pairs, reads=[ha_T[b], T_const], writes=[psum_T[pi]], transpose=True)
                src = psum[pi][:, 0:256].rearrange("p (a b) -> p a b", a=2)[:, :, 0:P_]
                dst = self.mixT[:, 2 * h:2 * h + 2, col0:col0 + P_]
                kb.op(ACT, lambda e: e.copy(out=dst, in_=src), reads=[psum_T[pi]],
                      writes=[self.mix_T[(2 * h, tts_of(col0))], self.mix_T[(2 * h + 1, tts_of(col0))]])

        def mix_heads(G, stack):
            ring = Ring(self, stack, 2)
            Gcol, Gcol_T = G["Gcol"], G["Gcol_T"]
            qT = self.scr(stack, "qT", [128, 2, NT], BF16)
            kT = self.scr(stack, "kT", [128, 2, NT], BF16)
            ktok = self.scr(stack, "ktok", [128, 9, 256], BF16)
            vext = self.scr(stack, "vext", [128, 9, 258], BF16)
            nmso = self.scr(stack, "nmso", [128, 9, 256], BF16)
            nmb = self.scr(stack, "nmb", [128, 256], F32)
            negMb = self.scr(stack, "negMb", [128, 1025], F32)
            Mend = self.scr(stack, "Mend", [128, 9], F32)
            Cst = self.scr(stack, "Cst", [128, 2, 257], F32)
            Cbs = [self.scr(stack, "Cb%d" % i, [128, 2, 258], BF16) for i in range(2)]
            Cb_Ts = [Tk(), Tk()]
            arg = [self.scr(stack, "arg%d" % i, [128, 128], F32) for i in range(2)]
            Dm = arg
            SpT = [self.scr(stack, "SpT%d" % i, [128, 128], BF16) for i in range(2)]
            tmp1f = self.scr(stack, "tmp1", [128, 1, 258], F32)
            tmp1 = tmp1f[:, 0, 0:257]
            num = self.scr(stack, "num", [128, 8, 257], F32)
            sc = self.scr(stack, "sc", [128, 6, 8], F32)
            junk = self.scr(stack, "junk", [128, 256], BF16)
            ha0 = self.scr(stack, "ha0", [128, 256], F32)
            ha = [ha0, ha0]
            wkc = [self.scr(stack, "wkc%d" % i, [128, 256], BF16) for i in range(2)]
            cs = self.scr(stack, "cs", [128, 3, 8], F32)
            q_T, k_T, ktok_T, vext_T, nmso_T, nmb_T, negMb_T, Mend_T = [Tk() for _ in range(8)]
            Cst_T, Cb_T, tmp1_T, num_T, sc_T, junk_T, cs_T = [Tk() for _ in range(7)]
            arg_T, SpT_T, wkc_T = [[Tk(), Tk()] for _ in range(3)]
            ha_T0 = Tk()
            ha_T = [ha_T0, ha_T0]
            Dm_T = arg_T
            fin_tiles = (sc, sc_T, junk, junk_T, ha, ha_T)
            NB = 3
            snrow = ha[0][0:64, :]
            nTt, nTt_T = G["nTt"], G["nTt_T"]
            nTn = self.scr(stack, "nTn", [128, 2, 64], F32)
            Cs_ = [self.scr(stack, "Cs%d" % i, [128, 2, 257], F32) for i in range(NB)]
            Cnb = [self.scr(stack, "Cnb%d" % i, [128, 2, 258], BF16) for i in range(2)]
            vsel = [self.scr(stack, "vsel%d" % i, [NS, 258], BF16) for i in range(2)]
            qsel = self.scr(stack, "qsel", [128, 2, NS, NS], BF16)
            Wdg = self.scr(stack, "Wdg", [NS, NS], F32)
            hs = tmp1f
            snrow_T, nTn_T, qsel_T, Wdg_T = [Tk() for _ in range(4)]
            hs_T = tmp1_T
            Cs_T = [Tk() for _ in range(NB)]
            Cnb_T = [Tk(), Tk()]
            vsel_T = [Tk(), Tk()]
            sgb, sgb_T, semt, swt = G["sgb"], G["sgb_T"], G["semt"], G["swt"]
            eyeb = cst[:, 256:512].rearrange("p (a b) -> p a b", a=NS)

            kb.op(POOL, lambda e: e.memset(vext[:, :, 256:258], 1.0), writes=[vext_T])

            ks_s = self.scr(stack, "ks_s", [NS, 256], BF16)
            vs_s = self.scr(stack, "vs_s", [NS, 258], BF16)
            nm_s = self.scr(stack, "nm_s", [NS, 256], BF16)
            ks_sT, vs_sT, nm_sT = Tk(), Tk(), Tk()

            def prep(h):
                kb.dma(SP, negMb, negM_d[h:h + 1, :].partition_broadcast(128), reads=[G["negM_dT"]],
                       writes=[negMb_T])
                kb.op(DVE, lambda e: e.tensor_scalar(out=Mend[:, 0:8],
                                                     in0=negMb[:, 0:1024].rearrange("p (c t) -> p c t", t=128)[:, :, 0],
                                                     scalar1=-1.0, scalar2=None, op0=ALU.mult),
                      reads=[negMb_T], writes=[Mend_T])
                kb.op(DVE, lambda e: e.tensor_scalar(out=Mend[:, 8:9], in0=negMb[:, 1024:1025], scalar1=-1.0,
                                                     scalar2=None, op0=ALU.mult), reads=[negMb_T], writes=[Mend_T])
                kb.op(DVE, lambda e: e.tensor_tensor(out=cs[:, 0, :], in0=Gcol[:, :, 1, h], in1=Mend[:, 0:8], op=ALU.add),
                      reads=[Gcol_T, Mend_T], writes=[cs_T])
                kb.op(DVE, lambda e: e.tensor_tensor(out=cs[:, 1, :], in0=Gcol[:, :, 0, h], in1=Mend[:, 1:9],
                                                     op=ALU.subtract), reads=[Gcol_T, Mend_T], writes=[cs_T])
                kb.op(DVE, lambda e: e.tensor_tensor(out=cs[:, 2, :], in0=Mend[:, 0:8], in1=Mend[:, 1:9],
                                                     op=ALU.subtract), reads=[Mend_T], writes=[cs_T])
                kb.op(ACT, lambda e: e.activation(out=cs, in_=cs, func=AF.Exp), reads=[cs_T], writes=[cs_T])
                kb.dma(SP, Cst, xdst[h * 256:(h + 1) * 256, 0:257].rearrange("(dc p) e -> p dc e", p=128),
                       reads=[T_xdst], writes=[Cst_T])
                kb.op(DVE, lambda e: e.tensor_scalar(out=Cst, in0=Cst, scalar1=flag[:, 0:1], scalar2=None,
                                                     op0=ALU.mult), reads=[Cst_T, T_const], writes=[Cst_T])
                kb.op(ACT, lambda e: e.copy(out=Cbs[0][:, :, 0:257], in_=Cst), reads=[Cst_T], writes=[Cb_Ts[0]])


            def proj_gen(h, plo, phi):
                def fm(view, vT, evac):
                    for ec in range(2):
                        for (t0, n) in token_tiles(NT):
                            pi = pick(plo, phi)
                            kb.mm(psum[pi][:, :n],
                                  [(view[:, k, ec * 128:(ec + 1) * 128], xnT[:, k, t0:t0 + n]) for k in range(KD)],
                                  reads=[xnTk(k, t0) for k in range(KD)] + [vT], writes=[psum_T[pi]])
                            evac(ec, t0, n, psum[pi][:, :n], psum_T[pi])
                            yield

                def tm(view, vT, evac):
                    for r in range(9):
                        rows = min(128, NT - r * 128)
                        pi = pick(plo, phi)
                        kb.mm(psum[pi][:rows, :256],
                              [(xnT[:, k, r * 128:r * 128 + rows], view[:, k, 0:256]) for k in range(KD)],
                              reads=[xnTk(k, tts_of(r * 128)) for k in range(KD)] + [vT], writes=[psum_T[pi]])
                        evac(r, rows, psum[pi][:rows, :256], psum_T[pi])
                        yield

                qv, qvT = ring.load(w["w_in"], 0, KD, h * 256, 256)

                def evq(ec, t0, n, ps, psT):
                    kb.op(ACT, lambda e: e.copy(out=qT[:, ec, t0:t0 + n], in_=ps), reads=[psT], writes=[q_T])

                yield from fm(qv, qvT, evq)
                kv, kvT = ring.load(w["w_in"], 0, KD, 1024 + h * 256, 256)

                def evk(ec, t0, n, ps, psT):
                    kb.op(ACT, lambda e: e.activation(out=kT[:, ec, t0:t0 + n], in_=ps, func=AF.Copy, scale=0.0625),
                          reads=[psT], writes=[k_T])

                yield from fm(kv, kvT, evk)

                def evkt(r, rows, ps, psT):
                    kb.op(DVE, lambda e: e.tensor_scalar(out=ktok[:rows, r, :], in0=ps, scalar1=0.0625, scalar2=None,
                                                         op0=ALU.mult), reads=[psT], writes=[ktok_T])

                for r in range(9):
                    rows = min(128, NT - r * 128)
                    pi = pick(plo, phi)
                    pb = psum[pi].bitcast(BF16)
                    outs = [pb[:rows, dc * 128:(dc + 1) * 128] for dc in range(2)]
                    pairs = [(kT[:, dc, r * 128:r * 128 + rows], identb) for dc in range(2)]
                    kb.mm(outs, pairs, reads=[k_T, T_const], writes=[psum_T[pi]], transpose=True)
                    kb.op(DVE, lambda e: e.tensor_copy(out=ktok[:rows, r, :], in_=pb[:rows, 0:256]),
                          reads=[psum_T[pi]], writes=[ktok_T])
                    yield
                vv, vvT = ring.load(w["w_in"], 0, KD, 2048 + h * 256, 256)

                def evv(r, rows, ps, psT):
                    kb.op(ACT, lambda e: e.copy(out=vext[:rows, r, 0:256], in_=ps), reads=[psT], writes=[vext_T])

                yield from tm(vv, vvT, evv)
                ov, ovT = ring.load(w["w_in"], 0, KD, 3072 + h * 256, 256)
                kb.dma(SP, nmb, nmb_d[:, h * 256:(h + 1) * 256], writes=[nmb_T])

                def evo(r, rows, ps, psT):
                    kb.op(ACT, lambda e: e.activation(out=nmso[:rows, r, :], in_=ps, func=AF.Sigmoid),
                          reads=[psT], writes=[nmso_T])
                    kb.op(DVE, lambda e: e.tensor_tensor(out=nmso[:rows, r, :], in0=nmso[:rows, r, :],
                                                         in1=nmb[:rows, :], op=ALU.mult),
                          reads=[nmso_T, nmb_T], writes=[nmso_T])

                yield from tm(ov, ovT, evo)

            def load_state(h, j):
                b = j % NB
                idx = j * 4 + h
                kb.dma(SP, Cs_[b][:, :, 0:256], sC_d[j, h].rearrange("(dc p) e -> p dc e", p=128),
                       writes=[Cs_T[b]])
                kb.op(POOL, lambda e: e.tensor_copy(out=Cs_[b][:, :, 256], in_=nTt[:, :, idx]),
                      reads=[nTt_T], writes=[Cs_T[b]])

            def chunks(h, gen):
                for j in range(NB):
                    load_state(h, j)
                def scores(c):
                    t0 = c * 128
                    tsl = slice(t0, t0 + 128)
                    b = c % 2
                    kb.mm(psum[b][:, 0:128], [(kT[:, dc, tsl], qT[:, dc, tsl]) for dc in range(2)],
                          reads=[k_T, q_T], writes=[psum_T[b]])
                    kb.op(POOL, lambda e: e.tensor_tensor(out=arg[b], in0=negMb[:, 1 + t0:1 + t0 + 128], in1=maskneg,
                                                          op=ALU.add), reads=[negMb_T, T_const], writes=[arg_T[b]])
                    kb.op(ACT, lambda e: e.activation(out=Dm[b], in_=arg[b], func=AF.Exp, scale=1.0,
                                                      bias=Gcol[:, c, 0, h:h + 1]),
                          reads=[arg_T[b], Gcol_T], writes=[Dm_T[b]])
                    kb.op(DVE, lambda e: e.tensor_tensor(out=SpT[b], in0=psum[b][:, 0:128], in1=Dm[b], op=ALU.mult),
                          reads=[psum_T[b], Dm_T[b]], writes=[SpT_T[b]])
                    kb.op(ACT, lambda e: e.activation(out=wkc[b], in_=ktok[:, c, :], func=AF.Copy,
                                                      scale=cs[:, 1, c:c + 1]),
                          reads=[ktok_T, cs_T], writes=[wkc_T[b]])

                scores(0)
                for c in range(8):
                    t0 = c * 128
                    tsl = slice(t0, t0 + 128)
                    b = c % 2
                    cbn, cbo = Cbs[(c + 1) % 2], Cbs[c % 2]
                    cbn_T, cbo_T = Cb_Ts[(c + 1) % 2], Cb_Ts[c % 2]
                    for dc in range(2):
                        pu = 4 + dc
                        kb.mm(psum[pu][:, 0:257], [(wkc[b][:, dc * 128:(dc + 1) * 128], vext[:, c, 0:257])],
                              reads=[wkc_T[b], vext_T], writes=[psum_T[pu]])
                    kb.mm(psum[3][:, 0:257], [(qT[:, dc, tsl], cbo[:, dc, 0:257]) for dc in range(2)],
                          reads=[q_T, cbo_T], writes=[psum_T[3]])
                    for dc in range(2):
                        pu = 4 + dc
                        kb.op(DVE, lambda e: e.scalar_tensor_tensor(out=Cst[:, dc, :], in0=Cst[:, dc, :],
                                                                    scalar=cs[:, 2, c:c + 1], in1=psum[pu][:, 0:257],
                                                                    op0=ALU.mult, op1=ALU.add),
                              reads=[Cst_T, cs_T, psum_T[pu]], writes=[Cst_T])
                    kb.op(ACT, lambda e: e.copy(out=cbn[:, :, 0:257], in_=Cst), reads=[Cst_T], writes=[cbn_T])
                    if c + 1 < 8:
                        scores(c + 1)
                    kb.mm(psum[2][:, 0:257], [(SpT[b], vext[:, c, 0:257])], reads=[SpT_T[b], vext_T],
                          writes=[psum_T[2]])
                    kb.op(ACT, lambda e: e.activation(out=tmp1, in_=psum[3][:, 0:257], func=AF.Copy,
                                                      scale=cs[:, 0, c:c + 1]), reads=[psum_T[3], cs_T],
                          writes=[tmp1_T])
                    kb.op(DVE, lambda e: e.tensor_tensor(out=num[:, c, :], in0=tmp1, in1=psum[2][:, 0:257], op=ALU.add),
                          reads=[tmp1_T, psum_T[2]], writes=[num_T])
                kb.dma(ACT, Cp_o[h].rearrange("(dc p) e -> p dc e", p=128), Cst[:, :, 0:256], reads=[Cst_T])
                with nc.allow_non_contiguous_dma(reason="n state column"):
                    kb.dma(ACT, np_o[h].rearrange("(dc p) -> p dc", p=128), Cst[:, :, 256], reads=[Cst_T])
                snapshot(h)
                finish_h(128, 8, num, num_T, Gcol[:, :, 2, h], Gcol_T, lambda c: nmso[:, c, :], nmso_T, h,
                         lambda c: c * 128, fin_tiles, gen=gen, npre=4, nper=1)

            def snapshot(h):
                kb.op(DVE, lambda e: e.tensor_copy(out=ks_s, in_=ktok[0:NS, 8, :]), reads=[ktok_T], writes=[ks_sT])
                kb.op(DVE, lambda e: e.tensor_copy(out=vs_s, in_=vext[0:NS, 8, :]), reads=[vext_T], writes=[vs_sT])
                kb.op(DVE, lambda e: e.tensor_copy(out=nm_s, in_=nmso[0:NS, 8, :]), reads=[nmso_T], writes=[nm_sT])
                kb.op(DVE, lambda e: e.tensor_scalar(out=Wdg, in0=ident[0:NS, 0:NS], scalar1=swt[:, h:h + 1],
                                                     scalar2=None, op0=ALU.mult),
                      reads=[T_const, sgb_T], writes=[Wdg_T])
                for dc in range(2):
                    kb.op(DVE, lambda e: e.tensor_tensor(out=qsel[:, dc, :, :],
                                                         in0=qT[:, dc, NPR:NT].unsqueeze(1).broadcast_to([128, NS, NS]),
                                                         in1=eyeb, op=ALU.mult),
                          reads=[q_T, T_const], writes=[qsel_T])

            def sample(h, gen):
                def mk_vsel(j):
                    kb.op(ACT, lambda e: e.activation(out=vsel[j % 2], in_=vs_s, func=AF.Copy,
                                                      scale=Wdg[:, j:j + 1]),
                          reads=[vs_sT, Wdg_T], writes=[vsel_T[j % 2]])

                def matvec(j):
                    b2 = j % 2
                    E = kb.pe
                    kb._deps(E, [qsel_T, Cnb_T[b2]], [psum_T[3]] if j == 0 else [])
                    for dc in range(2):
                        inst = nc.tensor.matmul(psum[3][0:NS, 0:257], lhsT=qsel[:, dc, j, :], rhs=Cnb[b2][:, dc, 0:257],
                                                start=(j == 0 and dc == 0), stop=(j == NS - 1 and dc == 1))
                    E.seq += 1
                    inst.then_inc(E.sem, 1)
                    tok = (E.sem, E.seq, E)
                    kb._mark(tok, [qsel_T, Cnb_T[b2]], [psum_T[3]] if j == NS - 1 else [])
                    if j != NS - 1:
                        psum_T[3].w = tok

                mk_vsel(0)
                for j in range(NS):
                    b = j % NB
                    b2 = j % 2
                    idx = j * 4 + h
                    if j >= NB:
                        load_state(h, j)
                    for dc in range(2):
                        po = 4 + dc
                        kb.mm(psum[po][:, 0:257],
                              [(ks_s[:, dc * 128:(dc + 1) * 128], vsel[b2][:, 0:257])],
                              reads=[ks_sT, vsel_T[b2]], writes=[psum_T[po]])
                    if j > 0:
                        matvec(j - 1)
                    if j + 1 < NS:
                        mk_vsel(j + 1)
                    for dc in range(2):
                        po = 4 + dc
                        kb.op(DVE, lambda e: e.scalar_tensor_tensor(out=Cs_[b][:, dc, :], in0=Cs_[b][:, dc, :],
                                                                    scalar=sgb[:, h * 48 + j:h * 48 + j + 1],
                                                                    in1=psum[po][:, 0:257], op0=ALU.mult, op1=ALU.add),
                              reads=[Cs_T[b], sgb_T, psum_T[po]], writes=[Cs_T[b]])
                    kb.op(ACT, lambda e: e.copy(out=Cnb[b2][:, :, 0:257], in_=Cs_[b]), reads=[Cs_T[b]],
                          writes=[Cnb_T[b2]])
                    kb.dma(ACT, Cs_o[j, h].rearrange("(dc p) e -> p dc e", p=128), Cs_[b][:, :, 0:256],
                           reads=[Cs_T[b]])
                    kb.op(ACT, lambda e: e.copy(out=nTn[:, :, idx], in_=Cs_[b][:, :, 256]),
                          reads=[Cs_T[b]], writes=[nTn_T])
                    if gen is not None:
                        for _ in range(2 if j < 7 else 1):
                            next(gen, None)
                matvec(NS - 1)
                kb.op(ACT, lambda e: e.copy(out=hs[0:NS, 0, 0:257], in_=psum[3][0:NS, 0:257]), reads=[psum_T[3]],
                      writes=[hs_T])
                finish_h(NS, 1, hs, hs_T, semt[:, h:h + 1], sgb_T, lambda c: nm_s, nm_sT, h,
                         lambda c: NPR, fin_tiles, gen=gen, npre=4, nper=0)


            prep(0)
            for _ in proj_gen(0, 0, 6):
                pass
            for h in range(H):
                gen = proj_gen(h + 1, 0, 3) if h + 1 < H else None
                with nc.named_scope("h%d_chunks" % h):
                    chunks(h, gen)
                if h + 1 < H:
                    prep(h + 1)
                with nc.named_scope("h%d_sample" % h):
                    sample(h, gen)
                    if gen is not None:
                        for _ in gen:
                            pass
            for dc in range(2):
                pi = 6 + dc
                kb.mm([psum[pi][0:64, 0:128]], [(nTn[:, dc, :], ident)], reads=[nTn_T, T_const],
                      writes=[psum_T[pi]], transpose=True)
                kb.op(DVE, lambda e: e.tensor_copy(out=snrow[:, dc * 128:(dc + 1) * 128], in_=psum[pi][0:64, 0:128]),
                      reads=[psum_T[pi]], writes=[snrow_T])
            kb.dma(SP, ns_o, snrow, reads=[snrow_T])

        def mix_conv(stack):
            ring = Ring(self, stack, 3)
            gcs = [self.scr(stack, "gcs%d" % i, [128, NT], F32) for i in range(2)]
            gbs = [self.scr(stack, "gbs%d" % i, [128, NT], F32) for i in range(2)]
            uext = [self.scr(stack, "uext%d" % i, [128, NPR + 2], F32) for i in range(2)]
            us = [self.scr(stack, "us%d" % i, [128, NS], F32) for i in range(2)]
            y1 = [self.scr(stack, "y1%d" % i, [128, NT], F32) for i in range(2)]
            sq = [self.scr(stack, "csq%d" % i, [128, 512], BF16) for i in range(2)]
            rs = self.scr(stack, "crs", [128, 512], F32)
            scrow = self.scr(stack, "scrow", [32, 1024], F32)
            cbT = self.scr(stack, "cbT", [128, 8, 32], F32)
            crow = self.scr(stack, "crow", [2, 1024], F32)
            csrow = scrow[0:NS, :]
            rs_T, scrow_T, cbT_T, crow_T = [Tk() for _ in range(4)]
            csrow_T = scrow_T
            gcs_T, gbs_T, uext_T, us_T, y1_T, sq_T = [[Tk(), Tk()] for _ in range(6)]
            kb.dma(SP, scrow, sconv_d, writes=[scrow_T])
            for cch in range(8):
                pi = 6 + cch % 2
                kb.mm([psum[pi][:, 0:32]], [(scrow[:, cch * 128:(cch + 1) * 128], ident[:32, :32])],
                      reads=[scrow_T, T_const], writes=[psum_T[pi]], transpose=True)
                kb.op(DVE, lambda e: e.tensor_copy(out=cbT[:, cch, :], in_=psum[pi][:, 0:32]), reads=[psum_T[pi]],
                      writes=[cbT_T])
            views = {}

            def stage_a(cch):
                blk, ec = cch // 2, cch % 2
                b = cch % 2
                if ec == 0:
                    views["gc"] = ring.load(w["w_in"], 0, KD, 5128 + blk * 256, 256)
                    views["xc"] = ring.load(w["w_in"], 0, KD, 6152 + blk * 256, 256)
                    views["gb"] = ring.load(w["w_in"], 0, KD, 4104 + blk * 256, 256)
                esl = slice(ec * 128, (ec + 1) * 128)
                gcv, gcT = views["gc"]
                xcv, xcT = views["xc"]
                gbv, gbT = views["gb"]
                for (t0, n) in token_tiles(NT):
                    pi = pick()
                    kb.mm(psum[pi][:, :n], [(gcv[:, k, esl], xnT[:, k, t0:t0 + n]) for k in range(KD)],
                          reads=[xnTk(k, t0) for k in range(KD)] + [gcT], writes=[psum_T[pi]])
                    kb.op(ACT, lambda e: e.copy(out=gcs[b][:, t0:t0 + n], in_=psum[pi][:, :n]), reads=[psum_T[pi]],
                          writes=[gcs_T[b]])
                pi = pick()
                kb.mm(psum[pi][:, 0:2], [(gcv[:, k, esl], xnpre[:, k, :]) for k in range(KD)],
                      reads=[T_convinit, gcT], writes=[psum_T[pi]])
                kb.op(ACT, lambda e: e.activation(out=g2, in_=psum[pi][:, 0:2], func=AF.Copy, scale=flag[:, 0:1]),
                      reads=[psum_T[pi], T_const], writes=[T_g2])
                pi = pick()
                kb.mm(psum[pi][:, 0:2], [(xcv[:, k, esl], xnpre[:, k, :]) for k in range(KD)],
                      reads=[T_convinit, xcT], writes=[psum_T[pi]])
                kb.op(DVE, lambda e: e.tensor_tensor(out=uext[b][:, 0:2], in0=g2, in1=psum[pi][:, 0:2], op=ALU.mult),
                      reads=[psum_T[pi], T_g2], writes=[uext_T[b]])
                for (t0, n) in token_tiles(NT):
                    pi = pick()
                    kb.mm(psum[pi][:, :n], [(xcv[:, k, esl], xnT[:, k, t0:t0 + n]) for k in range(KD)],
                          reads=[xnTk(k, t0) for k in range(KD)] + [xcT], writes=[psum_T[pi]])
                    if t0 < NPR:
                        kb.op(DVE, lambda e: e.tensor_tensor(out=uext[b][:, 2 + t0:2 + t0 + n], in0=gcs[b][:, t0:t0 + n],
                                                             in1=psum[pi][:, :n], op=ALU.mult),
                              reads=[gcs_T[b], psum_T[pi]], writes=[uext_T[b]])
                    else:
                        kb.op(DVE, lambda e: e.tensor_tensor(out=us[b], in0=gcs[b][:, t0:t0 + n], in1=psum[pi][:, :n],
                                                             op=ALU.mult),
                              reads=[gcs_T[b], psum_T[pi]], writes=[us_T[b]])
                for (t0, n) in token_tiles(NT):
                    pi = pick()
                    kb.mm(psum[pi][:, :n], [(gbv[:, k, esl], xnT[:, k, t0:t0 + n]) for k in range(KD)],
                          reads=[xnTk(k, t0) for k in range(KD)] + [gbT], writes=[psum_T[pi]])
                    kb.op(ACT, lambda e: e.copy(out=gbs[b][:, t0:t0 + n], in_=psum[pi][:, :n]), reads=[psum_T[pi]],
                          writes=[gbs_T[b]])

            def stage_b(cch):
                b = cch % 2
                cw = [pcol[:, 64 + 8 * jx + cch:65 + 8 * jx + cch] for jx in range(3)]
                cbias = pcol[:, 88 + cch:89 + cch]
                ncol = pcol[:, 96 + cch:97 + cch]
                yy, yT_ = y1[b], y1_T[b]
                ue, ueT = uext[b], uext_T[b]
                kb.op(DVE, lambda e: e.tensor_scalar(out=yy[:, 0:NPR], in0=ue[:, 0:NPR], scalar1=cw[0],
                                                     scalar2=cbias, op0=ALU.mult, op1=ALU.add),
                      reads=[ueT, T_const], writes=[yT_])
                kb.op(DVE, lambda e: e.scalar_tensor_tensor(out=yy[:, 0:NPR], in0=ue[:, 1:NPR + 1], scalar=cw[1],
                                                            in1=yy[:, 0:NPR], op0=ALU.mult, op1=ALU.add),
                      reads=[ueT, yT_], writes=[yT_])
                kb.op(DVE, lambda e: e.scalar_tensor_tensor(out=yy[:, 0:NPR], in0=ue[:, 2:NPR + 2], scalar=cw[2],
                                                            in1=yy[:, 0:NPR], op0=ALU.mult, op1=ALU.add),
                      reads=[ueT, yT_], writes=[yT_])
                cb3 = cbT[:, cch, :].rearrange("p (s j) -> p s j", j=2)
                kb.op(DVE, lambda e: e.tensor_scalar(out=yy[:, NPR:NT], in0=cb3[:, :, 0], scalar1=cw[0],
                                                     scalar2=cbias, op0=ALU.mult, op1=ALU.add),
                      reads=[cbT_T, T_const, yT_], writes=[yT_])
                kb.op(DVE, lambda e: e.scalar_tensor_tensor(out=yy[:, NPR:NT], in0=cb3[:, :, 1], scalar=cw[1],
                                                            in1=yy[:, NPR:NT], op0=ALU.mult, op1=ALU.add),
                      reads=[cbT_T, yT_], writes=[yT_])
                kb.op(DVE, lambda e: e.scalar_tensor_tensor(out=yy[:, NPR:NT], in0=us[b], scalar=cw[2],
                                                            in1=yy[:, NPR:NT], op0=ALU.mult, op1=ALU.add),
                      reads=[us_T[b], yT_], writes=[yT_])
                kb.op(DVE, lambda e: e.tensor_tensor(out=yy, in0=yy, in1=gbs[b], op=ALU.mult),
                      reads=[yT_, gbs_T[b]], writes=[yT_])
                for (t0, n) in token_tiles(NT):
                    rstd_from([(yy[:, t0:t0 + n], [yT_])], n, 1.0 / 128.0, sq, sq_T, rs[:, :n], rs_T, 7)
                    kb.op(DVE, lambda e: e.scalar_tensor_tensor(out=self.mixT[:, 8 + cch, t0:t0 + n],
                                                                in0=yy[:, t0:t0 + n], scalar=ncol, in1=rs[:, :n],
                                                                op0=ALU.mult, op1=ALU.mult),
                          reads=[yT_, rs_T, T_const], writes=[self.mix_T[(8 + cch, t0)]])
                pi = 6
                kb.mm([psum[pi][0:2, 0:128]], [(ue[:, NPR:NPR + 2], ident)], reads=[ueT, T_const],
                      writes=[psum_T[pi]], transpose=True)
                kb.op(ACT, lambda e: e.copy(out=crow[:, cch * 128:(cch + 1) * 128], in_=psum[pi][0:2, 0:128]),
                      reads=[psum_T[pi]], writes=[crow_T])
                kb.mm([psum[pi][0:NS, 0:128]], [(us[b], ident)], reads=[us_T[b], T_const], writes=[psum_T[pi]],
                      transpose=True)
                kb.op(ACT, lambda e: e.copy(out=csrow[:, cch * 128:(cch + 1) * 128], in_=psum[pi][0:NS, 0:128]),
                      reads=[psum_T[pi]], writes=[csrow_T])

            stage_a(0)
            for cch in range(8):
                if cch + 1 < 8:
                    stage_a(cch + 1)
                stage_b(cch)
            kb.dma(SP, convp_o, crow, reads=[crow_T])
            kb.dma(SP, convs_o[:, 1, :], csrow, reads=[csrow_T])
            kb.dma(SP, convs_o[:, 0, :], sconv_d.rearrange("(s j) c -> s j c", j=2)[:, 1, :])

        def mix_out(stack):
            ring = Ring(self, stack, 3)
            for blk in range(8):
                wv, wT = ring.load(w["w_out"], 0, KD, blk * 256, 256)
                for ec in range(2):
                    i = blk * 2 + ec
                    for (t0, n) in token_tiles(NT):
                        pi = pick()
                        kb.mm(psum[pi][:, :n],
                              [(wv[:, k, ec * 128:(ec + 1) * 128], self.mixT[:, k, t0:t0 + n]) for k in range(KD)],
                              reads=[self.mix_T[(k, t0)] for k in range(KD)] + [wT], writes=[psum_T[pi]])
                        kb.op(DVE, lambda e: e.tensor_tensor(out=hT[:, i, t0:t0 + n], in0=hT[:, i, t0:t0 + n],
                                                             in1=psum[pi][:, :n], op=ALU.add),
                              reads=[psum_T[pi], hTk(i, t0)], writes=[hTk(i, t0)])

        def phase(fn, *a):
            self.phase_id = getattr(self, "phase_id", 0) + 1
            with nc.named_scope("p%02d_%s" % (self.phase_id, fn.__name__)):
                with ExitStack() as s:
                    fn(*a, s)
                    kb.barrier()

        def phase2(*fns):
            self.phase_id = getattr(self, "phase_id", 0) + 1
            with nc.named_scope("p%02d_%s" % (self.phase_id, fns[-1][0].__name__)):
                with ExitStack() as s:
                    for f in fns:
                        f[0](*f[1:], s)
                    kb.barrier()

        F1 = (w["ffn1_gate"], w["ffn1_up"], w["ffn1_down"])
        F2 = (w["ffn2_gate"], w["ffn2_up"], w["ffn2_down"])
        phase(load_T, xm, NT)
        if ALL or "ffn1" in st:
            phase2((rmsnorm, NT, 0), (ffn, NT) + F1)
        if ALL or "mix" in st:
            with ExitStack() as sm:
                G = {"Gcol": self.scr(sm, "Gcol", [128, 8, 3, 4], F32), "Gcol_T": Tk(),
                     "sgb": self.scr(sm, "sgb", [128, 192], F32), "sgb_T": Tk(),
                     "semt": self.scr(sm, "semt", [NS, 4], F32), "swt": self.scr(sm, "swt", [NS, 4], F32),
                     "negM_dT": Tk(), "sgs_dT": Tk(),
                     "nTt": self.scr(sm, "nTt", [128, 2, 64], F32), "nTt_T": Tk()}
                phase2((rmsnorm, NT, 16), (prefix_state, G))
                self.mixT = self.scr(sm, "mixT", [128, KD, NT], BF16)
                self.mix_T = {(k, t0): Tk() for k in range(KD) for (t0, n) in token_tiles(NT)}
                phase(mix_heads, G)
                phase(mix_conv)
                phase(mix_out)
        if ALL or "ffn2" in st:
            phase2((rmsnorm, NT, 32), (ffn, NT) + F2)
        self.phase_id += 1
        with nc.named_scope("p%02d_final" % self.phase_id):
            with ExitStack() as s:
                final_out(y_out, NT, s)
        kb.finish()
        return nc


def make_pcol(inp):
    pc = np.zeros((128, 128), np.float32)

    def put(c0, v):
        v = np.asarray(v, np.float32).reshape(-1, 128)
        pc[:, c0:c0 + v.shape[0]] = v.T

    put(0, inp["norm_ffn1"][0])
    put(16, inp["norm_mix"][0])
    put(32, inp["norm_ffn2"][0])
    put(48, inp["norm_final"])
    put(64, inp["conv_w"][0, 0])
    put(72, inp["conv_w"][0, 1])
    put(80, inp["conv_w"][0, 2])
    put(88, inp["conv_b"][0])
    put(96, inp["norm_conv"][0])
    return pc


def make_cst():
    c = np.zeros((128, 512), np.float32)
    c[:, 0:128] = np.eye(128, dtype=np.float32)
    c[:, 256:512] = np.eye(16, dtype=np.float32).reshape(1, 256)
    s_i = np.arange(128)[:, None]
    t_i = np.arange(128)[None, :]
    c[:, 128:256] = np.where(s_i <= t_i, 0.0, -30000.0)
    return c


_CACHE = {}

W_NAMES = ["ffn1_gate", "ffn1_up", "ffn1_down", "w_in", "w_out", "ffn2_gate", "ffn2_up", "ffn2_down"]


def core_inputs(inp, c, shared):
    b, half = c // 2, c % 2
    f32 = np.float32
    xm = np.concatenate([inp["x_prompt"][b, half * NPR:(half + 1) * NPR], inp["x_sample"][c * NS:(c + 1) * NS, 0]], 0)
    m = dict(shared)
    m["xm"] = np.ascontiguousarray(xm, f32)
    m["flag"] = np.full((128, 1), float(half), f32)
    sl = slice(c * NS, (c + 1) * NS)
    m["sC"] = np.ascontiguousarray(inp["state_mlstm_C"][0, sl], f32)
    m["sn"] = np.ascontiguousarray(inp["state_mlstm_n"][0, sl], f32).reshape(NS * H, DK)
    m["sm"] = np.ascontiguousarray(inp["state_mlstm_m"][0, sl], f32)
    m["sconv"] = np.ascontiguousarray(inp["state_conv"][0, sl], f32).reshape(NS * 2, 1024)
    return m


def shared_inputs(inp):
    f32 = np.float32
    sh = {"pcol": make_pcol(inp), "cst": make_cst(),
          "nmb": np.ascontiguousarray(np.broadcast_to(np.asarray(inp["norm_mlstm"][0], f32)[None, :], (128, 1024))),
          "gfb": np.ascontiguousarray(np.broadcast_to(np.asarray(inp["norm_final"], f32)[None, :], (128, D))),
          "bg": np.ascontiguousarray(np.asarray(inp["b_gates"][0], f32).reshape(2, 4).T)}
    for nm in W_NAMES:
        sh[nm] = np.ascontiguousarray(inp[nm][0], f32)
    return sh


def assemble(res):
    f32 = np.float32
    y_p = np.zeros((4, 2048, D), f32)
    y_s = np.zeros((128, 1, D), f32)
    C_p = np.zeros((1, 4, H, DK, DK), f32)
    n_p = np.zeros((1, 4, H, DK), f32)
    m_p = np.zeros((1, 4, H), f32)
    conv_p = np.zeros((1, 4, 2, 1024), f32)
    C_s = np.zeros((1, 128, H, DK, DK), f32)
    n_s = np.zeros((1, 128, H, DK), f32)
    m_s = np.zeros((1, 128, H), f32)
    conv_s = np.zeros((1, 128, 2, 1024), f32)
    for c in range(8):
        r = res[c]
        b, half = c // 2, c % 2
        y_p[b, half * NPR:(half + 1) * NPR] = r["y"][:NPR]
        sl = slice(c * NS, (c + 1) * NS)
        y_s[sl, 0] = r["y"][NPR:NT]
        if half == 1:
            C_p[0, b] = r["Cp"]
            n_p[0, b] = r["np"]
            m_p[0, b] = r["mp"][:, 0]
            conv_p[0, b] = r["convp"]
        C_s[0, sl] = r["Cs"]
        n_s[0, sl] = r["ns"].reshape(NS, H, DK)
        m_s[0, sl] = r["ms"]
        conv_s[0, sl] = r["convs"]
    return (y_p, y_s, C_p, n_p, m_p, conv_p, C_s, n_s, m_s, conv_s)


def kernel(**inputs):
    if "prog" not in _CACHE:
        p = Prog(("all",))
        p.build()
        _CACHE["prog"] = p
    p = _CACHE["prog"]
    inp = {k: np.asarray(v) for k, v in inputs.items()}
    sh = shared_inputs(inp)
    in_maps = [core_inputs(inp, c, sh) for c in range(8)]
    res = run_bass_kernel_spmd(p.nc, in_maps, core_ids=list(range(8)))
    return assemble(res.results)
```
